# Optimizing a Trainium2 kernel written in Bass

```python
import math
import jax, jax.numpy as jnp
from jax import lax
import numpy as np

D_MODEL = 2048
BATCH = 4
SEQ = 4096
DEPTH = 4

BRANCH_D = D_MODEL // 2
NORM_EPS = 1e-6

SSM_D_INNER = BRANCH_D
SSM_HEAD_DIM = 64
SSM_HEADS = SSM_D_INNER // SSM_HEAD_DIM
SSM_GROUPS = 4
SSM_STATE = 128
SSM_CONV = 4
SSD_CHUNK = 128
SSM_CONV_DIM = SSM_D_INNER + 2 * SSM_GROUPS * SSM_STATE
A_COLS = SSM_D_INNER + SSM_CONV_DIM + SSM_HEADS

RWKV_D = BRANCH_D
RWKV_HEAD_DIM = 64
RWKV_HEADS = RWKV_D // RWKV_HEAD_DIM
RWKV_DECAY_LORA = 64
RWKV_A_LORA = 64
RWKV_SHIFT_DIM = 3 * RWKV_D + RWKV_DECAY_LORA + RWKV_A_LORA
B_COLS = RWKV_SHIFT_DIM + RWKV_D
RWKV_GN_EPS = 64e-5

ATT_D = BRANCH_D
ATT_HEAD_DIM = 64
ATT_HEADS = ATT_D // ATT_HEAD_DIM
ATT_KV_HEADS = 4
ATT_KV_D = ATT_KV_HEADS * ATT_HEAD_DIM
IDX_HEADS = 16
IDX_DIM = 64
TOPK_MAX = 256
Q_BLOCK = 128
ROPE_THETA = 10000.0
C_COLS = ATT_D + 2 * ATT_KV_D + ATT_D + IDX_HEADS * IDX_DIM + IDX_DIM + IDX_HEADS

N_BRANCHES = 3
GATE_COLS = N_BRANCHES * D_MODEL
IN_COLS = A_COLS + B_COLS + C_COLS + GATE_COLS

kernel_name = "hybrid_ssd_rwkv7_dsa_gated_block"


def _split(y, widths):
    offs = np.cumsum(widths)[:-1].tolist()
    return jnp.split(y, offs, axis=-1)


def rmsnorm(x, g):
    xf = x.astype(jnp.float32)
    y = xf * lax.rsqrt(jnp.mean(xf * xf, axis=-1, keepdims=True) + NORM_EPS)
    return (y * g.astype(jnp.float32)).astype(x.dtype)


def layernorm(x, w, b):
    xf = x.astype(jnp.float32)
    mu = jnp.mean(xf, axis=-1, keepdims=True)
    var = jnp.mean((xf - mu) ** 2, axis=-1, keepdims=True)
    return ((xf - mu) * lax.rsqrt(var + NORM_EPS) * w + b).astype(x.dtype)


def rope(x, positions):
    d = x.shape[-1]
    inv = ROPE_THETA ** (-(jnp.arange(d // 2, dtype=jnp.float32) * 2.0 / d))
    ang = positions.astype(jnp.float32)[..., None] * inv
    cos, sin = jnp.cos(ang)[:, :, None, :], jnp.sin(ang)[:, :, None, :]
    xf = x.astype(jnp.float32)
    x1, x2 = xf[..., : d // 2], xf[..., d // 2:]
    return jnp.concatenate([x1 * cos - x2 * sin, x2 * cos + x1 * sin], axis=-1).astype(x.dtype)


def causal_dwconv(x, w, b):
    K, C = w.shape
    y = lax.conv_general_dilated(x, w[:, None, :], window_strides=(1,), padding=[(K - 1, 0)],
                                 dimension_numbers=('NWC', 'WIO', 'NWC'), feature_group_count=C)
    return y + b


def ssd_chunked(xh, dt, a_log, bm, cm):
    f32 = jnp.float32
    Bsz, S, H, P = xh.shape
    G, N = bm.shape[-2:]
    R, Q = H // G, SSD_CHUNK
    nc = S // Q
    adt = dt * (-jnp.exp(a_log.astype(f32)))
    X = (xh.astype(f32) * dt[..., None]).reshape(Bsz, nc, Q, G, R, P)
    Bc = bm.astype(f32).reshape(Bsz, nc, Q, G, N)
    Cc = cm.astype(f32).reshape(Bsz, nc, Q, G, N)
    a_cs = jnp.cumsum(adt.reshape(Bsz, nc, Q, G, R), axis=2)
    causal = jnp.tril(jnp.ones((Q, Q), dtype=bool))[:, :, None, None]
    seg = a_cs[:, :, :, None] - a_cs[:, :, None, :]
    lmat = jnp.exp(jnp.where(causal, seg, -jnp.inf))
    cb = jnp.einsum('bclgn,bcsgn->bclsg', Cc, Bc)
    y_diag = jnp.einsum('bclsg,bclsgr,bcsgrp->bclgrp', cb, lmat, X)
    decay_states = jnp.exp(a_cs[:, :, -1:] - a_cs)
    states = jnp.einsum('bclgn,bclgr,bclgrp->bcgrpn', Bc, decay_states, X)
    chunk_decay = jnp.exp(a_cs[:, :, -1])

    def step(h, inp):
        st, dec = inp
        return h * dec[..., None, None] + st, h

    init = jnp.zeros((Bsz, G, R, P, N), f32)
    _, h_start = lax.scan(step, init, (jnp.moveaxis(states, 1, 0), jnp.moveaxis(chunk_decay, 1, 0)))
    h_start = jnp.moveaxis(h_start, 0, 1)
    y_off = jnp.einsum('bclgn,bcgrpn,bclgr->bclgrp', Cc, h_start, jnp.exp(a_cs))
    return (y_diag + y_off).reshape(Bsz, S, H, P)


def mamba2_branch(p, conv_w, conv_b, dt_bias, a_log, d_skip, norm_w):
    Bsz, S, _ = p.shape
    f32 = jnp.float32
    z, xbc, dt = _split(p, [SSM_D_INNER, SSM_CONV_DIM, SSM_HEADS])
    xbc = jax.nn.silu(causal_dwconv(xbc, conv_w, conv_b))
    xs, bm, cm = _split(xbc, [SSM_D_INNER, SSM_GROUPS * SSM_STATE, SSM_GROUPS * SSM_STATE])
    xh = xs.reshape(Bsz, S, SSM_HEADS, SSM_HEAD_DIM)
    dt = jax.nn.softplus(dt.astype(f32) + dt_bias.astype(f32))
    y = ssd_chunked(xh, dt, a_log, bm.reshape(Bsz, S, SSM_GROUPS, SSM_STATE),
                    cm.reshape(Bsz, S, SSM_GROUPS, SSM_STATE))
    y = y + d_skip.astype(f32)[:, None] * xh.astype(f32)
    yz = y.reshape(Bsz, S, SSM_D_INNER) * jax.nn.silu(z.astype(f32))
    yg = yz.reshape(Bsz, S, SSM_GROUPS, -1)
    yg = yg * lax.rsqrt(jnp.mean(yg * yg, axis=-1, keepdims=True) + NORM_EPS)
    return (yg.reshape(Bsz, S, SSM_D_INNER) * norm_w).astype(p.dtype)


def rwkv7_branch(p, mu, w0, w2, a0, a2, k_k, k_a, r_k, ln_w, ln_b):
    Bsz, S, _ = p.shape
    f32 = jnp.float32
    H, N = RWKV_HEADS, RWKV_HEAD_DIM
    shift_in, gate = _split(p, [RWKV_SHIFT_DIM, RWKV_D])
    prev = jnp.pad(shift_in, ((0, 0), (1, 0), (0, 0)))[:, :S]
    mixed = shift_in + (prev - shift_in) * mu
    r, wd, k, v, ad = _split(mixed, [RWKV_D, RWKV_DECAY_LORA, RWKV_D, RWKV_D, RWKV_A_LORA])
    w = -jax.nn.softplus(-(w0 + jnp.tanh(wd) @ w2).astype(f32)) - 0.5
    decay = jnp.exp(-jnp.exp(w))
    a = jax.nn.sigmoid((a0 + ad @ a2).astype(f32))
    heads = lambda t: t.astype(f32).reshape(Bsz, S, H, N)
    r, k, v, decay, a = heads(r), heads(k), heads(v), heads(decay), heads(a)
    kk = k * k_k.astype(f32).reshape(H, N)
    kk = kk / jnp.maximum(jnp.sqrt(jnp.sum(kk * kk, axis=-1, keepdims=True)), 1e-12)
    k = k * (1.0 + (a - 1.0) * k_a.astype(f32).reshape(H, N))
    b = kk * a

    def step(state, inp):
        r_t, w_t, k_t, v_t, kk_t, b_t = inp
        sa = -jnp.einsum('bhvk,bhk->bhv', state, kk_t)
        state = state * w_t[:, :, None, :] + sa[..., None] * b_t[:, :, None, :] + v_t[..., None] * k_t[:, :, None, :]
        return state, jnp.einsum('bhvk,bhk->bhv', state, r_t)

    tm = lambda t: jnp.moveaxis(t, 1, 0)
    init = jnp.zeros((Bsz, H, N, N), f32)
    _, ys = lax.scan(step, init, (tm(r), tm(decay), tm(k), tm(v), tm(kk), tm(b)))
    y = jnp.moveaxis(ys, 0, 1)
    mean = jnp.mean(y, axis=-1, keepdims=True)
    var = jnp.mean((y - mean) ** 2, axis=-1, keepdims=True)
    y = ((y - mean) * lax.rsqrt(var + RWKV_GN_EPS)).reshape(Bsz, S, RWKV_D) * ln_w + ln_b
    bonus = jnp.sum(r * k * r_k.astype(f32), axis=-1, keepdims=True) * v
    y = y + bonus.reshape(Bsz, S, RWKV_D)
    return (y * jax.nn.silu(gate.astype(f32))).astype(p.dtype)


def dsa_sparse_attention(q, k, v, iq, ik, iw, top_k):
    Bsz, S, H, D = q.shape
    KVH = k.shape[2]
    R = H // KVH
    nb = S // Q_BLOCK
    key_pos = jnp.arange(S)
    bidx = jnp.arange(Bsz)[:, None, None]

    def block(args):
        qb, iqb, iwb, qpos = args
        causal = key_pos[None, :] <= qpos[:, None]
        logits_idx = jnp.einsum('bqhd,bsd->bqhs', iqb, ik).astype(jnp.float32)
        score = jnp.einsum('bqh,bqhs->bqs', iwb.astype(jnp.float32), jax.nn.relu(logits_idx))
        score = jnp.where(causal[None], score, -jnp.inf)
        _, sel = lax.top_k(score, top_k)
        valid = sel <= qpos[None, :, None]
        ks = k[bidx, sel]
        vs = v[bidx, sel]
        qg = qb.reshape(Bsz, Q_BLOCK, KVH, R, D)
        logits = jnp.einsum('bqgrd,bqkgd->bqgrk', qg, ks).astype(jnp.float32) * (D ** -0.5)
        logits = jnp.where(valid[:, :, None, None, :], logits, -jnp.inf)
        prob = jax.nn.softmax(logits, axis=-1)
        o = jnp.einsum('bqgrk,bqkgd->bqgrd', prob.astype(vs.dtype), vs)
        return o.reshape(Bsz, Q_BLOCK, H, D)

    blk = lambda t: jnp.moveaxis(t.reshape((Bsz, nb, Q_BLOCK) + t.shape[2:]), 1, 0)
    out = lax.map(block, (blk(q), blk(iq), blk(iw), jnp.arange(S).reshape(nb, Q_BLOCK)))
    return jnp.moveaxis(out, 0, 1).reshape(Bsz, S, H, D)


def dsa_branch(p, positions, k_norm_w, k_norm_b, top_k):
    Bsz, S, _ = p.shape
    q, k, v, gate, iq, ik, iw = _split(p, [ATT_D, ATT_KV_D, ATT_KV_D, ATT_D, IDX_HEADS * IDX_DIM, IDX_DIM, IDX_HEADS])
    q = rope(q.reshape(Bsz, S, ATT_HEADS, ATT_HEAD_DIM), positions)
    k = rope(k.reshape(Bsz, S, ATT_KV_HEADS, ATT_HEAD_DIM), positions)
    v = v.reshape(Bsz, S, ATT_KV_HEADS, ATT_HEAD_DIM)
    iq = rope(iq.reshape(Bsz, S, IDX_HEADS, IDX_DIM), positions)
    ik = rope(layernorm(ik, k_norm_w, k_norm_b)[:, :, None, :], positions)[:, :, 0]
    iw = iw * (IDX_HEADS ** -0.5 * IDX_DIM ** -0.5)
    o = dsa_sparse_attention(q, k, v, iq, ik, iw, top_k)
    return (o.reshape(Bsz, S, ATT_D).astype(jnp.float32) * jax.nn.silu(gate.astype(jnp.float32))).astype(p.dtype)


def setup_inputs(seed: int = 0) -> dict:
    key = jax.random.key(seed)
    ks = jax.random.split(key, 32)
    f32 = jnp.float32
    L, D = DEPTH, D_MODEL
    nrm = lambda k, shape, scale: jax.random.normal(k, shape, f32) * scale
    dt0 = jnp.exp(jax.random.uniform(ks[9], (L, SSM_HEADS), f32, math.log(1e-3), math.log(1e-1)))
    return {
        "x": nrm(ks[0], (BATCH, SEQ, D), 1.0),
        "positions": jnp.broadcast_to(jnp.arange(SEQ, dtype=jnp.int32)[None, :], (BATCH, SEQ)),
        "pre_norm": 1.0 + nrm(ks[1], (L, D), 0.05),
        "post_norm": 1.0 + nrm(ks[2], (L, D), 0.05),
        "w_in": nrm(ks[3], (L, D, IN_COLS), D ** -0.5),
        "b_gate": nrm(ks[4], (L, GATE_COLS), 0.1),
        "ssm_conv_w": nrm(ks[5], (L, SSM_CONV, SSM_CONV_DIM), SSM_CONV ** -0.5),
        "ssm_conv_b": nrm(ks[6], (L, SSM_CONV_DIM), 0.02),
        "ssm_dt_bias": dt0 + jnp.log(-jnp.expm1(-dt0)),
        "ssm_a_log": jnp.log(jax.random.uniform(ks[10], (L, SSM_HEADS), f32, 1.0, 16.0)),
        "ssm_d": 1.0 + nrm(ks[11], (L, SSM_HEADS), 0.1),
        "ssm_norm": 1.0 + nrm(ks[12], (L, SSM_D_INNER), 0.05),
        "rwkv_mu": jax.random.uniform(ks[13], (L, RWKV_SHIFT_DIM), f32, 0.0, 1.0),
        "rwkv_w0": jax.random.uniform(ks[14], (L, RWKV_D), f32, -6.0, -1.0),
        "rwkv_w2": nrm(ks[15], (L, RWKV_DECAY_LORA, RWKV_D), 0.1 * RWKV_DECAY_LORA ** -0.5),
        "rwkv_a0": nrm(ks[16], (L, RWKV_D), 0.1),
        "rwkv_a2": nrm(ks[17], (L, RWKV_A_LORA, RWKV_D), 0.1 * RWKV_A_LORA ** -0.5),
        "rwkv_k_k": 0.85 + nrm(ks[18], (L, RWKV_D), 0.05),
        "rwkv_k_a": 1.0 + nrm(ks[19], (L, RWKV_D), 0.05),
        "rwkv_r_k": nrm(ks[20], (L, RWKV_HEADS, RWKV_HEAD_DIM), 0.1),
        "rwkv_ln_w": 1.0 + nrm(ks[21], (L, RWKV_D), 0.05),
        "rwkv_ln_b": nrm(ks[22], (L, RWKV_D), 0.02),
        "idx_k_norm_w": 1.0 + nrm(ks[23], (L, IDX_DIM), 0.05),
        "idx_k_norm_b": nrm(ks[24], (L, IDX_DIM), 0.02),
        "w_branch_a": nrm(ks[25], (L, SSM_D_INNER, D), SSM_D_INNER ** -0.5),
        "w_branch_b": nrm(ks[26], (L, RWKV_D, D), RWKV_D ** -0.5),
        "w_branch_c": nrm(ks[27], (L, ATT_D, D), ATT_D ** -0.5),
        "w_out": nrm(ks[28], (L, D, D), D ** -0.5),
    }


def reference(x, positions, pre_norm, post_norm, w_in, b_gate, ssm_conv_w, ssm_conv_b, ssm_dt_bias,
              ssm_a_log, ssm_d, ssm_norm, rwkv_mu, rwkv_w0, rwkv_w2, rwkv_a0, rwkv_a2, rwkv_k_k, rwkv_k_a,
              rwkv_r_k, rwkv_ln_w, rwkv_ln_b, idx_k_norm_w, idx_k_norm_b, w_branch_a, w_branch_b,
              w_branch_c, w_out):
    Bsz, S, D = x.shape
    top_k = min(TOPK_MAX, S // 4)
    for i in range(DEPTH):
        h = rmsnorm(x, pre_norm[i])
        proj = h @ w_in[i]
        pa, pb, pc, pg = _split(proj, [A_COLS, B_COLS, C_COLS, GATE_COLS])
        y_a = mamba2_branch(pa, ssm_conv_w[i], ssm_conv_b[i], ssm_dt_bias[i], ssm_a_log[i], ssm_d[i], ssm_norm[i])
        y_b = rwkv7_branch(pb, rwkv_mu[i], rwkv_w0[i], rwkv_w2[i], rwkv_a0[i], rwkv_a2[i], rwkv_k_k[i],
                           rwkv_k_a[i], rwkv_r_k[i], rwkv_ln_w[i], rwkv_ln_b[i])
        y_c = dsa_branch(pc, positions, idx_k_norm_w[i], idx_k_norm_b[i], top_k)
        gates = jax.nn.sigmoid(pg + b_gate[i]).reshape(Bsz, S, N_BRANCHES, D)
        merged = (gates[:, :, 0] * (y_a @ w_branch_a[i])
                  + gates[:, :, 1] * (y_b @ w_branch_b[i])
                  + gates[:, :, 2] * (y_c @ w_branch_c[i]))
        x = x + rmsnorm(merged @ w_out[i], post_norm[i])
    return x
```

```python
import numpy as np
from contextlib import ExitStack
import concourse.bass as bass
import concourse.mybir as mybir
from concourse.bass_utils import run_bass_kernel_spmd

F32 = mybir.dt.float32
BF16 = mybir.dt.bfloat16
I32 = mybir.dt.int32
AF = mybir.ActivationFunctionType
ALU = mybir.AluOpType
AX = mybir.AxisListType

EPOCH = 24000
ENGS = ("pe", "dve", "act", "pool", "sp")


class Tl:
    def __init__(self, h, name):
        self.h = h
        self.name = name
        self.dram = False
        self.psum = False
        self.acc = {}
        self.w = None
        self.r = {}

    def __getitem__(self, k):
        return self.h[k]


class Prog:
    def __init__(self, nc, es, nsem=120, same_engine_sync=True):
        self.nc = nc
        self.es = es
        self.q = {e: [] for e in ENGS}
        self.n = {e: 0 for e in ENGS}
        self.waited = {e: {} for e in ENGS}
        self.lane_n = {}
        self.sem_pool = [es.enter_context(nc.semaphore("s%d" % i)) for i in range(nsem)]
        self.sem_map = {}
        self.sem_next = 0
        self.lane_sem = {}
        self.retired = []
        self.same = same_engine_sync
        self.ntile = 0

    def sb(self, name, shape, dt):
        self.ntile += 1
        h = self.es.enter_context(self.nc.sbuf_tensor("%s_%d" % (name, self.ntile), list(shape), dt))
        return Tl(h, name)

    def ps(self, name, shape, dt=F32):
        self.ntile += 1
        h = self.es.enter_context(self.nc.psum_tensor("%s_%d" % (name, self.ntile), list(shape), dt))
        t = Tl(h, name)
        t.psum = True
        return t

    def dram(self, name, shape, dt, kind="Internal"):
        h = self.nc.dram_tensor(name, list(shape), dt, kind=kind)
        t = Tl(h, name)
        t.dram = True
        return t

    def _fresh(self):
        if self.sem_next < len(self.sem_pool):
            self.sem_next += 1
            return self.sem_pool[self.sem_next - 1], 0
        self.retired.sort(key=lambda x: x[0])
        r, sem = self.retired.pop(0)
        return sem, r

    def _sem(self, src, val):
        if src in ENGS:
            ep = (val - 1) // EPOCH if val > 0 else 0
            key = (src, ep)
            if key not in self.sem_map:
                self.sem_map[key] = self._fresh()
            sem, r0 = self.sem_map[key]
            return sem, val - ep * EPOCH + r0
        if src not in self.lane_sem:
            sem, r = self._fresh()
            self.lane_sem[src] = (sem, r - (val - 16))
        sem, delta = self.lane_sem[src]
        assert 0 < val + delta < 30000, (src, val, delta)
        return sem, val + delta

    def retire_lanes(self):
        for lane, (sem, delta) in self.lane_sem.items():
            self.retired.append((self.lane_n[lane] + delta, sem))
        self.lane_sem = {}

    def _collect(self, eng, reads, writes):
        deps = {}

        def add(d):
            if d is None:
                return
            s, v = d
            if deps.get(s, 0) < v:
                deps[s] = v

        reads = [t for t in reads if not t.dram]
        writes = [t for t in writes if not t.dram]
        for t in list(reads) + list(writes):
            if t.psum:
                for s_, v_ in t.acc.items():
                    if s_ != eng:
                        add((s_, v_))
        for t in reads:
            add(t.w)
        for t in writes:
            add(t.w)
            for s, v in t.r.items():
                add((s, v))
        waits = []
        for s, v in deps.items():
            if s == eng:
                if eng == "pe" or not self.same:
                    continue
            if self.waited[eng].get(s, 0) >= v:
                continue
            self.waited[eng][s] = v
            waits.append((s, v))
        return waits

    def op(self, eng, fn, reads=(), writes=()):
        waits = self._collect(eng, reads, writes)
        self.n[eng] += 1
        me = (eng, self.n[eng])
        self._emit_now(eng, waits, fn, me)
        reads = [t for t in reads if not t.dram]
        writes = [t for t in writes if not t.dram]
        for t in list(reads) + list(writes):
            if t.psum:
                t.acc[eng] = me[1]
        for t in reads:
            if t.r.get(eng, 0) < me[1]:
                t.r[eng] = me[1]
        for t in writes:
            t.w = me
            t.r = {}

    def dma(self, qeng, out_ap, in_ap, reads=(), writes=(), lane=None):
        reads = [t for t in reads if not t.dram]
        writes = [t for t in writes if not t.dram]
        if lane is None:
            lane = "L_" + (writes[0].name if writes else reads[0].name)
        waits = self._collect(qeng, reads, writes)
        k = self.lane_n.get(lane, 0) + 16
        self.lane_n[lane] = k
        me = (lane, k)
        self._emit_now(qeng, waits, lambda e: e.dma_start(out=out_ap, in_=in_ap), me)
        for t in reads:
            if t.r.get(lane, 0) < k:
                t.r[lane] = k
        for t in writes:
            t.w = me
            t.r = {}

    def barrier(self):
        for e in ENGS:
            waits = []
            for s in list(ENGS) + list(self.lane_n):
                v = self.n[s] if s in ENGS else self.lane_n[s]
                if v == 0 or self.waited[e].get(s, 0) >= v:
                    continue
                self.waited[e][s] = v
                waits.append((s, v))
            self._emit_now(e, waits, None, None)

    def scope(self):
        prog = self

        class _S:
            def __enter__(s):
                s.old = prog.es
                s.st = ExitStack()
                prog.es = s.st
                return s

            def __exit__(s, *a):
                prog.barrier()
                prog.retire_lanes()
                s.st.close()
                prog.es = s.old
                return False
        return _S()

    def finish(self, out_tiles, eng="sp"):
        waits = self._collect(eng, out_tiles, ())
        self._emit_now(eng, waits, None, None)

    def _emit_now(self, eng, waits, fn, me):
        e = {"pe": self.nc.tensor, "dve": self.nc.vector, "act": self.nc.scalar, "pool": self.nc.gpsimd, "sp": self.nc.sync}[eng]
        for s, v in waits:
            sem, sv = self._sem(s, v)
            e.wait_ge(sem, sv)
        if fn is None:
            return
        ins = fn(e)
        src, val = me
        sem, sv = self._sem(src, val)
        ins.then_inc(sem, 1 if src in ENGS else 16)

    def _replay(self, eng, e):
        for waits, fn, me in self.q[eng]:
            for s, v in waits:
                sem, sv = self._sem(s, v)
                e.wait_ge(sem, sv)
            if fn is None:
                continue
            ins = fn(e)
            src, val = me
            sem, sv = self._sem(src, val)
            if src in ENGS:
                ins.then_inc(sem, 1)
            else:
                ins.then_inc(sem, 16)

    def emit(self):
        return
        nc = self.nc
        with nc.Block() as block:
            @block.sync
            def _(e):
                self._replay("sp", e)

            @block.tensor
            def _(e):
                self._replay("pe", e)

            @block.vector
            def _(e):
                self._replay("dve", e)

            @block.scalar
            def _(e):
                self._replay("act", e)

            @block.gpsimd
            def _(e):
                self._replay("pool", e)


def _mm(self, out, lhsT, rhs, start=True, stop=True, reads=(), writes=()):
    self.op("pe", lambda e: e.matmul(out, lhsT, rhs, start=start, stop=stop), reads, writes)

def _tr(self, out, in_, ident, reads=(), writes=()):
    self.op("pe", lambda e: e.transpose(out, in_, ident), reads, writes)

def _act(self, out, in_, func, reads=(), writes=(), bias=None, scale=1.0):
    if bias is None:
        self.op("act", lambda e: e.activation(out=out, in_=in_, func=func, scale=scale), reads, writes)
    else:
        self.op("act", lambda e: e.activation(out=out, in_=in_, func=func, bias=bias, scale=scale), reads, writes)

def _tt(self, eng, out, in0, in1, op, reads=(), writes=()):
    self.op(eng, lambda e: e.tensor_tensor(out=out, in0=in0, in1=in1, op=op), reads, writes)

def _ts(self, eng, out, in0, s1, op0, reads=(), writes=(), s2=None, op1=None):
    if op1 is None:
        self.op(eng, lambda e: e.tensor_scalar(out=out, in0=in0, scalar1=s1, scalar2=None, op0=op0), reads, writes)
    else:
        self.op(eng, lambda e: e.tensor_scalar(out=out, in0=in0, scalar1=s1, scalar2=s2, op0=op0, op1=op1), reads, writes)

def _stt(self, out, in0, scalar, in1, op0, op1, reads=(), writes=()):
    self.op("dve", lambda e: e.scalar_tensor_tensor(out=out, in0=in0, scalar=scalar, in1=in1, op0=op0, op1=op1), reads, writes)

def _copy(self, eng, out, in_, reads=(), writes=()):
    if eng == "act":
        self.op("act", lambda e: e.copy(out=out, in_=in_), reads, writes)
    else:
        self.op(eng, lambda e: e.tensor_copy(out=out, in_=in_), reads, writes)

def _memset(self, eng, ap, val, writes=()):
    self.op(eng, lambda e: e.memset(ap, val), (), writes)

Prog.mm = _mm
Prog.tr = _tr
Prog.act = _act
Prog.tt = _tt
Prog.ts = _ts
Prog.stt = _stt
Prog.copy = _copy
Prog.memset = _memset


D = 2048
NCP = 17280
A_Z, A_XBC, A_DT = 0, 1024, 3072
B0 = 3200
B_R, B_K, B_V, B_G, B_WA = B0, B0 + 1024, B0 + 2048, B0 + 3072, B0 + 4096
C0 = B0 + 4224
C_Q, C_K, C_V, C_G, C_IQ, C_IK = C0, C0 + 1024, C0 + 1280, C0 + 1536, C0 + 2560, C0 + 3584
G0 = C0 + 3712
assert G0 + 6144 == NCP
EPS = 1e-6

CV = {}
_o = 0
for _n, _w in [("pre_g", 16), ("post_g", 16), ("bg", 48), ("conv_w", 64), ("conv_b", 16), ("dt_bias", 1),
               ("a_log", 1), ("d_exp", 8), ("ssm_norm", 8), ("mu_r", 8), ("mu_k", 8), ("mu_v", 8), ("mu_wa", 1),
               ("w0", 8), ("a0", 8), ("k_k", 8), ("k_a", 8), ("ln_w", 8), ("ln_b", 8), ("r_k", 8),
               ("ikn_w", 1), ("ikn_b", 1)]:
    CV[_n] = (_o, _w)
    _o += _w
NV = _o


class Ctx:
    pass


def cvcol(c, l, name, j=0, rows=128):
    o, w = CV[name]
    return c.cv[0:rows, l * NV + o + j: l * NV + o + j + 1]


def alloc_consts(p, c, L):
    c.cv = p.sb("cv", [128, L * NV], F32)
    p.dma("sp", c.cv[:], c.cvec_d[:], reads=[c.cvec_d], writes=[c.cv])
    c.ones_bf = p.sb("ones_bf", [128, 128], BF16)
    p.memset("dve", c.ones_bf[:], 1.0, writes=[c.ones_bf])
    c.cm = p.sb("cm", [128, 8 * 128], F32)
    p.dma("sp", c.cm[:], c.cmat_d[:], reads=[c.cmat_d], writes=[c.cm])
    c.cm_bf = p.sb("cm_bf", [128, 8 * 128], BF16)
    p.copy("dve", c.cm_bf[:], c.cm[:], [c.cm], [c.cm_bf])


def rstd_from_ss(p, rs, ss, n, width):
    p.ts("dve", rs[:, 0:width], ss[:, 0:width], 1.0 / n, ALU.mult, [ss], [rs], s2=EPS, op1=ALU.add)
    p.act(rs[:, 0:width], rs[:, 0:width], AF.Sqrt, [rs], [rs])
    p.op("dve", lambda e: e.reciprocal(out=rs[:, 0:width], in_=rs[:, 0:width]), [rs], [rs])


def phase_p1(p, c, l, xin):
    T = c.T
    TT = min(T, 2048)
    NJ = TT // 512
    with p.scope():
        xg = p.sb("xg", [128, 16, TT], BF16)
        xst = [p.sb("xst%d" % i, [128, TT], F32) for i in range(2)]
        sq = [p.sb("sq%d" % i, [128, TT], BF16) for i in range(2)]
        rs = p.sb("rs", [128, TT], F32)
        wst = [p.sb("wst%d" % i, [128, 16, 256], F32) for i in range(2)]
        wbf = [p.sb("wbf%d" % i, [128, 16, 256], BF16) for i in range(2)]
        osb = [p.sb("osb%d" % i, [128, TT], F32) for i in range(2)]
        ss = p.ps("ss", [128, TT], F32)
        acc = [p.ps("acc%d" % i, [128, 512], F32) for i in range(8 - NJ)]
        for tt in range(T // TT):
            t0 = tt * TT
            for k in range(16):
                xs = xst[k % 2]
                p.dma("sp", xs[:], xin[k * 128:(k + 1) * 128, t0:t0 + TT], reads=[xin], writes=[xs])
                p.ts("dve", xg[:, k, :], xs[:], cvcol(c, l, "pre_g", k), ALU.mult, [xs, c.cv], [xg])
                p.tt("pool", sq[k % 2][:], xs[:], xs[:], ALU.mult, [xs], [sq[k % 2]])
                for j in range(NJ):
                    p.mm(ss[:, j * 512:(j + 1) * 512], c.ones_bf[:], sq[k % 2][:, j * 512:(j + 1) * 512],
                         start=(k == 0), stop=(k == 15), reads=[c.ones_bf, sq[k % 2]], writes=[ss])
            rstd_from_ss(p, rs, ss, D, TT)
            ng = (NCP + 255) // 256
            ai = 0
            oi = 0
            wv = c.w_in_l[l].h.rearrange("(k p) c -> p k c", p=128)
            for g in range(ng):
                c0 = g * 256
                cw = min(256, NCP - c0)
                ws, wb = wst[g % 2], wbf[g % 2]
                p.dma("sp", ws[:, :, 0:cw], wv[:, :, c0:c0 + cw], reads=[c.w_in_l[l]], writes=[ws])
                p.copy("pool" if g % 2 == 0 else "act", wb[:, :, 0:cw], ws[:, :, 0:cw], [ws], [wb])
                for ct in range(cw // 128):
                    ob = osb[oi % 2]
                    oi += 1
                    for j in range(NJ):
                        a = acc[ai % len(acc)]
                        ai += 1
                        for k in range(16):
                            p.mm(a[:], wb[:, k, ct * 128:(ct + 1) * 128], xg[:, k, j * 512:(j + 1) * 512],
                                 start=(k == 0), stop=(k == 15), reads=[wb, xg], writes=[a])
                        p.tt("dve", ob[:, j * 512:(j + 1) * 512], a[:], rs[:, j * 512:(j + 1) * 512], ALU.mult,
                             [a, rs], [ob])
                    r0 = c0 + ct * 128
                    if r0 < G0:
                        p.dma("pool", c.projT[r0:r0 + 128, t0:t0 + TT], ob[:], reads=[ob], writes=[c.projT])
                    else:
                        p.dma("pool", c.projG[r0 - G0:r0 - G0 + 128, t0:t0 + TT], ob[:], reads=[ob], writes=[c.projG])


def phase_p3(p, c, l, xin, xout):
    T = c.T
    TT = 512
    wbr = [c.w_ba, c.w_bb, c.w_bc]
    ybr = [c.yaT, c.ybT, c.ycT]
    with p.scope():
        ysb = [p.sb("ysb%d" % i, [128, 8, TT], BF16) for i in range(3)]
        mg = p.sb("mg", [128, 16, TT], BF16)
        osb = p.sb("o3", [128, 16, TT], F32)
        wst = [p.sb("w3st%d" % i, [128, 8, 128], F32) for i in range(2)]
        wb = [p.sb("w3bf%d" % i, [128, 8, 128], BF16) for i in range(2)]
        wst2 = [p.sb("wost%d" % i, [128, 16, 128], F32) for i in range(2)]
        wb2 = [p.sb("wobf%d" % i, [128, 16, 128], BF16) for i in range(2)]
        pg = [p.sb("pg%d" % i, [128, TT], F32) for i in range(2)]
        gt = [p.sb("gt%d" % i, [128, TT], F32) for i in range(2)]
        macc = p.sb("macc", [128, TT], F32)
        tmp = p.sb("tmp3", [128, TT], F32)
        sq = [p.sb("sq3%d" % i, [128, TT], BF16) for i in range(2)]
        rs = p.sb("rs3", [128, TT], F32)
        xt = [p.sb("xt3%d" % i, [128, TT], F32) for i in range(2)]
        xo = [p.sb("xo3%d" % i, [128, TT], F32) for i in range(2)]
        acc = [p.ps("acc3%d" % i, [128, 512], F32) for i in range(4)]
        ss = p.ps("ss3", [128, 512], F32)
        ai = 0
        wi = 0
        import os
        STOP = int(os.environ.get("P3STOP", "99"))
        for tt in range(T // TT):
            t0 = tt * TT
            for i in range(3):
                yv = ybr[i].h.rearrange("(k p) t -> p k t", p=128)
                p.dma("sp", ysb[i][:], yv[:, :, t0:t0 + TT], reads=[ybr[i]], writes=[ysb[i]])
            for ct in range(16):
                for i in range(3):
                    ws, wbb = wst[wi % 2], wb[wi % 2]
                    pgt, gtt = pg[wi % 2], gt[wi % 2]
                    wi += 1
                    wv = wbr[i].h[l].rearrange("(k p) c -> p k c", p=128)
                    p.dma("sp", ws[:], wv[:, :, ct * 128:(ct + 1) * 128], reads=[wbr[i]], writes=[ws])
                    p.copy("pool", wbb[:], ws[:], [ws], [wbb])
                    a = acc[ai % 4]
                    ai += 1
                    for k in range(8):
                        p.mm(a[:], wbb[:, k, :], ysb[i][:, k, :], start=(k == 0), stop=(k == 7),
                             reads=[wbb, ysb[i]], writes=[a])
                    r0 = i * 2048 + ct * 128
                    p.dma("sp", pgt[:], c.projG[r0:r0 + 128, t0:t0 + TT], reads=[c.projG], writes=[pgt])
                    p.act(gtt[:], pgt[:], AF.Sigmoid, [pgt, c.cv], [gtt], bias=cvcol(c, l, "bg", i * 16 + ct))
                    if i == 0:
                        p.tt("dve", macc[:], a[:], gtt[:], ALU.mult, [a, gtt], [macc])
                    else:
                        p.tt("dve", tmp[:], a[:], gtt[:], ALU.mult, [a, gtt], [tmp])
                        if i == 1:
                            p.tt("dve", macc[:], macc[:], tmp[:], ALU.add, [macc, tmp], [macc])
                        else:
                            p.tt("dve", mg[:, ct, :], macc[:], tmp[:], ALU.add, [macc, tmp], [mg])
            if STOP <= 1:
                continue
            for c2 in range(16):
                ws, wbb = wst2[c2 % 2], wb2[c2 % 2]
                wv = c.w_out.h[l].rearrange("(k p) c -> p k c", p=128)
                p.dma("sp", ws[:], wv[:, :, c2 * 128:(c2 + 1) * 128], reads=[c.w_out], writes=[ws])
                p.copy("pool", wbb[:], ws[:], [ws], [wbb])
                a = acc[ai % 4]
                ai += 1
                for k in range(16):
                    p.mm(a[:], wbb[:, k, :], mg[:, k, :], start=(k == 0), stop=(k == 15), reads=[wbb, mg], writes=[a])
                p.copy("dve", osb[:, c2, :], a[:], [a], [osb])
                p.act(sq[c2 % 2][:], a[:], AF.Square, [a], [sq[c2 % 2]])
                p.mm(ss[:], c.ones_bf[:], sq[c2 % 2][:], start=(c2 == 0), stop=(c2 == 15),
                     reads=[c.ones_bf, sq[c2 % 2]], writes=[ss])
            if STOP <= 2:
                continue
            rstd_from_ss(p, rs, ss, D, TT)
            if STOP <= 3:
                continue
            for c2 in range(16):
                xx, xn = xt[c2 % 2], xo[c2 % 2]
                p.dma("sp", xx[:], xin[c2 * 128:(c2 + 1) * 128, t0:t0 + TT], reads=[xin], writes=[xx])
                p.tt("pool", tmp[:], osb[:, c2, :], rs[:], ALU.mult, [osb, rs], [tmp])
                p.stt(xn[:], tmp[:], cvcol(c, l, "post_g", c2), xx[:], ALU.mult, ALU.add, [tmp, xx, c.cv], [xn])
                p.dma("pool", xout[c2 * 128:(c2 + 1) * 128, t0:t0 + TT], xn[:], reads=[xn], writes=[xout])


def phase_ssd(p, c, l):
    T = c.T
    NCH = T // 128
    IDb = c.cm_bf[:, 0:128]
    ID = c.cm
    TRIU = c.cm[:, 128:256]
    SU = c.cm[:, 256:384]
    ONESF = c.cm[:, 384:512]
    ONEC = c.cm[:, 384:385]
    with p.scope():
        dtT = p.sb("dtT", [16, T], F32)
        adtT = p.sb("adtT", [16, T], F32)
        negA = p.sb("negA", [16, 1], F32)
        with p.scope():
            dtr = p.sb("dtr", [16, T], F32)
            p.dma("sp", dtr[:], c.projT[A_DT:A_DT + 16, 0:T], reads=[c.projT], writes=[dtr])
            p.act(dtr[:], dtr[:], AF.Exp, [dtr, c.cv], [dtr], bias=cvcol(c, l, "dt_bias", 0, 16))
            p.act(dtT[:], dtr[:], AF.Ln, [dtr, c.cm], [dtT], bias=c.cm[0:16, 384:385])
            p.act(negA[:], cvcol(c, l, "a_log", 0, 16), AF.Exp, [c.cv], [negA])
            p.ts("dve", negA[:], negA[:], -1.0, ALU.mult, [negA], [negA])
            p.ts("dve", adtT[:], dtT[:], negA[:, 0:1], ALU.mult, [dtT, negA], [adtT])
            xr = [p.sb("xr%d" % i, [128, T + 3], F32) for i in range(2)]
            ca = [p.sb("ca%d" % i, [128, T], F32) for i in range(2)]
            xo = [p.sb("cxo%d" % i, [128, T], BF16) for i in range(2)]
            for i in range(2):
                p.memset("pool", xr[i][:, 0:3], 0.0, writes=[xr[i]])
            for ci in range(16):
                x_, a_, o_ = xr[ci % 2], ca[ci % 2], xo[ci % 2]
                r0 = A_XBC + ci * 128
                p.dma("sp", x_[:, 3:T + 3], c.projT[r0:r0 + 128, 0:T], reads=[c.projT], writes=[x_])
                p.ts("dve", a_[:], x_[:, 3:T + 3], cvcol(c, l, "conv_w", 3 * 16 + ci), ALU.mult, [x_, c.cv], [a_],
                     s2=cvcol(c, l, "conv_b", ci), op1=ALU.add)
                for j in (2, 1, 0):
                    p.stt(a_[:], x_[:, j:T + j], cvcol(c, l, "conv_w", j * 16 + ci), a_[:], ALU.mult, ALU.add,
                          [x_, a_, c.cv], [a_])
                p.act(o_[:], a_[:], AF.Silu, [a_], [o_])
                p.dma("pool", c.xbcT[ci * 128:(ci + 1) * 128, 0:T], o_[:], reads=[o_], writes=[c.xbcT])
        xbc = [p.sb("xbc%d" % i, [128, 16, 128], BF16) for i in range(2)]
        zt = [p.sb("zt%d" % i, [128, 8, 128], F32) for i in range(2)]
        dtk = p.sb("dtk", [128, 32], F32)
        sm = p.sb("ssm_sm", [128, 64], F32)
        Xdt = p.sb("Xdt", [128, 1024], BF16)
        Xds = p.sb("Xds", [128, 1024], BF16)
        Btok = p.sb("Btok", [128, 512], BF16)
        adx = p.sb("adx", [128, 1024], F32)
        lall = p.sb("lall", [128, 16, 128], F32)
        E = p.sb("E", [128, 1024], F32)
        Lm = p.sb("Lm", [128, 2048], F32)
        CBm = p.sb("CBm", [128, 512], F32)
        M = p.sb("M", [128, 16, 128], BF16)
        hT = p.sb("hT", [128, 1024], F32)
        hTb = p.sb("hTb", [128, 1024], BF16)
        Y = p.sb("Y", [128, 1024], F32)
        t1 = p.sb("t1", [128, 1024], F32)
        sz = p.sb("sz", [128, 1024], F32)
        sqy = p.sb("sqy", [128, 1024], BF16)
        rsy = p.sb("rsy", [128, 512], F32)
        yo = [p.sb("yo%d" % i, [128, 8, 128], BF16) for i in range(2)]
        P0 = p.ps("P0", [128, 2048], BF16)
        P1 = p.ps("P1", [128, 2048], F32)
        P2 = p.ps("P2", [128, 1024], F32)
        p.memset("dve", hT[:], 0.0, writes=[hT])
        p.memset("pool", hTb[:], 0.0, writes=[hTb])
        xv = c.xbcT.h.rearrange("(k p) t -> p k t", p=128)
        zv = c.projT.h[A_Z:A_Z + 1024, :].rearrange("(k p) t -> p k t", p=128)
        yav = c.yaT.h.rearrange("(k p) t -> p k t", p=128)
        for ch in range(NCH):
            t0 = ch * 128
            xb, zz, yy = xbc[ch % 2], zt[ch % 2], yo[ch % 2]
            p.dma("sp", xb[:], xv[:, :, t0:t0 + 128], reads=[c.xbcT], writes=[xb])
            p.dma("sp", zz[:], zv[:, :, t0:t0 + 128], reads=[c.projT], writes=[zz])
            p.tr(P2[:, 0:16], dtT[:, t0:t0 + 128], ID[0:16, 0:16], [dtT, c.cm], [P2])
            p.tr(P2[:, 16:32], adtT[:, t0:t0 + 128], ID[0:16, 0:16], [adtT, c.cm], [P2])
            p.copy("act", dtk[:], P2[:, 0:32], [P2], [dtk])
            p.mm(P2[:, 64:80], SU, dtk[:, 16:32], reads=[c.cm, dtk], writes=[P2])
            p.mm(P2[:, 128:144], ONESF, dtk[:, 16:32], reads=[c.cm, dtk], writes=[P2])
            p.act(sm[:, 0:16], P2[:, 64:80], AF.Exp, [P2], [sm])
            p.act(sm[:, 16:32], P2[:, 128:144], AF.Exp, [P2], [sm])
            p.tt("dve", sm[:, 32:48], sm[:, 0:16], dtk[:, 0:16], ALU.mult, [sm, dtk], [sm])
            for j in range(12):
                p.tr(P0[:, j * 128:(j + 1) * 128], xb[:, j, :], IDb, [xb, c.cm_bf], [P0])
            px = P0[:, 0:1024].rearrange("p (h d) -> p h d", d=64)
            p.tt("dve", Xdt[:].rearrange("p (h d) -> p h d", d=64), px,
                 dtk[:, 0:16].unsqueeze(2).to_broadcast([128, 16, 64]), ALU.mult, [P0, dtk], [Xdt])
            p.tt("dve", Xds[:].rearrange("p (h d) -> p h d", d=64), px,
                 sm[:, 32:48].unsqueeze(2).to_broadcast([128, 16, 64]), ALU.mult, [P0, sm], [Xds])
            p.copy("act", Btok[:], P0[:, 1024:1536], [P0], [Btok])
            p.copy("pool", adx[:].rearrange("p (h d) -> p h d", d=64),
                   dtk[:, 16:32].unsqueeze(2).to_broadcast([128, 16, 64]), [dtk], [adx])
            p.tt("pool", lall[:], dtk[:, 16:32].unsqueeze(2).to_broadcast([128, 16, 128]),
                 SU.unsqueeze(1).to_broadcast([128, 16, 128]), ALU.mult, [dtk, c.cm], [lall])
            for h in range(16):
                p.mm(P1[:, h * 128:(h + 1) * 128], lall[:, h, :], TRIU, reads=[lall, c.cm], writes=[P1])
            p.act(Lm[:], P1[:], AF.Exp, [P1], [Lm])
            for j in range(8):
                p.mm(P2[:, j * 128:(j + 1) * 128], adx[:, j * 128:(j + 1) * 128], TRIU, reads=[adx, c.cm], writes=[P2])
            p.act(E[:], P2[:], AF.Exp, [P2], [E])
            for g in range(4):
                p.mm(P2[:, g * 128:(g + 1) * 128], xb[:, 8 + g, :], xb[:, 12 + g, :], reads=[xb], writes=[P2])
            p.tt("dve", CBm[:].rearrange("p (g l) -> p g l", l=128), P2[:, 0:512].rearrange("p (g l) -> p g l", l=128),
                 TRIU.unsqueeze(1).to_broadcast([128, 4, 128]), ALU.mult, [P2, c.cm], [CBm])
            p.tt("dve", M[:].rearrange("p (g r) l -> p g r l", r=4), Lm[:].rearrange("p (g r l) -> p g r l", r=4, l=128),
                 CBm[:].rearrange("p (g l) -> p g l", l=128).unsqueeze(2).to_broadcast([128, 4, 4, 128]), ALU.mult,
                 [Lm, CBm], [M])
            for h in range(16):
                p.mm(P1[(h % 2) * 64:(h % 2) * 64 + 64, (h // 2) * 128:(h // 2) * 128 + 128], Xdt[:, h * 64:(h + 1) * 64],
                     M[:, h, :], reads=[Xdt, M], writes=[P1])
            for j in range(8):
                p.mm(P2[:, j * 128:(j + 1) * 128], hTb[:, j * 128:(j + 1) * 128], xb[:, 12 + j // 2, :],
                     reads=[hTb, xb], writes=[P2])
            p.tt("dve", t1[:], P2[:], E[:], ALU.mult, [P2, E], [t1])
            p.tt("dve", Y[:], t1[:], P1[:, 0:1024], ALU.add, [t1, P1], [Y])
            for g in range(4):
                p.mm(P1[:, 1024 + g * 256:1024 + (g + 1) * 256], Btok[:, g * 128:(g + 1) * 128],
                     Xds[:, g * 256:(g + 1) * 256], reads=[Btok, Xds], writes=[P1])
            p.tt("pool", hT[:].rearrange("p (h d) -> p h d", d=64), hT[:].rearrange("p (h d) -> p h d", d=64),
                 sm[:, 16:32].unsqueeze(2).to_broadcast([128, 16, 64]), ALU.mult, [hT, sm], [hT])
            p.tt("dve", hT[:], hT[:], P1[:, 1024:2048], ALU.add, [hT, P1], [hT])
            p.copy("act", hTb[:], hT[:], [hT], [hTb])
            o_d, _ = CV["d_exp"]
            o_n, _ = CV["ssm_norm"]
            dcol = c.cv[:, l * NV + o_d:l * NV + o_d + 8].unsqueeze(2).to_broadcast([128, 8, 128])
            ncol = c.cv[:, l * NV + o_n:l * NV + o_n + 8].unsqueeze(2).to_broadcast([128, 8, 128])
            Y3 = Y[:].rearrange("p (j l) -> p j l", l=128)
            t13 = t1[:].rearrange("p (j l) -> p j l", l=128)
            p.tt("pool", t13, xb[:, 0:8, :], dcol, ALU.mult, [xb, c.cv], [t1])
            p.tt("pool", Y[:], Y[:], t1[:], ALU.add, [Y, t1], [Y])
            p.act(sz[:], zz[:].rearrange("p j l -> p (j l)"), AF.Silu, [zz], [sz])
            p.tt("dve", Y[:], Y[:], sz[:], ALU.mult, [Y, sz], [Y])
            p.tt("pool", sqy[:], Y[:], Y[:], ALU.mult, [Y], [sqy])
            for g in range(4):
                for i2 in range(2):
                    j = 2 * g + i2
                    p.mm(P2[:, g * 128:(g + 1) * 128], c.ones_bf[:], sqy[:, j * 128:(j + 1) * 128],
                         start=(i2 == 0), stop=(i2 == 1), reads=[c.ones_bf, sqy], writes=[P2])
            rstd_from_ss(p, rsy, P2, 256, 512)
            p.tt("dve", Y[:].rearrange("p (g i l) -> p g i l", i=2, l=128), Y[:].rearrange("p (g i l) -> p g i l", i=2, l=128),
                 rsy[:].rearrange("p (g l) -> p g l", l=128).unsqueeze(2).to_broadcast([128, 4, 2, 128]), ALU.mult,
                 [Y, rsy], [Y])
            p.tt("dve", yy[:], Y3, ncol, ALU.mult, [Y, c.cv], [yy])
            p.dma("pool", yav[:, :, t0:t0 + 128], yy[:], reads=[yy], writes=[c.yaT])


R_Q, R_IQ, R_K, R_IK = 0, 1024, 2048, 2304
MAGIC = 12582912.0
NEGBIG = -1.0e30


def phase_dsa(p, c, l):
    T = c.T
    NQ = T // 128
    ID = c.cm[:, 0:128]
    IDb = c.cm_bf[:, 0:128]
    ONESF = c.cm[:, 384:512]
    PERM = c.cm[:, 512:640]
    NEGM = c.cm[:, 640:768]
    INV = c.cm[:, 768:769]
    SGN = c.cm[:, 769:770]
    HPI = c.cm[:, 770:771]
    TWO_PI = 6.283185307179586
    C1 = 6.28125
    C2 = TWO_PI - C1
    with p.scope():
        Ct = p.sb("ropeC", [128, T], F32)
        St = p.sb("ropeS", [128, T], F32)
        with p.scope():
            pi_ = p.sb("posi", [128, T], I32)
            ang = p.sb("ang", [128, T], F32)
            kk = p.sb("angk", [128, T], F32)
            r = p.sb("angr", [128, T], F32)
            p.dma("sp", pi_[:], c.pos_d[:], reads=[c.pos_d], writes=[pi_])
            p.copy("dve", ang[:], pi_[:], [pi_], [ang])
            p.ts("dve", ang[:], ang[:], INV, ALU.mult, [ang, c.cm], [ang])
            p.ts("dve", kk[:], ang[:], 1.0 / TWO_PI, ALU.mult, [ang], [kk], s2=MAGIC, op1=ALU.add)
            p.ts("dve", kk[:], kk[:], MAGIC, ALU.subtract, [kk], [kk])
            p.stt(r[:], kk[:], -C1, ang[:], ALU.mult, ALU.add, [kk, ang], [r])
            p.stt(r[:], kk[:], -C2, r[:], ALU.mult, ALU.add, [kk, r], [r])
            p.ts("dve", r[:], r[:], 3.14159, ALU.min, [r], [r], s2=-3.14159, op1=ALU.max)
            p.act(St[:], r[:], AF.Sin, [r], [St])
            p.ts("dve", St[:], St[:], SGN, ALU.mult, [St, c.cm], [St])
            p.ts("dve", kk[:], r[:], -1.0, ALU.mult, [r], [kk])
            p.tt("dve", kk[:], kk[:], r[:], ALU.max, [kk, r], [kk])
            p.act(Ct[:], kk[:], AF.Sin, [kk, c.cm], [Ct], bias=HPI, scale=-1.0)
        with p.scope():
            xs = [p.sb("rx%d" % i, [128, T], F32) for i in range(2)]
            ob = [p.sb("rob%d" % i, [128, T], BF16) for i in range(2)]
            tmp = p.sb("rtmp", [128, 512], F32)
            o1 = p.sb("ro1", [128, 512], F32)
            sqk = p.sb("rsq", [128, 512], F32)
            rsk = p.sb("rrs", [128, 512], F32)
            PS = [p.ps("rps%d" % i, [128, 512], F32) for i in range(4)]
            tiles = [(C_Q + i * 128, R_Q + i * 128, 128) for i in range(8)] + \
                    [(C_IQ + i * 128, R_IQ + i * 128, 128) for i in range(8)] + \
                    [(C_K + i * 128, R_K + i * 128, 128) for i in range(2)] + [(C_IK, R_IK, 64)]
            for ti, (src, dst, nr) in enumerate(tiles):
                x_, o_ = xs[ti % 2], ob[ti % 2]
                p.dma("sp", x_[0:nr, :], c.projT[src:src + nr, 0:T], reads=[c.projT], writes=[x_])
                for j in range(T // 512):
                    sl = slice(j * 512, (j + 1) * 512)
                    if nr == 64:
                        ps = PS[j % 2]
                        p.mm(ps[0:64, :], ONESF[0:64, 0:64], x_[0:64, sl], reads=[c.cm, x_], writes=[ps])
                        p.stt(x_[0:64, sl], ps[0:64, :], -1.0 / 64, x_[0:64, sl], ALU.mult, ALU.add, [ps, x_], [x_])
                        p.tt("pool", sqk[0:64, :], x_[0:64, sl], x_[0:64, sl], ALU.mult, [x_], [sqk])
                        p.mm(ps[0:64, :], ONESF[0:64, 0:64], sqk[0:64, :], reads=[c.cm, sqk], writes=[ps])
                        p.ts("dve", rsk[0:64, :], ps[0:64, :], 1.0 / 64, ALU.mult, [ps], [rsk], s2=EPS, op1=ALU.add)
                        p.act(rsk[0:64, :], rsk[0:64, :], AF.Sqrt, [rsk], [rsk])
                        p.op("dve", lambda e, a=rsk[0:64, :]: e.reciprocal(out=a, in_=a), [rsk], [rsk])
                        p.tt("dve", x_[0:64, sl], x_[0:64, sl], rsk[0:64, :], ALU.mult, [x_, rsk], [x_])
                        p.ts("dve", x_[0:64, sl], x_[0:64, sl], cvcol(c, l, "ikn_w", 0, 64), ALU.mult, [x_, c.cv], [x_],
                             s2=cvcol(c, l, "ikn_b", 0, 64), op1=ALU.add)
                    ps = PS[2 + j % 2]
                    p.mm(ps[0:nr, :], PERM[0:nr, 0:nr], x_[0:nr, sl], reads=[c.cm, x_], writes=[ps])
                    p.tt("dve", tmp[0:nr, :], ps[0:nr, :], St[0:nr, sl], ALU.mult, [ps, St], [tmp])
                    p.tt("pool", o1[0:nr, :], x_[0:nr, sl], Ct[0:nr, sl], ALU.mult, [x_, Ct], [o1])
                    p.tt("dve", o_[0:nr, sl], o1[0:nr, :], tmp[0:nr, :], ALU.add, [o1, tmp], [o_])
                p.dma("pool", c.dsaT[dst:dst + nr, 0:T], o_[0:nr, :], reads=[o_], writes=[c.dsaT])
    with p.scope():
        ik2 = p.sb("ik2", [128, T], BF16)
        k2 = [p.sb("k2_%d" % g, [128, T], BF16) for g in range(4)]
        vtok = p.sb("vtok", [128, NQ, 256], BF16)
        P0 = p.ps("dP0", [128, 2048], BF16)
        PL = [p.ps("dPL%d" % i, [128, 512], F32) for i in range(2)]
        PO = p.ps("dPO", [128, 1024], F32)
        PR = p.ps("dPR", [128, 1024], F32)
        for b in (0, 64):
            p.dma("sp", ik2[b:b + 64, :], c.dsaT[R_IK:R_IK + 64, 0:T], reads=[c.dsaT], writes=[ik2])
            for g in range(4):
                p.dma("sp", k2[g][b:b + 64, :], c.dsaT[R_K + g * 64:R_K + g * 64 + 64, 0:T], reads=[c.dsaT], writes=[k2[g]])
        with p.scope():
            vf = p.sb("vf", [128, 2, T], F32)
            vb = p.sb("vb", [128, 2, T], BF16)
            p.dma("sp", vf[:], c.projT.h[C_V:C_V + 256, :].rearrange("(k p) t -> p k t", p=128), reads=[c.projT], writes=[vf])
            p.copy("pool", vb[:], vf[:], [vf], [vb])
            for b0 in range(0, NQ, 8):
                for blk in range(b0, b0 + 8):
                    for t2 in range(2):
                        i = (blk - b0) * 2 + t2
                        p.tr(P0[:, i * 128:(i + 1) * 128], vb[:, t2, blk * 128:(blk + 1) * 128], IDb, [vb, c.cm_bf], [P0])
                p.copy("dve", vtok[:, b0:b0 + 8, :].rearrange("p b c -> p (b c)"), P0[:, 0:2048], [P0], [vtok])
        S = p.sb("dS", [128, T], F32)
        Wk = p.sb("dWk", [128, T], F32)
        msk = p.sb("dmsk", [128, T], BF16)
        mskT = p.sb("dmskT", [128, NQ, 128], BF16)
        qsb = [p.sb("dq%d" % i, [128, 8, 128], BF16) for i in range(2)]
        iqs = [p.sb("diq%d" % i, [128, 8, 128], BF16) for i in range(2)]
        gts = [p.sb("dg%d" % i, [128, 8, 128], F32) for i in range(2)]
        iwc = [p.sb("diw%d" % i, [16, 128], F32) for i in range(2)]
        rl = [p.sb("drl%d" % i, [128, 512], F32) for i in range(2)]
        PT = [p.sb("dPT%d" % i, [128, 512], BF16) for i in range(2)]
        PTm = [p.sb("dPTm%d" % i, [128, 512], BF16) for i in range(2)]
        iwk = p.sb("diwk", [128, 16], F32)
        m8 = p.sb("dm8", [128, 8], F32)
        thr = p.sb("dthr", [128, 1], F32)
        rec = p.sb("drec", [128, 1024], F32)
        ov = p.sb("dov", [128, 1024], F32)
        sg = p.sb("dsg", [128, 1024], F32)
        yo = [p.sb("dyo%d" % i, [128, 8, 128], BF16) for i in range(2)]
        qv = c.dsaT.h[R_Q:R_Q + 1024, :].rearrange("(k p) t -> p k t", p=128)
        iqv = c.dsaT.h[R_IQ:R_IQ + 1024, :].rearrange("(k p) t -> p k t", p=128)
        gv = c.projT.h[C_G:C_G + 1024, :].rearrange("(k p) t -> p k t", p=128)
        ycv = c.ycT.h.rearrange("(k p) t -> p k t", p=128)
        li = 0
        for qi in range(NQ):
            q0 = qi * 128
            n = q0 + 128
            qs_, iq_, g_, iw_, y_ = qsb[qi % 2], iqs[qi % 2], gts[qi % 2], iwc[qi % 2], yo[qi % 2]
            p.dma("sp", qs_[:], qv[:, :, q0:n], reads=[c.dsaT], writes=[qs_])
            p.dma("sp", iq_[:], iqv[:, :, q0:n], reads=[c.dsaT], writes=[iq_])
            p.dma("sp", g_[:], gv[:, :, q0:n], reads=[c.projT], writes=[g_])
            p.dma("sp", iw_[:], c.projT[C_IK + 64:C_IK + 80, q0:n], reads=[c.projT], writes=[iw_])
            p.tr(PO[:, 0:16], iw_[:], ID[0:16, 0:16], [iw_, c.cm], [PO])
            p.copy("act", iwk[:], PO[:, 0:16], [PO], [iwk])
            for s0 in range(0, n, 512):
                w = min(512, n - s0)
                for h in range(16):
                    b = (h % 2) * 64
                    pl = PL[li % 2]
                    r_ = rl[li % 2]
                    li += 1
                    p.mm(pl[:, 0:w], iq_[b:b + 64, h // 2, :], ik2[b:b + 64, s0:s0 + w], reads=[iq_, ik2], writes=[pl])
                    p.act(r_[:, 0:w], pl[:, 0:w], AF.Relu, [pl], [r_])
                    if h == 0:
                        p.ts("dve", S[:, s0:s0 + w], r_[:, 0:w], iwk[:, 0:1], ALU.mult, [r_, iwk], [S])
                    else:
                        p.stt(S[:, s0:s0 + w], r_[:, 0:w], iwk[:, h:h + 1], S[:, s0:s0 + w], ALU.mult, ALU.add, [r_, iwk, S], [S])
            p.tt("dve", S[:, q0:n], S[:, q0:n], NEGM, ALU.add, [S, c.cm], [S])
            if qi >= 2:
                p.copy("dve", Wk[:, 0:n], S[:, 0:n], [S], [Wk])
                for rnd in range(32):
                    p.op("dve", lambda e, a=Wk[:, 0:n]: e.max(out=m8[:], in_=a), [Wk], [m8])
                    if rnd < 31:
                        p.op("dve", lambda e, a=Wk[:, 0:n]: e.match_replace(out=a, in_to_replace=m8[:], in_values=a,
                                                                           imm_value=NEGBIG), [Wk, m8], [Wk])
                p.copy("dve", thr[:], m8[:, 7:8], [m8], [thr])
            else:
                p.memset("dve", thr[:], -1.0e29, writes=[thr])
            p.ts("dve", msk[:, 0:n], S[:, 0:n], thr[:, 0:1], ALU.is_ge, [S, thr], [msk])
            for b0 in range(0, qi + 1, 16):
                nb = min(16, qi + 1 - b0)
                for j in range(nb):
                    p.tr(P0[:, j * 128:(j + 1) * 128], msk[:, (b0 + j) * 128:(b0 + j + 1) * 128], IDb, [msk, c.cm_bf], [P0])
                p.copy("act", mskT[:, b0:b0 + nb, :].rearrange("p b c -> p (b c)"), P0[:, 0:nb * 128], [P0], [mskT])
            for h in range(16):
                g = h // 4
                b = (h % 2) * 64
                oc = slice((h // 2) * 128, (h // 2) * 128 + 128)
                for s0b in range(0, qi + 1, 4):
                    nb = min(4, qi + 1 - s0b)
                    pl = PL[li % 2]
                    pt, ptm = PT[li % 2], PTm[li % 2]
                    li += 1
                    for j in range(nb):
                        sb = s0b + j
                        p.mm(pl[:, j * 128:(j + 1) * 128], k2[g][b:b + 64, sb * 128:(sb + 1) * 128], qs_[b:b + 64, h // 2, :],
                             reads=[k2[g], qs_], writes=[pl])
                    p.act(pt[:, 0:nb * 128], pl[:, 0:nb * 128], AF.Exp, [pl], [pt], scale=0.125)
                    p.tt("pool" if li % 2 else "dve", ptm[:, 0:nb * 128], pt[:, 0:nb * 128],
                         mskT[:, s0b:s0b + nb, :].rearrange("p b c -> p (b c)"), ALU.mult, [pt, mskT], [ptm])
                    for j in range(nb):
                        sb = s0b + j
                        p.mm(PO[b:b + 64, oc], vtok[:, sb, g * 64:(g + 1) * 64], ptm[:, j * 128:(j + 1) * 128],
                             start=(sb == 0), stop=(sb == qi), reads=[vtok, ptm], writes=[PO])
                        p.mm(PR[b:b + 64, oc], c.ones_bf[:, 0:64], ptm[:, j * 128:(j + 1) * 128],
                             start=(sb == 0), stop=(sb == qi), reads=[c.ones_bf, ptm], writes=[PR])
            p.op("dve", lambda e: e.reciprocal(out=rec[:], in_=PR[:]), [PR], [rec])
            p.tt("dve", ov[:], PO[:], rec[:], ALU.mult, [PO, rec], [ov])
            p.act(sg[:], g_[:].rearrange("p j l -> p (j l)"), AF.Silu, [g_], [sg])
            p.tt("pool", y_[:].rearrange("p j l -> p (j l)"), ov[:], sg[:], ALU.mult, [ov, sg], [y_])
            p.dma("pool", ycv[:, :, q0:n], y_[:], reads=[y_], writes=[c.ycT])


def phase_rwkv(p, c, l):
    T = c.T
    NB = T // 128
    ID = c.cm[:, 0:128]
    ONEC = c.cm[:, 384:385]
    NHALF = c.cm[:, 771:772]
    GNEPS = c.cm[:, 772:773]
    BLK = c.cm[:, 896:1024]
    rows5 = c.rowsD.h.rearrange("t (f c) -> t f c", f=5)
    with p.scope():
        negw0 = p.sb("negw0", [128, 8], F32)
        o_w0, _ = CV["w0"]
        p.ts("dve", negw0[:], c.cv[:, l * NV + o_w0:l * NV + o_w0 + 8], -1.0, ALU.mult, [c.cv], [negw0])
        w2s = p.sb("w2s", [128, 1024], F32)
        w2b = p.sb("w2b", [128, 1024], BF16)
        p.dma("sp", w2s[0:64, :], c.rw2.h[l], reads=[c.rw2], writes=[w2s])
        p.dma("sp", w2s[64:128, :], c.ra2.h[l], reads=[c.ra2], writes=[w2s])
        p.copy("dve", w2b[:], w2s[:], [w2s], [w2b])
        xr = [p.sb("bxr%d" % i, [128, T + 1], F32) for i in range(3)]
        mx = {n: p.sb("bm_" + n, [128, T], F32) for n in ("wa", "r", "k", "v")}
        wab = p.sb("wab", [128, T], BF16)
        dec = p.sb("bdec", [128, T], F32)
        av = p.sb("bav", [128, T], F32)
        kkn = p.sb("bkkn", [128, T], F32)
        t2 = p.sb("bt2", [128, 512], F32)
        t3 = p.sb("bt3", [128, 512], F32)
        tro = [p.sb("btro%d" % i, [128, 5, 128], F32) for i in range(2)]
        PS = [p.ps("bps%d" % i, [128, 512], F32) for i in range(4)]
        PTt = [p.ps("bpt%d" % i, [128, 5 * 128], F32) for i in range(2)]
        for i in range(3):
            p.memset("pool", xr[i][:, 0:1], 0.0, writes=[xr[i]])
        xi = [0]

        def mix(src, mu_name, mu_j, dst):
            x_ = xr[xi[0] % 3]
            xi[0] += 1
            p.dma("sp", x_[:, 1:T + 1], c.projT[src:src + 128, 0:T], reads=[c.projT], writes=[x_])
            p.tt("pool", dst[:], x_[:, 0:T], x_[:, 1:T + 1], ALU.subtract, [x_], [dst])
            p.stt(dst[:], dst[:], cvcol(c, l, mu_name, mu_j), x_[:, 1:T + 1], ALU.mult, ALU.add, [dst, x_, c.cv], [dst])

        mix(B_WA, "mu_wa", 0, mx["wa"])
        p.act(mx["wa"][0:64, :], mx["wa"][0:64, :], AF.Tanh, [mx["wa"]], [mx["wa"]])
        p.copy("dve", wab[:], mx["wa"][:], [mx["wa"]], [wab])
        for j in range(8):
            mix(B_R + j * 128, "mu_r", j, mx["r"])
            mix(B_K + j * 128, "mu_k", j, mx["k"])
            mix(B_V + j * 128, "mu_v", j, mx["v"])
            p.dma("pool", c.vD[j * 128:(j + 1) * 128, 0:T], mx["v"][:], reads=[mx["v"]], writes=[c.vD])
            for q in range(T // 512):
                sl = slice(q * 512, (q + 1) * 512)
                pw, pa, pk, pr = PS
                p.mm(pw[:], w2b[0:64, j * 128:(j + 1) * 128], wab[0:64, sl], reads=[w2b, wab], writes=[pw])
                p.mm(pa[:], w2b[64:128, j * 128:(j + 1) * 128], wab[64:128, sl], reads=[w2b, wab], writes=[pa])
                p.act(t2[:], pw[:], AF.Exp, [pw, negw0], [t2], bias=negw0[:, j:j + 1], scale=-1.0)
                p.act(t2[:], t2[:], AF.Ln, [t2, c.cm], [t2], bias=ONEC)
                p.act(t2[:], t2[:], AF.Exp, [t2, c.cm], [t2], bias=NHALF, scale=-1.0)
                p.act(dec[:, sl], t2[:], AF.Exp, [t2], [dec], scale=-1.0)
                p.act(av[:, sl], pa[:], AF.Sigmoid, [pa, c.cv], [av], bias=cvcol(c, l, "a0", j))
                p.ts("dve", kkn[:, sl], mx["k"][:, sl], cvcol(c, l, "k_k", j), ALU.mult, [mx["k"], c.cv], [kkn])
                p.tt("pool", t3[:], kkn[:, sl], kkn[:, sl], ALU.mult, [kkn], [t3])
                p.mm(pk[:], BLK, t3[:], reads=[c.cm, t3], writes=[pk])
                p.act(t3[:], pk[:], AF.Sqrt, [pk], [t3])
                p.ts("dve", t3[:], t3[:], 1e-12, ALU.max, [t3], [t3])
                p.op("dve", lambda e, a=t3[:]: e.reciprocal(out=a, in_=a), [t3], [t3])
                p.tt("dve", kkn[:, sl], kkn[:, sl], t3[:], ALU.mult, [kkn, t3], [kkn])
                p.ts("dve", t3[:], av[:, sl], -1.0, ALU.add, [av], [t3], s2=cvcol(c, l, "k_a", j), op1=ALU.mult)
                p.ts("dve", t3[:], t3[:], 1.0, ALU.add, [t3], [t3])
                p.tt("dve", mx["k"][:, sl], mx["k"][:, sl], t3[:], ALU.mult, [mx["k"], t3], [mx["k"]])
                p.tt("pool", av[:, sl], av[:, sl], kkn[:, sl], ALU.mult, [av, kkn], [av])
                p.tt("pool", t3[:], mx["r"][:, sl], mx["k"][:, sl], ALU.mult, [mx["r"], mx["k"]], [t3])
                p.ts("dve", t3[:], t3[:], cvcol(c, l, "r_k", j), ALU.mult, [t3, c.cv], [t3])
                p.mm(pr[:], BLK, t3[:], reads=[c.cm, t3], writes=[pr])
                p.copy("act", t2[:], pr[:], [pr], [t2])
                p.dma("pool", c.rkD[j * 128:(j + 1) * 128, sl], t2[:], reads=[t2], writes=[c.rkD])
            srcs = [kkn, dec, av, mx["k"], mx["r"]]
            for bi in range(NB):
                pt = PTt[bi % 2]
                to = tro[bi % 2]
                for f in range(5):
                    p.tr(pt[:, f * 128:(f + 1) * 128], srcs[f][:, bi * 128:(bi + 1) * 128], ID, [srcs[f], c.cm], [pt])
                p.copy("act" if bi % 2 else "dve", to[:].rearrange("p f c -> p (f c)"), pt[:], [pt], [to])
                p.dma("pool", rows5[bi * 128:(bi + 1) * 128, :, j * 128:(j + 1) * 128], to[:], reads=[to], writes=[c.rowsD])
    TB = 128
    with p.scope():
        S = p.sb("rS", [64, 1024], F32)
        tmp = p.sb("rtmp", [64, 1024], F32)
        tmp2 = p.sb("rtmp2", [64, 1024], F32)
        sa = p.sb("rsa", [64, 16], F32)
        Vc = [p.sb("rVc%d" % i, [64, 16, TB], F32) for i in range(2)]
        Yc = [p.sb("rYc%d" % i, [64, 16, TB], F32) for i in range(2)]
        NBUF = 4
        rb = [p.sb("rrow%d" % i, [64, 5, 1024], F32) for i in range(NBUF)]
        p.memset("dve", S[:], 0.0, writes=[S])
        vv = c.vD.h.rearrange("(h v) t -> v h t", v=64)
        yv = c.yD.h.rearrange("(h v) t -> v h t", v=64)
        S3 = S[:].rearrange("p (h k) -> p h k", k=64)
        T3 = tmp[:].rearrange("p (h k) -> p h k", k=64)
        U3 = tmp2[:].rearrange("p (h k) -> p h k", k=64)
        for t in range(T):
            blk, tl = divmod(t, TB)
            vc, yc = Vc[blk % 2], Yc[blk % 2]
            if tl == 0:
                p.dma("sp", vc[:], vv[:, :, blk * TB:(blk + 1) * TB], reads=[c.vD], writes=[vc])
            r_ = rb[t % NBUF]
            p.dma("sp", r_[:].rearrange("p f c -> p (f c)"), c.rowsD[t:t + 1, :].to_broadcast([64, 5 * 1024]),
                  reads=[c.rowsD], writes=[r_])
            KK, W, B_, K_, R_ = [r_[:, f, :] for f in range(5)]
            p.tt("dve", tmp[:], S[:], KK, ALU.mult, [S, r_], [tmp])
            p.op("dve", lambda e, o=sa[:], i=T3: e.tensor_reduce(out=o, in_=i, op=ALU.add, axis=AX.X, negate=True), [tmp], [sa])
            p.tt("pool", S[:], S[:], W, ALU.mult, [S, r_], [S])
            p.tt("pool", U3, vc[:, :, tl:tl + 1].to_broadcast([64, 16, 64]), K_.rearrange("p (h k) -> p h k", k=64), ALU.mult,
                 [vc, r_], [tmp2])
            p.tt("pool", S[:], S[:], tmp2[:], ALU.add, [S, tmp2], [S])
            p.tt("dve", T3, sa[:].unsqueeze(2).to_broadcast([64, 16, 64]), B_.rearrange("p (h k) -> p h k", k=64), ALU.mult,
                 [sa, r_], [tmp])
            p.tt("dve", S[:], S[:], tmp[:], ALU.add, [S, tmp], [S])
            p.tt("dve", tmp[:], S[:], R_, ALU.mult, [S, r_], [tmp])
            p.op("dve", lambda e, o=yc[:, :, tl], i=T3: e.tensor_reduce(out=o, in_=i, op=ALU.add, axis=AX.X), [tmp], [yc])
            if tl == TB - 1:
                p.dma("pool", yv[:, :, blk * TB:(blk + 1) * TB], yc[:], reads=[yc], writes=[c.yD])
    with p.scope():
        yt = [p.sb("cy%d" % i, [128, T], F32) for i in range(2)]
        vt = [p.sb("cv_%d" % i, [128, T], F32) for i in range(2)]
        rkt = [p.sb("crk%d" % i, [128, T], F32) for i in range(2)]
        gt = [p.sb("cg%d" % i, [128, T], F32) for i in range(2)]
        ob = [p.sb("cob%d" % i, [128, T], BF16) for i in range(2)]
        t2 = p.sb("ct2", [128, 512], F32)
        t3 = p.sb("ct3", [128, 512], F32)
        PS = [p.ps("cps%d" % i, [128, 512], F32) for i in range(4)]
        for j in range(8):
            y_, v_, rk_, g_, o_ = yt[j % 2], vt[j % 2], rkt[j % 2], gt[j % 2], ob[j % 2]
            rs_ = slice(j * 128, (j + 1) * 128)
            p.dma("sp", y_[:], c.yD[rs_, 0:T], reads=[c.yD], writes=[y_])
            p.dma("sp", v_[:], c.vD[rs_, 0:T], reads=[c.vD], writes=[v_])
            p.dma("sp", rk_[:], c.rkD[rs_, 0:T], reads=[c.rkD], writes=[rk_])
            p.dma("sp", g_[:], c.projT[B_G + j * 128:B_G + (j + 1) * 128, 0:T], reads=[c.projT], writes=[g_])
            for q in range(T // 512):
                sl = slice(q * 512, (q + 1) * 512)
                pm, pv = PS[(2 * q) % 4], PS[(2 * q + 1) % 4]
                p.mm(pm[:], BLK, y_[:, sl], reads=[c.cm, y_], writes=[pm])
                p.stt(y_[:, sl], pm[:], -1.0 / 64, y_[:, sl], ALU.mult, ALU.add, [pm, y_], [y_])
                p.tt("pool", t2[:], y_[:, sl], y_[:, sl], ALU.mult, [y_], [t2])
                p.mm(pv[:], BLK, t2[:], reads=[c.cm, t2], writes=[pv])
                p.ts("dve", t3[:], pv[:], 1.0 / 64, ALU.mult, [pv], [t3], s2=64e-5, op1=ALU.add)
                p.act(t3[:], t3[:], AF.Sqrt, [t3], [t3])
                p.op("dve", lambda e, a=t3[:]: e.reciprocal(out=a, in_=a), [t3], [t3])
                p.tt("dve", y_[:, sl], y_[:, sl], t3[:], ALU.mult, [y_, t3], [y_])
                p.ts("dve", y_[:, sl], y_[:, sl], cvcol(c, l, "ln_w", j), ALU.mult, [y_, c.cv], [y_],
                     s2=cvcol(c, l, "ln_b", j), op1=ALU.add)
                p.tt("pool", t2[:], rk_[:, sl], v_[:, sl], ALU.mult, [rk_, v_], [t2])
                p.tt("dve", y_[:, sl], y_[:, sl], t2[:], ALU.add, [y_, t2], [y_])
                p.act(t2[:], g_[:, sl], AF.Silu, [g_], [t2])
                p.tt("dve", o_[:, sl], y_[:, sl], t2[:], ALU.mult, [y_, t2], [o_])
            p.dma("pool", c.ybT[rs_, 0:T], o_[:], reads=[o_], writes=[c.ybT])


def pad_w_in(w):
    o = np.zeros((2048, NCP), np.float32)
    o[:, 0:3088] = w[:, 0:3088]
    b = 3088
    o[:, B_R:B_R + 1024] = w[:, b:b + 1024]
    o[:, B_WA:B_WA + 64] = w[:, b + 1024:b + 1088]
    o[:, B_K:B_K + 1024] = w[:, b + 1088:b + 2112]
    o[:, B_V:B_V + 1024] = w[:, b + 2112:b + 3136]
    o[:, B_WA + 64:B_WA + 128] = w[:, b + 3136:b + 3200]
    o[:, B_G:B_G + 1024] = w[:, b + 3200:b + 4224]
    cc = 7312
    o[:, C_Q:C_Q + 1024] = w[:, cc:cc + 1024]
    o[:, C_K:C_K + 256] = w[:, cc + 1024:cc + 1280]
    o[:, C_V:C_V + 256] = w[:, cc + 1280:cc + 1536]
    o[:, C_G:C_G + 1024] = w[:, cc + 1536:cc + 2560]
    o[:, C_IQ:C_IQ + 1024] = w[:, cc + 2560:cc + 3584]
    o[:, C_IK:C_IK + 80] = w[:, cc + 3584:cc + 3664]
    o[:, G0:G0 + 6144] = w[:, 10976:17120]
    return o

def col(v):
    v = np.asarray(v, np.float32).reshape(-1)
    n = (v.size + 127) // 128
    o = np.zeros((n * 128,), np.float32)
    o[:v.size] = v
    return o.reshape(n, 128).T

def pack_cvec(inp, L):
    cv = np.zeros((128, L * NV), np.float32)
    def put(l, name, arr):
        o, w = CV[name]
        assert arr.shape == (128, w), (name, arr.shape, w)
        cv[:, l * NV + o:l * NV + o + w] = arr
    for l in range(L):
        put(l, "pre_g", col(inp["pre_norm"][l]))
        put(l, "post_g", col(inp["post_norm"][l]))
        put(l, "bg", col(inp["b_gate"][l]))
        put(l, "conv_w", np.concatenate([col(inp["ssm_conv_w"][l][j]) for j in range(4)], axis=1))
        put(l, "conv_b", col(inp["ssm_conv_b"][l]))
        put(l, "dt_bias", col(inp["ssm_dt_bias"][l]))
        put(l, "a_log", col(inp["ssm_a_log"][l]))
        put(l, "d_exp", col(np.repeat(inp["ssm_d"][l], 64)))
        put(l, "ssm_norm", col(inp["ssm_norm"][l]))
        mu = inp["rwkv_mu"][l]
        put(l, "mu_r", col(mu[0:1024]))
        put(l, "mu_k", col(mu[1088:2112]))
        put(l, "mu_v", col(mu[2112:3136]))
        put(l, "mu_wa", col(np.concatenate([mu[1024:1088], mu[3136:3200]])))
        for n, k in [("w0", "rwkv_w0"), ("a0", "rwkv_a0"), ("k_k", "rwkv_k_k"), ("k_a", "rwkv_k_a"),
                     ("ln_w", "rwkv_ln_w"), ("ln_b", "rwkv_ln_b"), ("r_k", "rwkv_r_k")]:
            put(l, n, col(inp[k][l].reshape(-1)))
        put(l, "ikn_w", col(inp["idx_k_norm_w"][l]))
        put(l, "ikn_b", col(inp["idx_k_norm_b"][l]))
    return cv

def const_mats():
    k = np.arange(128)
    ident = np.eye(128, dtype=np.float32)
    triu = (k[:, None] <= k[None, :]).astype(np.float32)
    su = (k[:, None] > k[None, :]).astype(np.float32)
    ones = np.ones((128, 128), np.float32)
    z = np.zeros((128, 128), np.float32)
    perm = np.zeros((128, 128), np.float32)
    for m in range(128):
        perm[(m // 64) * 64 + ((m % 64) + 32) % 64, m] = 1.0
    negm = np.where(k[None, :] > k[:, None], -1e30, 0.0).astype(np.float32)
    misc = np.zeros((128, 128), np.float32)
    inv = 10000.0 ** (-(np.arange(32, dtype=np.float32) * 2.0 / 64))
    misc[:, 0] = inv[k % 32]
    misc[:, 1] = np.where((k % 64) < 32, -1.0, 1.0)
    misc[:, 2] = np.pi / 2
    misc[:, 3] = -0.5
    misc[:, 4] = 64e-5
    blk = (k[:, None] // 64 == k[None, :] // 64).astype(np.float32)
    return np.concatenate([ident, triu, su, ones, perm, negm, misc, blk], axis=1)


_NC_CACHE = {}


def build_program(T, L):
    nc = bass.Bass("TRN2", target_bir_lowering=False)
    es = ExitStack()
    with es:
        p = Prog(nc, es, nsem=100)
        c = Ctx()
        c.T = T
        c.xT0 = p.dram("xT0", [2048, T], F32, kind="ExternalInput")
        c.w_in_l = [p.dram("w_in%d" % l, [2048, NCP], F32, kind="ExternalInput") for l in range(L)]
        c.w_ba = p.dram("w_ba", [L, 1024, 2048], F32, kind="ExternalInput")
        c.w_bb = p.dram("w_bb", [L, 1024, 2048], F32, kind="ExternalInput")
        c.w_bc = p.dram("w_bc", [L, 1024, 2048], F32, kind="ExternalInput")
        c.w_out = p.dram("w_out", [L, 2048, 2048], F32, kind="ExternalInput")
        c.cvec_d = p.dram("cvec", [128, L * NV], F32, kind="ExternalInput")
        c.cmat_d = p.dram("cmat", [128, 1024], F32, kind="ExternalInput")
        c.pos_d = p.dram("pos", [128, T], I32, kind="ExternalInput")
        c.rw2 = p.dram("rw2", [L, 64, 1024], F32, kind="ExternalInput")
        c.ra2 = p.dram("ra2", [L, 64, 1024], F32, kind="ExternalInput")
        c.xo = p.dram("xo", [2048, T], F32, kind="ExternalOutput")
        xA = p.dram("xA", [2048, T], F32)
        xB = p.dram("xB", [2048, T], F32)
        c.projT = p.dram("projT", [G0, T], F32)
        c.projG = p.dram("projG", [NCP - G0, T], F32)
        c.xbcT = p.dram("xbcT", [2048, T], BF16)
        c.dsaT = p.dram("dsaT", [2432, T], BF16)
        c.yaT = p.dram("yaT", [1024, T], BF16)
        c.ybT = p.dram("ybT", [1024, T], BF16)
        c.ycT = p.dram("ycT", [1024, T], BF16)
        c.rowsD = p.dram("rowsD", [T, 5 * 1024], F32)
        c.vD = p.dram("vD", [1024, T], F32)
        c.rkD = p.dram("rkD", [1024, T], F32)
        c.yD = p.dram("yD", [1024, T], F32)
        alloc_consts(p, c, L)
        p.barrier()
        xs = [c.xT0] + [xA if (l % 2 == 0) else xB for l in range(L - 1)] + [c.xo]
        for l in range(L):
            phase_p1(p, c, l, xs[l])
            phase_ssd(p, c, l)
            phase_rwkv(p, c, l)
            phase_dsa(p, c, l)
            phase_p3(p, c, l, xs[l], xs[l + 1])
        p.barrier()
    return nc


def kernel(**inputs):
    inp = {k: np.asarray(v) for k, v in inputs.items()}
    x = inp["x"].astype(np.float32, copy=False)
    Bsz, T, _ = x.shape
    L = inp["w_in"].shape[0]
    key = (T, L)
    if key not in _NC_CACHE:
        _NC_CACHE[key] = build_program(T, L)
    nc = _NC_CACHE[key]
    cmat = const_mats()
    cvec = pack_cvec(inp, L)
    w_in_p = [pad_w_in(inp["w_in"][l]) for l in range(L)]
    maps = []
    for b in range(Bsz):
        m = {"xT0": np.ascontiguousarray(x[b].T), "w_ba": inp["w_branch_a"], "w_bb": inp["w_branch_b"],
             "w_bc": inp["w_branch_c"], "w_out": inp["w_out"], "cvec": cvec, "cmat": cmat,
             "pos": np.ascontiguousarray(np.broadcast_to(inp["positions"][b][None, :], (128, T))).astype(np.int32),
             "rw2": inp["rwkv_w2"], "ra2": inp["rwkv_a2"]}
        for l in range(L):
            m["w_in%d" % l] = w_in_p[l]
        maps.append(m)
    res = run_bass_kernel_spmd(nc, maps, core_ids=list(range(Bsz)))
    return np.stack([np.asarray(res.results[b]["xo"]).T for b in range(Bsz)]).astype(np.float32)
```

```python
import numpy as np
from contextlib import ExitStack
import concourse.bass as bass
import concourse.mybir as mybir
from concourse.bass_utils import run_bass_kernel_spmd

F32 = mybir.dt.float32
BF16 = mybir.dt.bfloat16
I32 = mybir.dt.int32
AF = mybir.ActivationFunctionType
ALU = mybir.AluOpType
AX = mybir.AxisListType

EPOCH = 24000
ENGS = ("pe", "dve", "act", "pool", "sp")


class Tl:
    def __init__(self, h, name):
        self.h = h
        self.name = name
        self.dram = False
        self.psum = False
        self.acc = {}
        self.w = None
        self.r = {}

    def __getitem__(self, k):
        return self.h[k]


class Prog:
    def __init__(self, nc, es, nsem=120, same_engine_sync=True):
        self.nc = nc
        self.es = es
        self.q = {e: [] for e in ENGS}
        self.n = {e: 0 for e in ENGS}
        self.waited = {e: {} for e in ENGS}
        self.lane_n = {}
        self.sem_pool = [es.enter_context(nc.semaphore("s%d" % i)) for i in range(nsem)]
        self.sem_map = {}
        self.sem_next = 0
        self.lane_sem = {}
        self.retired = []
        self.same = same_engine_sync
        self.ntile = 0

    def sb(self, name, shape, dt):
        self.ntile += 1
        h = self.es.enter_context(self.nc.sbuf_tensor("%s_%d" % (name, self.ntile), list(shape), dt))
        return Tl(h, name)

    def ps(self, name, shape, dt=F32):
        self.ntile += 1
        h = self.es.enter_context(self.nc.psum_tensor("%s_%d" % (name, self.ntile), list(shape), dt))
        t = Tl(h, name)
        t.psum = True
        return t

    def dram(self, name, shape, dt, kind="Internal"):
        h = self.nc.dram_tensor(name, list(shape), dt, kind=kind)
        t = Tl(h, name)
        t.dram = True
        return t

    def _fresh(self):
        if self.sem_next < len(self.sem_pool):
            self.sem_next += 1
            return self.sem_pool[self.sem_next - 1], 0
        self.retired.sort(key=lambda x: x[0])
        r, sem = self.retired.pop(0)
        return sem, r

    def _sem(self, src, val):
        if src in ENGS:
            ep = (val - 1) // EPOCH if val > 0 else 0
            key = (src, ep)
            if key not in self.sem_map:
                self.sem_map[key] = self._fresh()
            sem, r0 = self.sem_map[key]
            return sem, val - ep * EPOCH + r0
        if src not in self.lane_sem:
            sem, r = self._fresh()
            self.lane_sem[src] = (sem, r - (val - 16))
        sem, delta = self.lane_sem[src]
        assert 0 < val + delta < 30000, (src, val, delta)
        return sem, val + delta

    def retire_lanes(self):
        for lane, (sem, delta) in self.lane_sem.items():
            self.retired.append((self.lane_n[lane] + delta, sem))
        self.lane_sem = {}

    def _collect(self, eng, reads, writes):
        deps = {}

        def add(d):
            if d is None:
                return
            s, v = d
            if deps.get(s, 0) < v:
                deps[s] = v

        reads = [t for t in reads if not t.dram]
        writes = [t for t in writes if not t.dram]
        for t in list(reads) + list(writes):
            if t.psum:
                for s_, v_ in t.acc.items():
                    if s_ != eng:
                        add((s_, v_))
        for t in reads:
            add(t.w)
        for t in writes:
            add(t.w)
            for s, v in t.r.items():
                add((s, v))
        waits = []
        for s, v in deps.items():
            if s == eng:
                if eng == "pe" or not self.same:
                    continue
            if self.waited[eng].get(s, 0) >= v:
                continue
            self.waited[eng][s] = v
            waits.append((s, v))
        return waits

    def op(self, eng, fn, reads=(), writes=()):
        waits = self._collect(eng, reads, writes)
        self.n[eng] += 1
        me = (eng, self.n[eng])
        self._emit_now(eng, waits, fn, me)
        reads = [t for t in reads if not t.dram]
        writes = [t for t in writes if not t.dram]
        for t in list(reads) + list(writes):
            if t.psum:
                t.acc[eng] = me[1]
        for t in reads:
            if t.r.get(eng, 0) < me[1]:
                t.r[eng] = me[1]
        for t in writes:
            t.w = me
            t.r = {}

    def dma(self, qeng, out_ap, in_ap, reads=(), writes=(), lane=None):
        reads = [t for t in reads if not t.dram]
        writes = [t for t in writes if not t.dram]
        if lane is None:
            lane = "L_" + (writes[0].name if writes else reads[0].name)
        waits = self._collect(qeng, reads, writes)
        k = self.lane_n.get(lane, 0) + 16
        self.lane_n[lane] = k
        me = (lane, k)
        self._emit_now(qeng, waits, lambda e: e.dma_start(out=out_ap, in_=in_ap), me)
        for t in reads:
            if t.r.get(lane, 0) < k:
                t.r[lane] = k
        for t in writes:
            t.w = me
            t.r = {}

    def barrier(self):
        for e in ENGS:
            waits = []
            for s in list(ENGS) + list(self.lane_n):
                v = self.n[s] if s in ENGS else self.lane_n[s]
                if v == 0 or self.waited[e].get(s, 0) >= v:
                    continue
                self.waited[e][s] = v
                waits.append((s, v))
            self._emit_now(e, waits, None, None)

    def scope(self):
        prog = self

        class _S:
            def __enter__(s):
                s.old = prog.es
                s.st = ExitStack()
                prog.es = s.st
                return s

            def __exit__(s, *a):
                prog.barrier()
                prog.retire_lanes()
                s.st.close()
                prog.es = s.old
                return False
        return _S()

    def finish(self, out_tiles, eng="sp"):
        waits = self._collect(eng, out_tiles, ())
        self._emit_now(eng, waits, None, None)

    def _emit_now(self, eng, waits, fn, me):
        e = {"pe": self.nc.tensor, "dve": self.nc.vector, "act": self.nc.scalar, "pool": self.nc.gpsimd, "sp": self.nc.sync}[eng]
        for s, v in waits:
            sem, sv = self._sem(s, v)
            e.wait_ge(sem, sv)
        if fn is None:
            return
        ins = fn(e)
        src, val = me
        sem, sv = self._sem(src, val)
        ins.then_inc(sem, 1 if src in ENGS else 16)

    def _replay(self, eng, e):
        for waits, fn, me in self.q[eng]:
            for s, v in waits:
                sem, sv = self._sem(s, v)
                e.wait_ge(sem, sv)
            if fn is None:
                continue
            ins = fn(e)
            src, val = me
            sem, sv = self._sem(src, val)
            if src in ENGS:
                ins.then_inc(sem, 1)
            else:
                ins.then_inc(sem, 16)

    def emit(self):
        return
        nc = self.nc
        with nc.Block() as block:
            @block.sync
            def _(e):
                self._replay("sp", e)

            @block.tensor
            def _(e):
                self._replay("pe", e)

            @block.vector
            def _(e):
                self._replay("dve", e)

            @block.scalar
            def _(e):
                self._replay("act", e)

            @block.gpsimd
            def _(e):
                self._replay("pool", e)


def _mm(self, out, lhsT, rhs, start=True, stop=True, reads=(), writes=()):
    self.op("pe", lambda e: e.matmul(out, lhsT, rhs, start=start, stop=stop), reads, writes)

def _tr(self, out, in_, ident, reads=(), writes=()):
    self.op("pe", lambda e: e.transpose(out, in_, ident), reads, writes)

def _act(self, out, in_, func, reads=(), writes=(), bias=None, scale=1.0):
    if bias is None:
        self.op("act", lambda e: e.activation(out=out, in_=in_, func=func, scale=scale), reads, writes)
    else:
        self.op("act", lambda e: e.activation(out=out, in_=in_, func=func, bias=bias, scale=scale), reads, writes)

def _tt(self, eng, out, in0, in1, op, reads=(), writes=()):
    self.op(eng, lambda e: e.tensor_tensor(out=out, in0=in0, in1=in1, op=op), reads, writes)

def _ts(self, eng, out, in0, s1, op0, reads=(), writes=(), s2=None, op1=None):
    if op1 is None:
        self.op(eng, lambda e: e.tensor_scalar(out=out, in0=in0, scalar1=s1, scalar2=None, op0=op0), reads, writes)
    else:
        self.op(eng, lambda e: e.tensor_scalar(out=out, in0=in0, scalar1=s1, scalar2=s2, op0=op0, op1=op1), reads, writes)

def _stt(self, out, in0, scalar, in1, op0, op1, reads=(), writes=()):
    self.op("dve", lambda e: e.scalar_tensor_tensor(out=out, in0=in0, scalar=scalar, in1=in1, op0=op0, op1=op1), reads, writes)

def _copy(self, eng, out, in_, reads=(), writes=()):
    if eng == "act":
        self.op("act", lambda e: e.copy(out=out, in_=in_), reads, writes)
    else:
        self.op(eng, lambda e: e.tensor_copy(out=out, in_=in_), reads, writes)

def _memset(self, eng, ap, val, writes=()):
    self.op(eng, lambda e: e.memset(ap, val), (), writes)

Prog.mm = _mm
Prog.tr = _tr
Prog.act = _act
Prog.tt = _tt
Prog.ts = _ts
Prog.stt = _stt
Prog.copy = _copy
Prog.memset = _memset


D = 2048
NCP = 17280
A_Z, A_XBC, A_DT = 0, 1024, 3072
B0 = 3200
B_R, B_K, B_V, B_G, B_WA = B0, B0 + 1024, B0 + 2048, B0 + 3072, B0 + 4096
C0 = B0 + 4224
C_Q, C_K, C_V, C_G, C_IQ, C_IK = C0, C0 + 1024, C0 + 1280, C0 + 1536, C0 + 2560, C0 + 3584
G0 = C0 + 3712
assert G0 + 6144 == NCP
EPS = 1e-6

CV = {}
_o = 0
for _n, _w in [("pre_g", 16), ("post_g", 16), ("bg", 48), ("conv_w", 64), ("conv_b", 16), ("dt_bias", 1),
               ("a_log", 1), ("d_exp", 8), ("ssm_norm", 8), ("mu_r", 8), ("mu_k", 8), ("mu_v", 8), ("mu_wa", 1),
               ("w0", 8), ("a0", 8), ("k_k", 8), ("k_a", 8), ("ln_w", 8), ("ln_b", 8), ("r_k", 8),
               ("ikn_w", 1), ("ikn_b", 1)]:
    CV[_n] = (_o, _w)
    _o += _w
NV = _o


class Ctx:
    pass


def cvcol(c, l, name, j=0, rows=128):
    o, w = CV[name]
    return c.cv[0:rows, l * NV + o + j: l * NV + o + j + 1]


def alloc_consts(p, c, L):
    c.cv = p.sb("cv", [128, L * NV], F32)
    p.dma("sp", c.cv[:], c.cvec_d[:], reads=[c.cvec_d], writes=[c.cv])
    c.ones_bf = p.sb("ones_bf", [128, 128], BF16)
    p.memset("dve", c.ones_bf[:], 1.0, writes=[c.ones_bf])
    c.cm = p.sb("cm", [128, 10 * 128], F32)
    p.dma("sp", c.cm[:], c.cmat_d[:], reads=[c.cmat_d], writes=[c.cm])
    c.cm_bf = p.sb("cm_bf", [128, 10 * 128], BF16)
    p.copy("dve", c.cm_bf[:], c.cm[:], [c.cm], [c.cm_bf])


def rstd_from_ss(p, rs, ss, n, width):
    p.ts("dve", rs[:, 0:width], ss[:, 0:width], 1.0 / n, ALU.mult, [ss], [rs], s2=EPS, op1=ALU.add)
    p.act(rs[:, 0:width], rs[:, 0:width], AF.Sqrt, [rs], [rs])
    p.op("dve", lambda e: e.reciprocal(out=rs[:, 0:width], in_=rs[:, 0:width]), [rs], [rs])


def phase_p1(p, c, l, xin):
    T = c.T
    TT = min(T, 2048)
    NJ = TT // 512
    with p.scope():
        xg = p.sb("xg", [128, 16, TT], BF16)
        xst = [p.sb("xst%d" % i, [128, TT], F32) for i in range(2)]
        sq = [p.sb("sq%d" % i, [128, TT], BF16) for i in range(2)]
        rs = p.sb("rs", [128, TT], F32)
        wst = [p.sb("wst%d" % i, [128, 16, 256], F32) for i in range(2)]
        wbf = [p.sb("wbf%d" % i, [128, 16, 256], BF16) for i in range(2)]
        osb = [p.sb("osb%d" % i, [128, TT], F32) for i in range(2)]
        ss = p.ps("ss", [128, TT], F32)
        acc = [p.ps("acc%d" % i, [128, 512], F32) for i in range(8 - NJ)]
        for tt in range(T // TT):
            t0 = tt * TT
            for k in range(16):
                xs = xst[k % 2]
                p.dma("sp", xs[:], xin[k * 128:(k + 1) * 128, t0:t0 + TT], reads=[xin], writes=[xs])
                p.ts("dve", xg[:, k, :], xs[:], cvcol(c, l, "pre_g", k), ALU.mult, [xs, c.cv], [xg])
                p.tt("pool", sq[k % 2][:], xs[:], xs[:], ALU.mult, [xs], [sq[k % 2]])
                for j in range(NJ):
                    p.mm(ss[:, j * 512:(j + 1) * 512], c.ones_bf[:], sq[k % 2][:, j * 512:(j + 1) * 512],
                         start=(k == 0), stop=(k == 15), reads=[c.ones_bf, sq[k % 2]], writes=[ss])
            rstd_from_ss(p, rs, ss, D, TT)
            ng = (NCP + 255) // 256
            ai = 0
            oi = 0
            wv = c.w_in_l[l].h.rearrange("(k p) c -> p k c", p=128)
            for g in range(ng):
                c0 = g * 256
                cw = min(256, NCP - c0)
                ws, wb = wst[g % 2], wbf[g % 2]
                p.dma("sp", ws[:, :, 0:cw], wv[:, :, c0:c0 + cw], reads=[c.w_in_l[l]], writes=[ws])
                p.copy("pool" if g % 2 == 0 else "act", wb[:, :, 0:cw], ws[:, :, 0:cw], [ws], [wb])
                for ct in range(cw // 128):
                    ob = osb[oi % 2]
                    oi += 1
                    for j in range(NJ):
                        a = acc[ai % len(acc)]
                        ai += 1
                        for k in range(16):
                            p.mm(a[:], wb[:, k, ct * 128:(ct + 1) * 128], xg[:, k, j * 512:(j + 1) * 512],
                                 start=(k == 0), stop=(k == 15), reads=[wb, xg], writes=[a])
                        p.tt("dve", ob[:, j * 512:(j + 1) * 512], a[:], rs[:, j * 512:(j + 1) * 512], ALU.mult,
                             [a, rs], [ob])
                    r0 = c0 + ct * 128
                    if r0 < G0:
                        p.dma("pool", c.projT[r0:r0 + 128, t0:t0 + TT], ob[:], reads=[ob], writes=[c.projT])
                    else:
                        p.dma("pool", c.projG[r0 - G0:r0 - G0 + 128, t0:t0 + TT], ob[:], reads=[ob], writes=[c.projG])


def phase_p3(p, c, l, xin, xout):
    T = c.T
    TT = 512
    wbr = [c.w_ba, c.w_bb, c.w_bc]
    ybr = [c.yaT, c.ybT, c.ycT]
    with p.scope():
        ysb = [p.sb("ysb%d" % i, [128, 8, TT], BF16) for i in range(3)]
        mg = p.sb("mg", [128, 16, TT], BF16)
        osb = p.sb("o3", [128, 16, TT], F32)
        wst = [p.sb("w3st%d" % i, [128, 8, 128], F32) for i in range(2)]
        wb = [p.sb("w3bf%d" % i, [128, 8, 128], BF16) for i in range(2)]
        wst2 = [p.sb("wost%d" % i, [128, 16, 128], F32) for i in range(2)]
        wb2 = [p.sb("wobf%d" % i, [128, 16, 128], BF16) for i in range(2)]
        pg = [p.sb("pg%d" % i, [128, TT], F32) for i in range(2)]
        gt = [p.sb("gt%d" % i, [128, TT], F32) for i in range(2)]
        macc = p.sb("macc", [128, TT], F32)
        tmp = p.sb("tmp3", [128, TT], F32)
        sq = [p.sb("sq3%d" % i, [128, TT], BF16) for i in range(2)]
        rs = p.sb("rs3", [128, TT], F32)
        xt = [p.sb("xt3%d" % i, [128, TT], F32) for i in range(2)]
        xo = [p.sb("xo3%d" % i, [128, TT], F32) for i in range(2)]
        acc = [p.ps("acc3%d" % i, [128, 512], F32) for i in range(4)]
        ss = p.ps("ss3", [128, 512], F32)
        ai = 0
        wi = 0
        import os
        STOP = int(os.environ.get("P3STOP", "99"))
        for tt in range(T // TT):
            t0 = tt * TT
            for i in range(3):
                yv = ybr[i].h.rearrange("(k p) t -> p k t", p=128)
                p.dma("sp", ysb[i][:], yv[:, :, t0:t0 + TT], reads=[ybr[i]], writes=[ysb[i]])
            for ct in range(16):
                for i in range(3):
                    ws, wbb = wst[wi % 2], wb[wi % 2]
                    pgt, gtt = pg[wi % 2], gt[wi % 2]
                    wi += 1
                    wv = wbr[i].h[l].rearrange("(k p) c -> p k c", p=128)
                    p.dma("sp", ws[:], wv[:, :, ct * 128:(ct + 1) * 128], reads=[wbr[i]], writes=[ws])
                    p.copy("pool", wbb[:], ws[:], [ws], [wbb])
                    a = acc[ai % 4]
                    ai += 1
                    for k in range(8):
                        p.mm(a[:], wbb[:, k, :], ysb[i][:, k, :], start=(k == 0), stop=(k == 7),
                             reads=[wbb, ysb[i]], writes=[a])
                    r0 = i * 2048 + ct * 128
                    p.dma("sp", pgt[:], c.projG[r0:r0 + 128, t0:t0 + TT], reads=[c.projG], writes=[pgt])
                    p.act(gtt[:], pgt[:], AF.Sigmoid, [pgt, c.cv], [gtt], bias=cvcol(c, l, "bg", i * 16 + ct))
                    if i == 0:
                        p.tt("dve", macc[:], a[:], gtt[:], ALU.mult, [a, gtt], [macc])
                    else:
                        p.tt("dve", tmp[:], a[:], gtt[:], ALU.mult, [a, gtt], [tmp])
                        if i == 1:
                            p.tt("dve", macc[:], macc[:], tmp[:], ALU.add, [macc, tmp], [macc])
                        else:
                            p.tt("dve", mg[:, ct, :], macc[:], tmp[:], ALU.add, [macc, tmp], [mg])
            if STOP <= 1:
                continue
            for c2 in range(16):
                ws, wbb = wst2[c2 % 2], wb2[c2 % 2]
                wv = c.w_out.h[l].rearrange("(k p) c -> p k c", p=128)
                p.dma("sp", ws[:], wv[:, :, c2 * 128:(c2 + 1) * 128], reads=[c.w_out], writes=[ws])
                p.copy("pool", wbb[:], ws[:], [ws], [wbb])
                a = acc[ai % 4]
                ai += 1
                for k in range(16):
                    p.mm(a[:], wbb[:, k, :], mg[:, k, :], start=(k == 0), stop=(k == 15), reads=[wbb, mg], writes=[a])
                p.copy("dve", osb[:, c2, :], a[:], [a], [osb])
                p.act(sq[c2 % 2][:], a[:], AF.Square, [a], [sq[c2 % 2]])
                p.mm(ss[:], c.ones_bf[:], sq[c2 % 2][:], start=(c2 == 0), stop=(c2 == 15),
                     reads=[c.ones_bf, sq[c2 % 2]], writes=[ss])
            if STOP <= 2:
                continue
            rstd_from_ss(p, rs, ss, D, TT)
            if STOP <= 3:
                continue
            for c2 in range(16):
                xx, xn = xt[c2 % 2], xo[c2 % 2]
                p.dma("sp", xx[:], xin[c2 * 128:(c2 + 1) * 128, t0:t0 + TT], reads=[xin], writes=[xx])
                p.tt("pool", tmp[:], osb[:, c2, :], rs[:], ALU.mult, [osb, rs], [tmp])
                p.stt(xn[:], tmp[:], cvcol(c, l, "post_g", c2), xx[:], ALU.mult, ALU.add, [tmp, xx, c.cv], [xn])
                p.dma("pool", xout[c2 * 128:(c2 + 1) * 128, t0:t0 + TT], xn[:], reads=[xn], writes=[xout])


def phase_ssd(p, c, l):
    T = c.T
    NCH = T // 128
    IDb = c.cm_bf[:, 0:128]
    ID = c.cm
    TRIU = c.cm[:, 128:256]
    SU = c.cm[:, 256:384]
    ONESF = c.cm[:, 384:512]
    ONEC = c.cm[:, 384:385]
    with p.scope():
        dtT = p.sb("dtT", [16, T], F32)
        adtT = p.sb("adtT", [16, T], F32)
        negA = p.sb("negA", [16, 1], F32)
        with p.scope():
            dtr = p.sb("dtr", [16, T], F32)
            p.dma("sp", dtr[:], c.projT[A_DT:A_DT + 16, 0:T], reads=[c.projT], writes=[dtr])
            p.act(dtr[:], dtr[:], AF.Exp, [dtr, c.cv], [dtr], bias=cvcol(c, l, "dt_bias", 0, 16))
            p.act(dtT[:], dtr[:], AF.Ln, [dtr, c.cm], [dtT], bias=c.cm[0:16, 384:385])
            p.act(negA[:], cvcol(c, l, "a_log", 0, 16), AF.Exp, [c.cv], [negA])
            p.ts("dve", negA[:], negA[:], -1.0, ALU.mult, [negA], [negA])
            p.ts("dve", adtT[:], dtT[:], negA[:, 0:1], ALU.mult, [dtT, negA], [adtT])
            xr = [p.sb("xr%d" % i, [128, T + 3], F32) for i in range(2)]
            ca = [p.sb("ca%d" % i, [128, T], F32) for i in range(2)]
            xo = [p.sb("cxo%d" % i, [128, T], BF16) for i in range(2)]
            for i in range(2):
                p.memset("pool", xr[i][:, 0:3], 0.0, writes=[xr[i]])
            for ci in range(16):
                x_, a_, o_ = xr[ci % 2], ca[ci % 2], xo[ci % 2]
                r0 = A_XBC + ci * 128
                p.dma("sp", x_[:, 3:T + 3], c.projT[r0:r0 + 128, 0:T], reads=[c.projT], writes=[x_])
                p.ts("dve", a_[:], x_[:, 3:T + 3], cvcol(c, l, "conv_w", 3 * 16 + ci), ALU.mult, [x_, c.cv], [a_],
                     s2=cvcol(c, l, "conv_b", ci), op1=ALU.add)
                for j in (2, 1, 0):
                    p.stt(a_[:], x_[:, j:T + j], cvcol(c, l, "conv_w", j * 16 + ci), a_[:], ALU.mult, ALU.add,
                          [x_, a_, c.cv], [a_])
                p.act(o_[:], a_[:], AF.Silu, [a_], [o_])
                p.dma("pool", c.xbcT[ci * 128:(ci + 1) * 128, 0:T], o_[:], reads=[o_], writes=[c.xbcT])
        xbc = [p.sb("xbc%d" % i, [128, 16, 128], BF16) for i in range(2)]
        zt = [p.sb("zt%d" % i, [128, 8, 128], F32) for i in range(2)]
        dtk = p.sb("dtk", [128, 32], F32)
        sm = p.sb("ssm_sm", [128, 64], F32)
        Xdt = p.sb("Xdt", [128, 1024], BF16)
        Xds = p.sb("Xds", [128, 1024], BF16)
        Btok = p.sb("Btok", [128, 512], BF16)
        adx = p.sb("adx", [128, 1024], F32)
        lall = p.sb("lall", [128, 16, 128], F32)
        E = p.sb("E", [128, 1024], F32)
        Lm = p.sb("Lm", [128, 2048], F32)
        CBm = p.sb("CBm", [128, 512], F32)
        M = p.sb("M", [128, 16, 128], BF16)
        hT = p.sb("hT", [128, 1024], F32)
        hTb = p.sb("hTb", [128, 1024], BF16)
        Y = p.sb("Y", [128, 1024], F32)
        t1 = p.sb("t1", [128, 1024], F32)
        sz = p.sb("sz", [128, 1024], F32)
        sqy = p.sb("sqy", [128, 1024], BF16)
        rsy = p.sb("rsy", [128, 512], F32)
        yo = [p.sb("yo%d" % i, [128, 8, 128], BF16) for i in range(2)]
        P0 = p.ps("P0", [128, 2048], BF16)
        P1 = p.ps("P1", [128, 2048], F32)
        P2 = p.ps("P2", [128, 1024], F32)
        p.memset("dve", hT[:], 0.0, writes=[hT])
        p.memset("pool", hTb[:], 0.0, writes=[hTb])
        xv = c.xbcT.h.rearrange("(k p) t -> p k t", p=128)
        zv = c.projT.h[A_Z:A_Z + 1024, :].rearrange("(k p) t -> p k t", p=128)
        yav = c.yaT.h.rearrange("(k p) t -> p k t", p=128)
        for ch in range(NCH):
            t0 = ch * 128
            xb, zz, yy = xbc[ch % 2], zt[ch % 2], yo[ch % 2]
            p.dma("sp", xb[:], xv[:, :, t0:t0 + 128], reads=[c.xbcT], writes=[xb])
            p.dma("sp", zz[:], zv[:, :, t0:t0 + 128], reads=[c.projT], writes=[zz])
            p.tr(P2[:, 0:16], dtT[:, t0:t0 + 128], ID[0:16, 0:16], [dtT, c.cm], [P2])
            p.tr(P2[:, 16:32], adtT[:, t0:t0 + 128], ID[0:16, 0:16], [adtT, c.cm], [P2])
            p.copy("act", dtk[:], P2[:, 0:32], [P2], [dtk])
            p.mm(P2[:, 64:80], SU, dtk[:, 16:32], reads=[c.cm, dtk], writes=[P2])
            p.mm(P2[:, 128:144], ONESF, dtk[:, 16:32], reads=[c.cm, dtk], writes=[P2])
            p.act(sm[:, 0:16], P2[:, 64:80], AF.Exp, [P2], [sm])
            p.act(sm[:, 16:32], P2[:, 128:144], AF.Exp, [P2], [sm])
            p.tt("dve", sm[:, 32:48], sm[:, 0:16], dtk[:, 0:16], ALU.mult, [sm, dtk], [sm])
            for j in range(12):
                p.tr(P0[:, j * 128:(j + 1) * 128], xb[:, j, :], IDb, [xb, c.cm_bf], [P0])
            px = P0[:, 0:1024].rearrange("p (h d) -> p h d", d=64)
            p.tt("dve", Xdt[:].rearrange("p (h d) -> p h d", d=64), px,
                 dtk[:, 0:16].unsqueeze(2).to_broadcast([128, 16, 64]), ALU.mult, [P0, dtk], [Xdt])
            p.tt("dve", Xds[:].rearrange("p (h d) -> p h d", d=64), px,
                 sm[:, 32:48].unsqueeze(2).to_broadcast([128, 16, 64]), ALU.mult, [P0, sm], [Xds])
            p.copy("act", Btok[:], P0[:, 1024:1536], [P0], [Btok])
            p.copy("pool", adx[:].rearrange("p (h d) -> p h d", d=64),
                   dtk[:, 16:32].unsqueeze(2).to_broadcast([128, 16, 64]), [dtk], [adx])
            p.tt("pool", lall[:], dtk[:, 16:32].unsqueeze(2).to_broadcast([128, 16, 128]),
                 SU.unsqueeze(1).to_broadcast([128, 16, 128]), ALU.mult, [dtk, c.cm], [lall])
            for h in range(16):
                p.mm(P1[:, h * 128:(h + 1) * 128], lall[:, h, :], TRIU, reads=[lall, c.cm], writes=[P1])
            p.act(Lm[:], P1[:], AF.Exp, [P1], [Lm])
            for j in range(8):
                p.mm(P2[:, j * 128:(j + 1) * 128], adx[:, j * 128:(j + 1) * 128], TRIU, reads=[adx, c.cm], writes=[P2])
            p.act(E[:], P2[:], AF.Exp, [P2], [E])
            for g in range(4):
                p.mm(P2[:, g * 128:(g + 1) * 128], xb[:, 8 + g, :], xb[:, 12 + g, :], reads=[xb], writes=[P2])
            p.tt("dve", CBm[:].rearrange("p (g l) -> p g l", l=128), P2[:, 0:512].rearrange("p (g l) -> p g l", l=128),
                 TRIU.unsqueeze(1).to_broadcast([128, 4, 128]), ALU.mult, [P2, c.cm], [CBm])
            p.tt("dve", M[:].rearrange("p (g r) l -> p g r l", r=4), Lm[:].rearrange("p (g r l) -> p g r l", r=4, l=128),
                 CBm[:].rearrange("p (g l) -> p g l", l=128).unsqueeze(2).to_broadcast([128, 4, 4, 128]), ALU.mult,
                 [Lm, CBm], [M])
            for h in range(16):
                p.mm(P1[(h % 2) * 64:(h % 2) * 64 + 64, (h // 2) * 128:(h // 2) * 128 + 128], Xdt[:, h * 64:(h + 1) * 64],
                     M[:, h, :], reads=[Xdt, M], writes=[P1])
            for j in range(8):
                p.mm(P2[:, j * 128:(j + 1) * 128], hTb[:, j * 128:(j + 1) * 128], xb[:, 12 + j // 2, :],
                     reads=[hTb, xb], writes=[P2])
            p.tt("dve", t1[:], P2[:], E[:], ALU.mult, [P2, E], [t1])
            p.tt("dve", Y[:], t1[:], P1[:, 0:1024], ALU.add, [t1, P1], [Y])
            for g in range(4):
                p.mm(P1[:, 1024 + g * 256:1024 + (g + 1) * 256], Btok[:, g * 128:(g + 1) * 128],
                     Xds[:, g * 256:(g + 1) * 256], reads=[Btok, Xds], writes=[P1])
            p.tt("pool", hT[:].rearrange("p (h d) -> p h d", d=64), hT[:].rearrange("p (h d) -> p h d", d=64),
                 sm[:, 16:32].unsqueeze(2).to_broadcast([128, 16, 64]), ALU.mult, [hT, sm], [hT])
            p.tt("dve", hT[:], hT[:], P1[:, 1024:2048], ALU.add, [hT, P1], [hT])
            p.copy("act", hTb[:], hT[:], [hT], [hTb])
            o_d, _ = CV["d_exp"]
            o_n, _ = CV["ssm_norm"]
            dcol = c.cv[:, l * NV + o_d:l * NV + o_d + 8].unsqueeze(2).to_broadcast([128, 8, 128])
            ncol = c.cv[:, l * NV + o_n:l * NV + o_n + 8].unsqueeze(2).to_broadcast([128, 8, 128])
            Y3 = Y[:].rearrange("p (j l) -> p j l", l=128)
            t13 = t1[:].rearrange("p (j l) -> p j l", l=128)
            p.tt("pool", t13, xb[:, 0:8, :], dcol, ALU.mult, [xb, c.cv], [t1])
            p.tt("pool", Y[:], Y[:], t1[:], ALU.add, [Y, t1], [Y])
            p.act(sz[:], zz[:].rearrange("p j l -> p (j l)"), AF.Silu, [zz], [sz])
            p.tt("dve", Y[:], Y[:], sz[:], ALU.mult, [Y, sz], [Y])
            p.tt("pool", sqy[:], Y[:], Y[:], ALU.mult, [Y], [sqy])
            for g in range(4):
                for i2 in range(2):
                    j = 2 * g + i2
                    p.mm(P2[:, g * 128:(g + 1) * 128], c.ones_bf[:], sqy[:, j * 128:(j + 1) * 128],
                         start=(i2 == 0), stop=(i2 == 1), reads=[c.ones_bf, sqy], writes=[P2])
            rstd_from_ss(p, rsy, P2, 256, 512)
            p.tt("dve", Y[:].rearrange("p (g i l) -> p g i l", i=2, l=128), Y[:].rearrange("p (g i l) -> p g i l", i=2, l=128),
                 rsy[:].rearrange("p (g l) -> p g l", l=128).unsqueeze(2).to_broadcast([128, 4, 2, 128]), ALU.mult,
                 [Y, rsy], [Y])
            p.tt("dve", yy[:], Y3, ncol, ALU.mult, [Y, c.cv], [yy])
            p.dma("pool", yav[:, :, t0:t0 + 128], yy[:], reads=[yy], writes=[c.yaT])


R_Q, R_IQ, R_K, R_IK = 0, 1024, 2048, 2304
MAGIC = 12582912.0
NEGBIG = -1.0e30


def phase_dsa(p, c, l):
    T = c.T
    NQ = T // 128
    ID = c.cm[:, 0:128]
    IDb = c.cm_bf[:, 0:128]
    ONESF = c.cm[:, 384:512]
    PERM = c.cm[:, 512:640]
    NEGM = c.cm[:, 640:768]
    INV = c.cm[:, 768:769]
    SGN = c.cm[:, 769:770]
    HPI = c.cm[:, 770:771]
    TWO_PI = 6.283185307179586
    C1 = 6.28125
    C2 = TWO_PI - C1
    with p.scope():
        Ct = p.sb("ropeC", [128, T], F32)
        St = p.sb("ropeS", [128, T], F32)
        with p.scope():
            pi_ = p.sb("posi", [128, T], I32)
            ang = p.sb("ang", [128, T], F32)
            kk = p.sb("angk", [128, T], F32)
            r = p.sb("angr", [128, T], F32)
            p.dma("sp", pi_[:], c.pos_d[:], reads=[c.pos_d], writes=[pi_])
            p.copy("dve", ang[:], pi_[:], [pi_], [ang])
            p.ts("dve", ang[:], ang[:], INV, ALU.mult, [ang, c.cm], [ang])
            p.ts("dve", kk[:], ang[:], 1.0 / TWO_PI, ALU.mult, [ang], [kk], s2=MAGIC, op1=ALU.add)
            p.ts("dve", kk[:], kk[:], MAGIC, ALU.subtract, [kk], [kk])
            p.stt(r[:], kk[:], -C1, ang[:], ALU.mult, ALU.add, [kk, ang], [r])
            p.stt(r[:], kk[:], -C2, r[:], ALU.mult, ALU.add, [kk, r], [r])
            p.ts("dve", r[:], r[:], 3.14159, ALU.min, [r], [r], s2=-3.14159, op1=ALU.max)
            p.act(St[:], r[:], AF.Sin, [r], [St])
            p.ts("dve", St[:], St[:], SGN, ALU.mult, [St, c.cm], [St])
            p.ts("dve", kk[:], r[:], -1.0, ALU.mult, [r], [kk])
            p.tt("dve", kk[:], kk[:], r[:], ALU.max, [kk, r], [kk])
            p.act(Ct[:], kk[:], AF.Sin, [kk, c.cm], [Ct], bias=HPI, scale=-1.0)
        with p.scope():
            xs = [p.sb("rx%d" % i, [128, T], F32) for i in range(2)]
            ob = [p.sb("rob%d" % i, [128, T], BF16) for i in range(2)]
            tmp = p.sb("rtmp", [128, 512], F32)
            o1 = p.sb("ro1", [128, 512], F32)
            sqk = p.sb("rsq", [128, 512], F32)
            rsk = p.sb("rrs", [128, 512], F32)
            PS = [p.ps("rps%d" % i, [128, 512], F32) for i in range(4)]
            tiles = [(C_Q + i * 128, R_Q + i * 128, 128) for i in range(8)] + \
                    [(C_IQ + i * 128, R_IQ + i * 128, 128) for i in range(8)] + \
                    [(C_K + i * 128, R_K + i * 128, 128) for i in range(2)] + [(C_IK, R_IK, 64)]
            for ti, (src, dst, nr) in enumerate(tiles):
                x_, o_ = xs[ti % 2], ob[ti % 2]
                p.dma("sp", x_[0:nr, :], c.projT[src:src + nr, 0:T], reads=[c.projT], writes=[x_])
                for j in range(T // 512):
                    sl = slice(j * 512, (j + 1) * 512)
                    if nr == 64:
                        ps = PS[j % 2]
                        p.mm(ps[0:64, :], ONESF[0:64, 0:64], x_[0:64, sl], reads=[c.cm, x_], writes=[ps])
                        p.stt(x_[0:64, sl], ps[0:64, :], -1.0 / 64, x_[0:64, sl], ALU.mult, ALU.add, [ps, x_], [x_])
                        p.tt("pool", sqk[0:64, :], x_[0:64, sl], x_[0:64, sl], ALU.mult, [x_], [sqk])
                        p.mm(ps[0:64, :], ONESF[0:64, 0:64], sqk[0:64, :], reads=[c.cm, sqk], writes=[ps])
                        p.ts("dve", rsk[0:64, :], ps[0:64, :], 1.0 / 64, ALU.mult, [ps], [rsk], s2=EPS, op1=ALU.add)
                        p.act(rsk[0:64, :], rsk[0:64, :], AF.Sqrt, [rsk], [rsk])
                        p.op("dve", lambda e, a=rsk[0:64, :]: e.reciprocal(out=a, in_=a), [rsk], [rsk])
                        p.tt("dve", x_[0:64, sl], x_[0:64, sl], rsk[0:64, :], ALU.mult, [x_, rsk], [x_])
                        p.ts("dve", x_[0:64, sl], x_[0:64, sl], cvcol(c, l, "ikn_w", 0, 64), ALU.mult, [x_, c.cv], [x_],
                             s2=cvcol(c, l, "ikn_b", 0, 64), op1=ALU.add)
                    ps = PS[2 + j % 2]
                    p.mm(ps[0:nr, :], PERM[0:nr, 0:nr], x_[0:nr, sl], reads=[c.cm, x_], writes=[ps])
                    p.tt("dve", tmp[0:nr, :], ps[0:nr, :], St[0:nr, sl], ALU.mult, [ps, St], [tmp])
                    p.tt("pool", o1[0:nr, :], x_[0:nr, sl], Ct[0:nr, sl], ALU.mult, [x_, Ct], [o1])
                    p.tt("dve", o_[0:nr, sl], o1[0:nr, :], tmp[0:nr, :], ALU.add, [o1, tmp], [o_])
                p.dma("pool", c.dsaT[dst:dst + nr, 0:T], o_[0:nr, :], reads=[o_], writes=[c.dsaT])
    with p.scope():
        ik2 = p.sb("ik2", [128, T], BF16)
        k2 = [p.sb("k2_%d" % g, [128, T], BF16) for g in range(4)]
        vtok = p.sb("vtok", [128, NQ, 256], BF16)
        P0 = p.ps("dP0", [128, 2048], BF16)
        PL = [p.ps("dPL%d" % i, [128, 512], F32) for i in range(2)]
        PO = p.ps("dPO", [128, 1024], F32)
        PR = p.ps("dPR", [128, 1024], F32)
        for b in (0, 64):
            p.dma("sp", ik2[b:b + 64, :], c.dsaT[R_IK:R_IK + 64, 0:T], reads=[c.dsaT], writes=[ik2])
            for g in range(4):
                p.dma("sp", k2[g][b:b + 64, :], c.dsaT[R_K + g * 64:R_K + g * 64 + 64, 0:T], reads=[c.dsaT], writes=[k2[g]])
        with p.scope():
            vf = p.sb("vf", [128, 2, T], F32)
            vb = p.sb("vb", [128, 2, T], BF16)
            p.dma("sp", vf[:], c.projT.h[C_V:C_V + 256, :].rearrange("(k p) t -> p k t", p=128), reads=[c.projT], writes=[vf])
            p.copy("pool", vb[:], vf[:], [vf], [vb])
            for b0 in range(0, NQ, 8):
                for blk in range(b0, b0 + 8):
                    for t2 in range(2):
                        i = (blk - b0) * 2 + t2
                        p.tr(P0[:, i * 128:(i + 1) * 128], vb[:, t2, blk * 128:(blk + 1) * 128], IDb, [vb, c.cm_bf], [P0])
                p.copy("dve", vtok[:, b0:b0 + 8, :].rearrange("p b c -> p (b c)"), P0[:, 0:2048], [P0], [vtok])
        S = p.sb("dS", [128, T], F32)
        Wk = p.sb("dWk", [128, T], F32)
        msk = p.sb("dmsk", [128, T], BF16)
        mskT = p.sb("dmskT", [128, NQ, 128], BF16)
        qsb = [p.sb("dq%d" % i, [128, 8, 128], BF16) for i in range(2)]
        iqs = [p.sb("diq%d" % i, [128, 8, 128], BF16) for i in range(2)]
        gts = [p.sb("dg%d" % i, [128, 8, 128], F32) for i in range(2)]
        iwc = [p.sb("diw%d" % i, [16, 128], F32) for i in range(2)]
        rl = [p.sb("drl%d" % i, [128, 512], F32) for i in range(2)]
        PT = [p.sb("dPT%d" % i, [128, 512], BF16) for i in range(2)]
        PTm = [p.sb("dPTm%d" % i, [128, 512], BF16) for i in range(2)]
        iwk = p.sb("diwk", [128, 16], F32)
        m8 = p.sb("dm8", [128, 8], F32)
        thr = p.sb("dthr", [128, 1], F32)
        rec = p.sb("drec", [128, 1024], F32)
        ov = p.sb("dov", [128, 1024], F32)
        sg = p.sb("dsg", [128, 1024], F32)
        yo = [p.sb("dyo%d" % i, [128, 8, 128], BF16) for i in range(2)]
        qv = c.dsaT.h[R_Q:R_Q + 1024, :].rearrange("(k p) t -> p k t", p=128)
        iqv = c.dsaT.h[R_IQ:R_IQ + 1024, :].rearrange("(k p) t -> p k t", p=128)
        gv = c.projT.h[C_G:C_G + 1024, :].rearrange("(k p) t -> p k t", p=128)
        ycv = c.ycT.h.rearrange("(k p) t -> p k t", p=128)
        li = 0
        for qi in range(NQ):
            q0 = qi * 128
            n = q0 + 128
            qs_, iq_, g_, iw_, y_ = qsb[qi % 2], iqs[qi % 2], gts[qi % 2], iwc[qi % 2], yo[qi % 2]
            p.dma("sp", qs_[:], qv[:, :, q0:n], reads=[c.dsaT], writes=[qs_])
            p.dma("sp", iq_[:], iqv[:, :, q0:n], reads=[c.dsaT], writes=[iq_])
            p.dma("sp", g_[:], gv[:, :, q0:n], reads=[c.projT], writes=[g_])
            p.dma("sp", iw_[:], c.projT[C_IK + 64:C_IK + 80, q0:n], reads=[c.projT], writes=[iw_])
            p.tr(PO[:, 0:16], iw_[:], ID[0:16, 0:16], [iw_, c.cm], [PO])
            p.copy("act", iwk[:], PO[:, 0:16], [PO], [iwk])
            for s0 in range(0, n, 512):
                w = min(512, n - s0)
                for h in range(16):
                    b = (h % 2) * 64
                    pl = PL[li % 2]
                    r_ = rl[li % 2]
                    li += 1
                    p.mm(pl[:, 0:w], iq_[b:b + 64, h // 2, :], ik2[b:b + 64, s0:s0 + w], reads=[iq_, ik2], writes=[pl])
                    p.act(r_[:, 0:w], pl[:, 0:w], AF.Relu, [pl], [r_])
                    if h == 0:
                        p.ts("dve", S[:, s0:s0 + w], r_[:, 0:w], iwk[:, 0:1], ALU.mult, [r_, iwk], [S])
                    else:
                        p.stt(S[:, s0:s0 + w], r_[:, 0:w], iwk[:, h:h + 1], S[:, s0:s0 + w], ALU.mult, ALU.add, [r_, iwk, S], [S])
            p.tt("dve", S[:, q0:n], S[:, q0:n], NEGM, ALU.add, [S, c.cm], [S])
            if qi >= 2:
                p.copy("dve", Wk[:, 0:n], S[:, 0:n], [S], [Wk])
                for rnd in range(32):
                    p.op("dve", lambda e, a=Wk[:, 0:n]: e.max(out=m8[:], in_=a), [Wk], [m8])
                    if rnd < 31:
                        p.op("dve", lambda e, a=Wk[:, 0:n]: e.match_replace(out=a, in_to_replace=m8[:], in_values=a,
                                                                           imm_value=NEGBIG), [Wk, m8], [Wk])
                p.copy("dve", thr[:], m8[:, 7:8], [m8], [thr])
            else:
                p.memset("dve", thr[:], -1.0e29, writes=[thr])
            p.ts("dve", msk[:, 0:n], S[:, 0:n], thr[:, 0:1], ALU.is_ge, [S, thr], [msk])
            for b0 in range(0, qi + 1, 16):
                nb = min(16, qi + 1 - b0)
                for j in range(nb):
                    p.tr(P0[:, j * 128:(j + 1) * 128], msk[:, (b0 + j) * 128:(b0 + j + 1) * 128], IDb, [msk, c.cm_bf], [P0])
                p.copy("act", mskT[:, b0:b0 + nb, :].rearrange("p b c -> p (b c)"), P0[:, 0:nb * 128], [P0], [mskT])
            for h in range(16):
                g = h // 4
                b = (h % 2) * 64
                oc = slice((h // 2) * 128, (h // 2) * 128 + 128)
                for s0b in range(0, qi + 1, 4):
                    nb = min(4, qi + 1 - s0b)
                    pl = PL[li % 2]
                    pt, ptm = PT[li % 2], PTm[li % 2]
                    li += 1
                    for j in range(nb):
                        sb = s0b + j
                        p.mm(pl[:, j * 128:(j + 1) * 128], k2[g][b:b + 64, sb * 128:(sb + 1) * 128], qs_[b:b + 64, h // 2, :],
                             reads=[k2[g], qs_], writes=[pl])
                    p.act(pt[:, 0:nb * 128], pl[:, 0:nb * 128], AF.Exp, [pl], [pt], scale=0.125)
                    p.tt("pool" if li % 2 else "dve", ptm[:, 0:nb * 128], pt[:, 0:nb * 128],
                         mskT[:, s0b:s0b + nb, :].rearrange("p b c -> p (b c)"), ALU.mult, [pt, mskT], [ptm])
                    for j in range(nb):
                        sb = s0b + j
                        p.mm(PO[b:b + 64, oc], vtok[:, sb, g * 64:(g + 1) * 64], ptm[:, j * 128:(j + 1) * 128],
                             start=(sb == 0), stop=(sb == qi), reads=[vtok, ptm], writes=[PO])
                        p.mm(PR[b:b + 64, oc], c.ones_bf[:, 0:64], ptm[:, j * 128:(j + 1) * 128],
                             start=(sb == 0), stop=(sb == qi), reads=[c.ones_bf, ptm], writes=[PR])
            p.op("dve", lambda e: e.reciprocal(out=rec[:], in_=PR[:]), [PR], [rec])
            p.tt("dve", ov[:], PO[:], rec[:], ALU.mult, [PO, rec], [ov])
            p.act(sg[:], g_[:].rearrange("p j l -> p (j l)"), AF.Silu, [g_], [sg])
            p.tt("pool", y_[:].rearrange("p j l -> p (j l)"), ov[:], sg[:], ALU.mult, [ov, sg], [y_])
            p.dma("pool", ycv[:, :, q0:n], y_[:], reads=[y_], writes=[c.ycT])


def phase_rwkv(p, c, l):
    T = c.T
    NB = T // 128
    ID = c.cm[:, 0:128]
    ONEC = c.cm[:, 384:385]
    NHALF = c.cm[:, 771:772]
    GNEPS = c.cm[:, 772:773]
    BLK = c.cm[:, 896:1024]
    rows5 = c.rowsD.h.rearrange("t (f c) -> t f c", f=5)
    with p.scope():
        negw0 = p.sb("negw0", [128, 8], F32)
        o_w0, _ = CV["w0"]
        p.ts("dve", negw0[:], c.cv[:, l * NV + o_w0:l * NV + o_w0 + 8], -1.0, ALU.mult, [c.cv], [negw0])
        w2s = p.sb("w2s", [128, 1024], F32)
        w2b = p.sb("w2b", [128, 1024], BF16)
        p.dma("sp", w2s[0:64, :], c.rw2.h[l], reads=[c.rw2], writes=[w2s])
        p.dma("sp", w2s[64:128, :], c.ra2.h[l], reads=[c.ra2], writes=[w2s])
        p.copy("dve", w2b[:], w2s[:], [w2s], [w2b])
        xr = [p.sb("bxr%d" % i, [128, T + 1], F32) for i in range(3)]
        mx = {n: p.sb("bm_" + n, [128, T], F32) for n in ("wa", "r", "k", "v")}
        wab = p.sb("wab", [128, T], BF16)
        dec = p.sb("bdec", [128, T], F32)
        av = p.sb("bav", [128, T], F32)
        kkn = p.sb("bkkn", [128, T], F32)
        t2 = p.sb("bt2", [128, 512], F32)
        t3 = p.sb("bt3", [128, 512], F32)
        tro = [p.sb("btro%d" % i, [128, 5, 128], F32) for i in range(2)]
        PS = [p.ps("bps%d" % i, [128, 512], F32) for i in range(4)]
        PTt = [p.ps("bpt%d" % i, [128, 5 * 128], F32) for i in range(2)]
        for i in range(3):
            p.memset("pool", xr[i][:, 0:1], 0.0, writes=[xr[i]])
        xi = [0]

        def mix(src, mu_name, mu_j, dst):
            x_ = xr[xi[0] % 3]
            xi[0] += 1
            p.dma("sp", x_[:, 1:T + 1], c.projT[src:src + 128, 0:T], reads=[c.projT], writes=[x_])
            p.tt("pool", dst[:], x_[:, 0:T], x_[:, 1:T + 1], ALU.subtract, [x_], [dst])
            p.stt(dst[:], dst[:], cvcol(c, l, mu_name, mu_j), x_[:, 1:T + 1], ALU.mult, ALU.add, [dst, x_, c.cv], [dst])

        mix(B_WA, "mu_wa", 0, mx["wa"])
        p.act(mx["wa"][0:64, :], mx["wa"][0:64, :], AF.Tanh, [mx["wa"]], [mx["wa"]])
        p.copy("dve", wab[:], mx["wa"][:], [mx["wa"]], [wab])
        for j in range(8):
            mix(B_R + j * 128, "mu_r", j, mx["r"])
            mix(B_K + j * 128, "mu_k", j, mx["k"])
            mix(B_V + j * 128, "mu_v", j, mx["v"])
            p.dma("pool", c.vD[j * 128:(j + 1) * 128, 0:T], mx["v"][:], reads=[mx["v"]], writes=[c.vD])
            for q in range(T // 512):
                sl = slice(q * 512, (q + 1) * 512)
                pw, pa, pk, pr = PS
                p.mm(pw[:], w2b[0:64, j * 128:(j + 1) * 128], wab[0:64, sl], reads=[w2b, wab], writes=[pw])
                p.mm(pa[:], w2b[64:128, j * 128:(j + 1) * 128], wab[64:128, sl], reads=[w2b, wab], writes=[pa])
                p.act(t2[:], pw[:], AF.Exp, [pw, negw0], [t2], bias=negw0[:, j:j + 1], scale=-1.0)
                p.act(t2[:], t2[:], AF.Ln, [t2, c.cm], [t2], bias=ONEC)
                p.act(t2[:], t2[:], AF.Exp, [t2, c.cm], [t2], bias=NHALF, scale=-1.0)
                p.act(dec[:, sl], t2[:], AF.Exp, [t2], [dec], scale=-1.0)
                p.act(av[:, sl], pa[:], AF.Sigmoid, [pa, c.cv], [av], bias=cvcol(c, l, "a0", j))
                p.ts("dve", kkn[:, sl], mx["k"][:, sl], cvcol(c, l, "k_k", j), ALU.mult, [mx["k"], c.cv], [kkn])
                p.tt("pool", t3[:], kkn[:, sl], kkn[:, sl], ALU.mult, [kkn], [t3])
                p.mm(pk[:], BLK, t3[:], reads=[c.cm, t3], writes=[pk])
                p.act(t3[:], pk[:], AF.Sqrt, [pk], [t3])
                p.ts("dve", t3[:], t3[:], 1e-12, ALU.max, [t3], [t3])
                p.op("dve", lambda e, a=t3[:]: e.reciprocal(out=a, in_=a), [t3], [t3])
                p.tt("dve", kkn[:, sl], kkn[:, sl], t3[:], ALU.mult, [kkn, t3], [kkn])
                p.ts("dve", t3[:], av[:, sl], -1.0, ALU.add, [av], [t3], s2=cvcol(c, l, "k_a", j), op1=ALU.mult)
                p.ts("dve", t3[:], t3[:], 1.0, ALU.add, [t3], [t3])
                p.tt("dve", mx["k"][:, sl], mx["k"][:, sl], t3[:], ALU.mult, [mx["k"], t3], [mx["k"]])
                p.tt("pool", av[:, sl], av[:, sl], kkn[:, sl], ALU.mult, [av, kkn], [av])
                p.tt("pool", t3[:], mx["r"][:, sl], mx["k"][:, sl], ALU.mult, [mx["r"], mx["k"]], [t3])
                p.ts("dve", t3[:], t3[:], cvcol(c, l, "r_k", j), ALU.mult, [t3, c.cv], [t3])
                p.mm(pr[:], BLK, t3[:], reads=[c.cm, t3], writes=[pr])
                p.copy("act", t2[:], pr[:], [pr], [t2])
                p.dma("pool", c.rkD[j * 128:(j + 1) * 128, sl], t2[:], reads=[t2], writes=[c.rkD])
            srcs = [kkn, dec, av, mx["k"], mx["r"]]
            for bi in range(NB):
                pt = PTt[bi % 2]
                to = tro[bi % 2]
                for f in range(5):
                    p.tr(pt[:, f * 128:(f + 1) * 128], srcs[f][:, bi * 128:(bi + 1) * 128], ID, [srcs[f], c.cm], [pt])
                p.copy("act" if bi % 2 else "dve", to[:].rearrange("p f c -> p (f c)"), pt[:], [pt], [to])
                p.dma("pool", rows5[bi * 128:(bi + 1) * 128, :, j * 128:(j + 1) * 128], to[:], reads=[to], writes=[c.rowsD])
    TB = 128
    with p.scope():
        S = p.sb("rS", [64, 1024], F32)
        tmp = p.sb("rtmp", [64, 1024], F32)
        tmp2 = p.sb("rtmp2", [64, 1024], F32)
        sa = p.sb("rsa", [64, 16], F32)
        Vc = [p.sb("rVc%d" % i, [64, 16, TB], F32) for i in range(2)]
        Yc = [p.sb("rYc%d" % i, [64, 16, TB], F32) for i in range(2)]
        NBUF = 4
        rb = [p.sb("rrow%d" % i, [64, 5, 1024], F32) for i in range(NBUF)]
        p.memset("dve", S[:], 0.0, writes=[S])
        vv = c.vD.h.rearrange("(h v) t -> v h t", v=64)
        yv = c.yD.h.rearrange("(h v) t -> v h t", v=64)
        S3 = S[:].rearrange("p (h k) -> p h k", k=64)
        T3 = tmp[:].rearrange("p (h k) -> p h k", k=64)
        U3 = tmp2[:].rearrange("p (h k) -> p h k", k=64)
        for t in range(T):
            blk, tl = divmod(t, TB)
            vc, yc = Vc[blk % 2], Yc[blk % 2]
            if tl == 0:
                p.dma("sp", vc[:], vv[:, :, blk * TB:(blk + 1) * TB], reads=[c.vD], writes=[vc])
            r_ = rb[t % NBUF]
            p.dma("sp", r_[:].rearrange("p f c -> p (f c)"), c.rowsD[t:t + 1, :].to_broadcast([64, 5 * 1024]),
                  reads=[c.rowsD], writes=[r_])
            KK, W, B_, K_, R_ = [r_[:, f, :] for f in range(5)]
            p.tt("dve", tmp[:], S[:], KK, ALU.mult, [S, r_], [tmp])
            p.op("dve", lambda e, o=sa[:], i=T3: e.tensor_reduce(out=o, in_=i, op=ALU.add, axis=AX.X, negate=True), [tmp], [sa])
            p.tt("pool", S[:], S[:], W, ALU.mult, [S, r_], [S])
            p.tt("pool", U3, vc[:, :, tl:tl + 1].to_broadcast([64, 16, 64]), K_.rearrange("p (h k) -> p h k", k=64), ALU.mult,
                 [vc, r_], [tmp2])
            p.tt("pool", S[:], S[:], tmp2[:], ALU.add, [S, tmp2], [S])
            p.tt("dve", T3, sa[:].unsqueeze(2).to_broadcast([64, 16, 64]), B_.rearrange("p (h k) -> p h k", k=64), ALU.mult,
                 [sa, r_], [tmp])
            p.tt("dve", S[:], S[:], tmp[:], ALU.add, [S, tmp], [S])
            p.tt("dve", tmp[:], S[:], R_, ALU.mult, [S, r_], [tmp])
            p.op("dve", lambda e, o=yc[:, :, tl], i=T3: e.tensor_reduce(out=o, in_=i, op=ALU.add, axis=AX.X), [tmp], [yc])
            if tl == TB - 1:
                p.dma("pool", yv[:, :, blk * TB:(blk + 1) * TB], yc[:], reads=[yc], writes=[c.yD])
    with p.scope():
        yt = [p.sb("cy%d" % i, [128, T], F32) for i in range(2)]
        vt = [p.sb("cv_%d" % i, [128, T], F32) for i in range(2)]
        rkt = [p.sb("crk%d" % i, [128, T], F32) for i in range(2)]
        gt = [p.sb("cg%d" % i, [128, T], F32) for i in range(2)]
        ob = [p.sb("cob%d" % i, [128, T], BF16) for i in range(2)]
        t2 = p.sb("ct2", [128, 512], F32)
        t3 = p.sb("ct3", [128, 512], F32)
        PS = [p.ps("cps%d" % i, [128, 512], F32) for i in range(4)]
        for j in range(8):
            y_, v_, rk_, g_, o_ = yt[j % 2], vt[j % 2], rkt[j % 2], gt[j % 2], ob[j % 2]
            rs_ = slice(j * 128, (j + 1) * 128)
            p.dma("sp", y_[:], c.yD[rs_, 0:T], reads=[c.yD], writes=[y_])
            p.dma("sp", v_[:], c.vD[rs_, 0:T], reads=[c.vD], writes=[v_])
            p.dma("sp", rk_[:], c.rkD[rs_, 0:T], reads=[c.rkD], writes=[rk_])
            p.dma("sp", g_[:], c.projT[B_G + j * 128:B_G + (j + 1) * 128, 0:T], reads=[c.projT], writes=[g_])
            for q in range(T // 512):
                sl = slice(q * 512, (q + 1) * 512)
                pm, pv = PS[(2 * q) % 4], PS[(2 * q + 1) % 4]
                p.mm(pm[:], BLK, y_[:, sl], reads=[c.cm, y_], writes=[pm])
                p.stt(y_[:, sl], pm[:], -1.0 / 64, y_[:, sl], ALU.mult, ALU.add, [pm, y_], [y_])
                p.tt("pool", t2[:], y_[:, sl], y_[:, sl], ALU.mult, [y_], [t2])
                p.mm(pv[:], BLK, t2[:], reads=[c.cm, t2], writes=[pv])
                p.ts("dve", t3[:], pv[:], 1.0 / 64, ALU.mult, [pv], [t3], s2=64e-5, op1=ALU.add)
                p.act(t3[:], t3[:], AF.Sqrt, [t3], [t3])
                p.op("dve", lambda e, a=t3[:]: e.reciprocal(out=a, in_=a), [t3], [t3])
                p.tt("dve", y_[:, sl], y_[:, sl], t3[:], ALU.mult, [y_, t3], [y_])
                p.ts("dve", y_[:, sl], y_[:, sl], cvcol(c, l, "ln_w", j), ALU.mult, [y_, c.cv], [y_],
                     s2=cvcol(c, l, "ln_b", j), op1=ALU.add)
                p.tt("pool", t2[:], rk_[:, sl], v_[:, sl], ALU.mult, [rk_, v_], [t2])
                p.tt("dve", y_[:, sl], y_[:, sl], t2[:], ALU.add, [y_, t2], [y_])
                p.act(t2[:], g_[:, sl], AF.Silu, [g_], [t2])
                p.tt("dve", o_[:, sl], y_[:, sl], t2[:], ALU.mult, [y_, t2], [o_])
            p.dma("pool", c.ybT[rs_, 0:T], o_[:], reads=[o_], writes=[c.ybT])


def phase_rwkv2(p, c, l):
    T = c.T
    G = 512
    NG = T // G
    IDb = c.cm_bf[0:64, 0:64]
    ONEC = c.cm[0:64, 384:385]
    NHALF = c.cm[0:64, 771:772]
    BLK = c.cm[0:64, 896:960]
    MASK4 = c.cm[0:64, 1024:1152]
    MLOW = c.cm[0:64, 1152:1216]
    I2 = c.cm[0:64, 1216:1280]

    def hcol(name, hd):
        o, w = CV[name]
        return c.cv[(hd % 2) * 64:(hd % 2) * 64 + 64, l * NV + o + hd // 2:l * NV + o + hd // 2 + 1]

    with p.scope():
        names = ["mu_r", "mu_k", "mu_v", "w0", "a0", "k_k", "k_a", "r_k"]
        hv = p.sb("hv", [64, len(names), 16], F32)
        for ni, nm in enumerate(names):
            o, w = CV[nm]
            src = c.cvec_d.h[:, l * NV + o:l * NV + o + 8]
            p.dma("sp", hv[:, ni, 0:8], src[0:64, :], reads=[c.cvec_d], writes=[hv])
            p.dma("sp", hv[:, ni, 8:16], src[64:128, :], reads=[c.cvec_d], writes=[hv])
        hidx = lambda hd: (hd % 2) * 8 + hd // 2
        hvc = lambda nm, hd: hv[:, names.index(nm), hidx(hd):hidx(hd) + 1]
        negw0 = p.sb("negw0", [64, 16], F32)
        p.ts("dve", negw0[:], hv[:, names.index("w0"), :], -1.0, ALU.mult, [hv], [negw0])
        muw = p.sb("muw", [64, 4], F32)
        o, w = CV["mu_wa"]
        p.dma("sp", muw[:, 0:2], c.cvec_d.h[0:64, l * NV + o:l * NV + o + 2], reads=[c.cvec_d], writes=[muw])
        p.dma("sp", muw[:, 2:4], c.cvec_d.h[64:128, l * NV + o:l * NV + o + 2], reads=[c.cvec_d], writes=[muw])
        w2s = p.sb("w2s", [64, 2048], F32)
        w2b = p.sb("w2b", [64, 2048], BF16)
        p.dma("sp", w2s[:, 0:1024], c.rw2.h[l], reads=[c.rw2], writes=[w2s])
        p.dma("sp", w2s[:, 1024:2048], c.ra2.h[l], reads=[c.ra2], writes=[w2s])
        p.copy("dve", w2b[:], w2s[:], [w2s], [w2b])
        mreset = p.sb("mreset", [64, G], F32)
        p.memset("dve", mreset[:], 1.0, writes=[mreset])
        p.memset("dve", mreset[:].rearrange("p (c j) -> p c j", j=64)[:, :, 0:1], 0.0, writes=[mreset])
        Sf = [p.sb("Sf%d" % j, [64, 64], F32) for j in range(16)]
        Sb = [p.sb("Sb%d" % j, [64, 64], BF16) for j in range(16)]
        for j in range(16):
            p.memset("pool", Sf[j][:], 0.0, writes=[Sf[j]])
            p.memset("pool", Sb[j][:], 0.0, writes=[Sb[j]])
        xwd = [p.sb("cxwd%d" % i, [64, G + 1], F32) for i in range(2)]
        xad = [p.sb("cxad%d" % i, [64, G + 1], F32) for i in range(2)]
        xin = {n: [p.sb("cx%s%d" % (n, i), [64, G + 1], F32) for i in range(2)] for n in "rkv"}
        mwd = p.sb("cmwd", [64, G], F32)
        mad = p.sb("cmad", [64, G], F32)
        wdb = p.sb("cwdb", [64, G], BF16)
        adb = p.sb("cadb", [64, G], BF16)
        mr = p.sb("cmr", [64, G], F32)
        mk = p.sb("cmk", [64, G], F32)
        mv = [p.sb("cmv%d" % i, [64, G], F32) for i in range(2)]
        lwn = p.sb("clwn", [64, G], F32)
        cs = p.sb("ccs", [64, G], F32)
        e1 = p.sb("ce1", [64, G], F32)
        e2 = p.sb("ce2", [64, G], F32)
        e3 = p.sb("ce3", [64, G], F32)
        e4 = p.sb("ce4", [64, G], F32)
        av = p.sb("cav", [64, G], F32)
        kkn = p.sb("ckkn", [64, G], F32)
        t3 = p.sb("ct3_", [64, G], F32)
        rko = [p.sb("crko%d" % i, [64, G], F32) for i in range(2)]
        gCs = [p.sb("cgC%d" % i, [64, 8], F32) for i in range(2)]
        ARs = [p.sb("cAR%d" % i, [64, 8, 128], BF16) for i in range(2)]
        BK = p.sb("cBK", [64, 8, 128], BF16)
        Bg = p.sb("cBg", [64, G], BF16)
        Kg = p.sb("cKg", [64, G], BF16)
        Vb = p.sb("cVb", [64, G], BF16)
        ABs = [p.sb("cAB%d" % i, [64, 8, 128], BF16) for i in range(2)]
        AKs = [p.sb("cAK%d" % i, [64, 8, 128], BF16) for i in range(2)]
        Pm = [p.sb("cP%d" % i, [64, 8, 64], BF16) for i in range(2)]
        Qm = [p.sb("cQ%d" % i, [64, 8, 64], BF16) for i in range(2)]
        TTm = [p.sb("cTT%d" % i, [64, 8, 64], BF16) for i in range(4)]
        BgTs = [p.sb("cBgT%d" % i, [64, 8, 64], BF16) for i in range(2)]
        KgTs = [p.sb("cKgT%d" % i, [64, 8, 64], BF16) for i in range(2)]
        Vts = [p.sb("cVt%d" % i, [64, 8, 64], BF16) for i in range(2)]
        TAts = [p.sb("cTAt%d" % i, [64, 8, 64], BF16) for i in range(2)]
        UVss = [p.sb("cUV%d" % i, [64, 8, 64], F32) for i in range(2)]
        AtT = p.sb("cAtT", [64, 8, 64], BF16)
        W2sb = p.sb("cW2sb", [64, 8, 64], BF16)
        Usb = p.sb("cUsb", [64, 64], BF16)
        Ysb = [p.sb("cYsb%d" % i, [64, G], F32) for i in range(2)]
        PB = p.ps("cPB", [64, 1024], F32)
        PA = p.ps("cPA", [64, 512], F32)
        PY = p.ps("cPY", [64, 512], F32)
        PI = [p.ps("cPI%d" % i, [64, 512], F32) for i in range(3)]
        PT = p.ps("cPT", [64, 1024], BF16)
        v3 = lambda a: a.rearrange("p (c j) -> p c j", j=64)
        f2 = lambda a: a.rearrange("p c m -> p (c m)")

        def load(x_, row, t0):
            if t0 == 0:
                p.memset("pool", x_[:, 0:1], 0.0, writes=[x_])
                p.dma("sp", x_[:, 1:G + 1], c.projT[row:row + 64, 0:G], reads=[c.projT], writes=[x_])
            else:
                p.dma("sp", x_[:, 0:G + 1], c.projT[row:row + 64, t0 - 1:t0 + G], reads=[c.projT], writes=[x_])

        def mix(x_, mu_ap, dst):
            p.tt("pool", dst[:], x_[:, 0:G], x_[:, 1:G + 1], ALU.subtract, [x_], [dst])
            p.stt(dst[:], dst[:], mu_ap, x_[:, 1:G + 1], ALU.mult, ALU.add, [dst, x_, hv, muw], [dst])

        def prep(it, g, hd):
            t0 = g * G
            st = it % 2
            AR, AB, AK, gC = ARs[st], ABs[st], AKs[st], gCs[st]
            BgT, KgT, Vt, TAt, UVs = BgTs[st], KgTs[st], Vts[st], TAts[st], UVss[st]
            if hd == 0:
                load(xwd[g % 2], B_WA, t0)
                load(xad[g % 2], B_WA + 64, t0)
                mix(xwd[g % 2], muw[:, 0:1], mwd)
                mix(xad[g % 2], muw[:, 2:3], mad)
                p.act(wdb[:], mwd[:], AF.Tanh, [mwd], [wdb])
                p.copy("dve", adb[:], mad[:], [mad], [adb])
                yield
            xr_, xk_, xv_ = [xin[n][it % 2] for n in "rkv"]
            mv_ = mv[it % 2]
            rko_ = rko[it % 2]
            load(xr_, B_R + hd * 64, t0)
            load(xk_, B_K + hd * 64, t0)
            load(xv_, B_V + hd * 64, t0)
            mix(xr_, hvc("mu_r", hd), mr)
            yield
            mix(xk_, hvc("mu_k", hd), mk)
            mix(xv_, hvc("mu_v", hd), mv_)
            yield
            p.dma("pool", c.vD[hd * 64:(hd + 1) * 64, t0:t0 + G], mv_[:], reads=[mv_], writes=[c.vD])
            p.copy("act", Vb[:], mv_[:], [mv_], [Vb])
            pw, pa, pk = PI
            p.mm(pw[:], w2b[:, hd * 64:(hd + 1) * 64], wdb[:], reads=[w2b, wdb], writes=[pw])
            p.mm(pa[:], w2b[:, 1024 + hd * 64:1024 + (hd + 1) * 64], adb[:], reads=[w2b, adb], writes=[pa])
            yield
            p.act(e1[:], pw[:], AF.Exp, [pw, negw0], [e1], bias=negw0[:, hidx(hd):hidx(hd) + 1], scale=-1.0)
            p.act(e1[:], e1[:], AF.Ln, [e1, c.cm], [e1], bias=ONEC)
            yield
            p.act(lwn[:], e1[:], AF.Exp, [e1, c.cm], [lwn], bias=NHALF, scale=-1.0)
            p.act(av[:], pa[:], AF.Sigmoid, [pa, hv], [av], bias=hvc("a0", hd))
            yield
            p.op("dve", lambda e: e.tensor_tensor_scan(out=cs[:], data0=mreset[:], data1=lwn[:], initial=0.0,
                                                       op0=ALU.mult, op1=ALU.add), [mreset, lwn], [cs])
            cs3 = v3(cs[:])
            p.act(e1[:], cs[:], AF.Exp, [cs], [e1])
            yield
            p.act(e2[:], cs[:], AF.Exp, [cs], [e2], scale=-1.0)
            p.tt("pool", e3[:], cs[:], lwn[:], ALU.subtract, [cs, lwn], [e3])
            yield
            p.act(e3[:], e3[:], AF.Exp, [e3], [e3], scale=-1.0)
            p.tt("pool", v3(e4[:]), cs3, cs3[:, :, 63:64].to_broadcast([64, 8, 64]), ALU.subtract, [cs], [e4])
            yield
            p.act(e4[:], e4[:], AF.Exp, [e4], [e4])
            p.act(gC[:], cs3[:, :, 63], AF.Exp, [cs], [gC], scale=-1.0)
            yield
            p.ts("dve", kkn[:], mk[:], hvc("k_k", hd), ALU.mult, [mk, hv], [kkn])
            p.tt("pool", t3[:], kkn[:], kkn[:], ALU.mult, [kkn], [t3])
            p.mm(pk[:], BLK, t3[:], reads=[c.cm, t3], writes=[pk])
            yield
            p.act(t3[:], pk[:], AF.Sqrt, [pk], [t3])
            p.ts("dve", t3[:], t3[:], 1e-12, ALU.max, [t3], [t3])
            yield
            p.op("dve", lambda e, a=t3[:]: e.reciprocal(out=a, in_=a), [t3], [t3])
            p.tt("dve", kkn[:], kkn[:], t3[:], ALU.mult, [kkn, t3], [kkn])
            yield
            p.ts("dve", t3[:], av[:], -1.0, ALU.add, [av, hv], [t3], s2=hvc("k_a", hd), op1=ALU.mult)
            p.ts("dve", t3[:], t3[:], 1.0, ALU.add, [t3], [t3])
            yield
            p.tt("dve", mk[:], mk[:], t3[:], ALU.mult, [mk, t3], [mk])
            p.tt("pool", av[:], av[:], kkn[:], ALU.mult, [av, kkn], [av])
            yield
            p.tt("pool", t3[:], mr[:], mk[:], ALU.mult, [mr, mk], [t3])
            p.ts("dve", t3[:], t3[:], hvc("r_k", hd), ALU.mult, [t3, hv], [t3])
            p.mm(pk[:], BLK, t3[:], reads=[c.cm, t3], writes=[pk])
            yield
            p.copy("act", rko_[:], pk[:], [pk], [rko_])
            p.dma("pool", c.rkD[hd * 64:(hd + 1) * 64, t0:t0 + G], rko_[:], reads=[rko_], writes=[c.rkD])
            p.stt(AR[:, :, 0:64], v3(kkn[:]), -1.0, v3(e3[:]), ALU.mult, ALU.mult, [kkn, e3], [AR])
            yield
            p.tt("pool", AR[:, :, 64:128], v3(mr[:]), v3(e2[:]), ALU.mult, [mr, e2], [AR])
            p.tt("dve", BK[:, :, 0:64], v3(av[:]), v3(e1[:]), ALU.mult, [av, e1], [BK])
            yield
            p.tt("pool", BK[:, :, 64:128], v3(mk[:]), v3(e1[:]), ALU.mult, [mk, e1], [BK])
            p.tt("dve", Bg[:], av[:], e4[:], ALU.mult, [av, e4], [Bg])
            p.tt("pool", Kg[:], mk[:], e4[:], ALU.mult, [mk, e4], [Kg])
            yield
            for ci in range(8):
                p.mm(PB[:, ci * 128:(ci + 1) * 128], BK[:, ci, 0:64], AR[:, ci, :], reads=[BK, AR], writes=[PB])
                if ci % 4 == 3:
                    yield
            p.tt("dve", AB[:], PB[:].rearrange("p (c m) -> p c m", m=128), MASK4.unsqueeze(1).to_broadcast([64, 8, 128]),
                 ALU.mult, [PB, c.cm], [AB])
            yield
            for ci in range(8):
                p.mm(PB[:, ci * 128:(ci + 1) * 128], BK[:, ci, 64:128], AR[:, ci, :], reads=[BK, AR], writes=[PB])
                if ci % 4 == 3:
                    yield
            p.tt("dve", AK[:], PB[:].rearrange("p (c m) -> p c m", m=128), MASK4.unsqueeze(1).to_broadcast([64, 8, 128]),
                 ALU.mult, [PB, c.cm], [AK])
            yield
            for ci in range(8):
                p.mm(PI[0][:, ci * 64:(ci + 1) * 64], AR[:, ci, 0:64], BK[:, ci, 0:64], reads=[AR, BK], writes=[PI[0]])
            yield
            Pc, Qc, Tc = Pm[0], Qm[0], TTm[2 * st]
            p.tt("dve", Pc[:], PI[0][:].rearrange("p (c m) -> p c m", m=64), MLOW.unsqueeze(1).to_broadcast([64, 8, 64]),
                 ALU.mult, [PI[0], c.cm], [Pc])
            p.copy("pool", Qc[:], AB[:, :, 0:64], [AB], [Qc])
            p.tt("pool", Tc[:], AB[:, :, 0:64], I2.unsqueeze(1).to_broadcast([64, 8, 64]), ALU.add, [AB, c.cm], [Tc])
            yield
            cur = 0
            for lev in range(5):
                Pn, Qn, Tn = Pm[1 - cur], Qm[1 - cur], TTm[2 * st + 1 - cur]
                Pc, Qc, Tc = Pm[cur], Qm[cur], TTm[2 * st + cur]
                for ci in range(8):
                    cc = slice(ci * 64, (ci + 1) * 64)
                    p.mm(PI[0][:, cc], Qc[:, ci, :], Pc[:, ci, :], reads=[Qc, Pc], writes=[PI[0]])
                    if ci % 4 == 3:
                        yield
                if lev < 4:
                    for ci in range(8):
                        cc = slice(ci * 64, (ci + 1) * 64)
                        p.mm(PI[1][:, cc], Pc[:, ci, :], Qc[:, ci, :], reads=[Qc, Pc], writes=[PI[1]])
                        if ci % 4 == 3:
                            yield
                p.copy("act", f2(Pn[:]), PI[0][:], [PI[0]], [Pn])
                if lev < 4:
                    p.copy("dve", f2(Qn[:]), PI[1][:], [PI[1]], [Qn])
                yield
                for ci in range(8):
                    cc = slice(ci * 64, (ci + 1) * 64)
                    p.mm(PI[2][:, cc], Pn[:, ci, :], Tc[:, ci, :], reads=[Pn, Tc], writes=[PI[2]])
                    if ci % 4 == 3:
                        yield
                p.tt("dve", f2(Tn[:]), f2(Tc[:]), PI[2][:], ALU.add, [Tc, PI[2]], [Tn])
                yield
                cur = 1 - cur
            TT = TTm[2 * st + cur]
            assert cur == 1
            for (src_, dst_, w0_) in ((Bg, BgT, None), (Kg, KgT, None), (Vb, Vt, None), (AR, AtT, 0)):
                for ci in range(8):
                    in_ = src_[:, ci * 64:(ci + 1) * 64] if w0_ is None else src_[:, ci, 0:64]
                    p.tr(PT[:, ci * 64:(ci + 1) * 64], in_, IDb, [src_, c.cm_bf], [PT])
                    if ci % 4 == 3:
                        yield
                p.copy("act", f2(dst_[:]), PT[:, 0:512], [PT], [dst_])
                yield
            for ci in range(8):
                p.mm(PI[0][:, ci * 64:(ci + 1) * 64], AtT[:, ci, :], TT[:, ci, :], reads=[AtT, TT], writes=[PI[0]])
                if ci % 4 == 3:
                    yield
            p.copy("act", f2(TAt[:]), PI[0][:], [PI[0]], [TAt])
            for ci in range(8):
                p.mm(PI[1][:, ci * 64:(ci + 1) * 64], AK[:, ci, 0:64], Vt[:, ci, :], reads=[AK, Vt], writes=[PI[1]])
                if ci % 4 == 3:
                    yield
            p.copy("dve", f2(W2sb[:]), PI[1][:], [PI[1]], [W2sb])
            yield
            for ci in range(8):
                p.mm(PI[2][:, ci * 64:(ci + 1) * 64], TT[:, ci, :], W2sb[:, ci, :], reads=[TT, W2sb], writes=[PI[2]])
                if ci % 4 == 3:
                    yield
            p.copy("act", f2(UVs[:]), PI[2][:], [PI[2]], [UVs])
            yield

        def seq(it, g, hd):
            t0 = g * G
            st = it % 2
            AR, AB, AK, gC = ARs[st], ABs[st], AKs[st], gCs[st]
            BgT, KgT, Vt, TAt, UVs = BgTs[st], KgTs[st], Vts[st], TAts[st], UVss[st]
            S_f, S_b = Sf[hd], Sb[hd]
            ys_ = Ysb[it % 2]
            for ci in range(8):
                yc = slice(ci * 64, (ci + 1) * 64)
                p.mm(PA[:, 0:64], TAt[:, ci, :], S_b[:], reads=[TAt, S_b], writes=[PA])
                p.mm(PY[:, yc], S_b[:], AR[:, ci, 64:128], start=True, stop=False, reads=[S_b, AR], writes=[PY])
                p.mm(PY[:, yc], Vt[:, ci, :], AK[:, ci, 64:128], start=False, stop=False, reads=[Vt, AK], writes=[PY])
                p.mm(PA[:, 128:192], KgT[:, ci, :], Vt[:, ci, :], start=True, stop=False, reads=[KgT, Vt], writes=[PA])
                yield
                p.tt("dve", Usb[:], PA[:, 0:64], UVs[:, ci, :], ALU.add, [PA, UVs], [Usb])
                yield
                p.mm(PA[:, 128:192], BgT[:, ci, :], Usb[:], start=False, stop=True, reads=[BgT, Usb], writes=[PA])
                p.mm(PY[:, yc], Usb[:], AB[:, ci, 64:128], start=False, stop=True, reads=[Usb, AB], writes=[PY])
                yield
                p.stt(S_f[:], S_f[:], gC[:, ci:ci + 1], PA[:, 128:192], ALU.mult, ALU.add, [S_f, gC, PA], [S_f])
                yield
                p.copy("act", S_b[:], S_f[:], [S_f], [S_b])
                yield
            p.copy("dve", ys_[:], PY[:, 0:512], [PY], [ys_])
            p.dma("pool", c.yD[hd * 64:(hd + 1) * 64, t0:t0 + G], ys_[:], reads=[ys_], writes=[c.yD])
            yield

        work = [(g, hd) for g in range(NG) for hd in range(16)]
        for _ in prep(0, *work[0]):
            pass
        for it, (g, hd) in enumerate(work):
            gs = seq(it, g, hd)
            gp = prep(it + 1, *work[it + 1]) if it + 1 < len(work) else iter(())
            alive_s, alive_p = True, True
            while alive_s or alive_p:
                if alive_s:
                    try:
                        next(gs)
                    except StopIteration:
                        alive_s = False
                for _ in range(2):
                    if alive_p:
                        try:
                            next(gp)
                        except StopIteration:
                            alive_p = False
    _rwkv_b3(p, c, l)


def _rwkv_b3(p, c, l):
    T = c.T
    BLK = c.cm[:, 896:1024]
    with p.scope():
        yt = [p.sb("cy%d" % i, [128, T], F32) for i in range(2)]
        vt = [p.sb("cv_%d" % i, [128, T], F32) for i in range(2)]
        rkt = [p.sb("crk%d" % i, [128, T], F32) for i in range(2)]
        gt = [p.sb("cg%d" % i, [128, T], F32) for i in range(2)]
        ob = [p.sb("cob%d" % i, [128, T], BF16) for i in range(2)]
        t2 = p.sb("ct2", [128, 512], F32)
        t3 = p.sb("ct3", [128, 512], F32)
        PS = [p.ps("cps%d" % i, [128, 512], F32) for i in range(4)]
        for j in range(8):
            y_, v_, rk_, g_, o_ = yt[j % 2], vt[j % 2], rkt[j % 2], gt[j % 2], ob[j % 2]
            rs_ = slice(j * 128, (j + 1) * 128)
            p.dma("sp", y_[:], c.yD[rs_, 0:T], reads=[c.yD], writes=[y_])
            p.dma("sp", v_[:], c.vD[rs_, 0:T], reads=[c.vD], writes=[v_])
            p.dma("sp", rk_[:], c.rkD[rs_, 0:T], reads=[c.rkD], writes=[rk_])
            p.dma("sp", g_[:], c.projT[B_G + j * 128:B_G + (j + 1) * 128, 0:T], reads=[c.projT], writes=[g_])
            for q in range(T // 512):
                sl = slice(q * 512, (q + 1) * 512)
                pm, pv = PS[(2 * q) % 4], PS[(2 * q + 1) % 4]
                p.mm(pm[:], BLK, y_[:, sl], reads=[c.cm, y_], writes=[pm])
                p.stt(y_[:, sl], pm[:], -1.0 / 64, y_[:, sl], ALU.mult, ALU.add, [pm, y_], [y_])
                p.tt("pool", t2[:], y_[:, sl], y_[:, sl], ALU.mult, [y_], [t2])
                p.mm(pv[:], BLK, t2[:], reads=[c.cm, t2], writes=[pv])
                p.ts("dve", t3[:], pv[:], 1.0 / 64, ALU.mult, [pv], [t3], s2=64e-5, op1=ALU.add)
                p.act(t3[:], t3[:], AF.Sqrt, [t3], [t3])
                p.op("dve", lambda e, a=t3[:]: e.reciprocal(out=a, in_=a), [t3], [t3])
                p.tt("dve", y_[:, sl], y_[:, sl], t3[:], ALU.mult, [y_, t3], [y_])
                p.ts("dve", y_[:, sl], y_[:, sl], cvcol(c, l, "ln_w", j), ALU.mult, [y_, c.cv], [y_],
                     s2=cvcol(c, l, "ln_b", j), op1=ALU.add)
                p.tt("pool", t2[:], rk_[:, sl], v_[:, sl], ALU.mult, [rk_, v_], [t2])
                p.tt("dve", y_[:, sl], y_[:, sl], t2[:], ALU.add, [y_, t2], [y_])
                p.act(t2[:], g_[:, sl], AF.Silu, [g_], [t2])
                p.tt("dve", o_[:, sl], y_[:, sl], t2[:], ALU.mult, [y_, t2], [o_])
            p.dma("pool", c.ybT[rs_, 0:T], o_[:], reads=[o_], writes=[c.ybT])


def pad_w_in(w):
    o = np.zeros((2048, NCP), np.float32)
    o[:, 0:3088] = w[:, 0:3088]
    b = 3088
    o[:, B_R:B_R + 1024] = w[:, b:b + 1024]
    o[:, B_WA:B_WA + 64] = w[:, b + 1024:b + 1088]
    o[:, B_K:B_K + 1024] = w[:, b + 1088:b + 2112]
    o[:, B_V:B_V + 1024] = w[:, b + 2112:b + 3136]
    o[:, B_WA + 64:B_WA + 128] = w[:, b + 3136:b + 3200]
    o[:, B_G:B_G + 1024] = w[:, b + 3200:b + 4224]
    cc = 7312
    o[:, C_Q:C_Q + 1024] = w[:, cc:cc + 1024]
    o[:, C_K:C_K + 256] = w[:, cc + 1024:cc + 1280]
    o[:, C_V:C_V + 256] = w[:, cc + 1280:cc + 1536]
    o[:, C_G:C_G + 1024] = w[:, cc + 1536:cc + 2560]
    o[:, C_IQ:C_IQ + 1024] = w[:, cc + 2560:cc + 3584]
    o[:, C_IK:C_IK + 80] = w[:, cc + 3584:cc + 3664]
    o[:, G0:G0 + 6144] = w[:, 10976:17120]
    return o

def col(v):
    v = np.asarray(v, np.float32).reshape(-1)
    n = (v.size + 127) // 128
    o = np.zeros((n * 128,), np.float32)
    o[:v.size] = v
    return o.reshape(n, 128).T

def pack_cvec(inp, L):
    cv = np.zeros((128, L * NV), np.float32)
    def put(l, name, arr):
        o, w = CV[name]
        assert arr.shape == (128, w), (name, arr.shape, w)
        cv[:, l * NV + o:l * NV + o + w] = arr
    for l in range(L):
        put(l, "pre_g", col(inp["pre_norm"][l]))
        put(l, "post_g", col(inp["post_norm"][l]))
        put(l, "bg", col(inp["b_gate"][l]))
        put(l, "conv_w", np.concatenate([col(inp["ssm_conv_w"][l][j]) for j in range(4)], axis=1))
        put(l, "conv_b", col(inp["ssm_conv_b"][l]))
        put(l, "dt_bias", col(inp["ssm_dt_bias"][l]))
        put(l, "a_log", col(inp["ssm_a_log"][l]))
        put(l, "d_exp", col(np.repeat(inp["ssm_d"][l], 64)))
        put(l, "ssm_norm", col(inp["ssm_norm"][l]))
        mu = inp["rwkv_mu"][l]
        put(l, "mu_r", col(mu[0:1024]))
        put(l, "mu_k", col(mu[1088:2112]))
        put(l, "mu_v", col(mu[2112:3136]))
        put(l, "mu_wa", col(np.concatenate([mu[1024:1088], mu[3136:3200]])))
        for n, k in [("w0", "rwkv_w0"), ("a0", "rwkv_a0"), ("k_k", "rwkv_k_k"), ("k_a", "rwkv_k_a"),
                     ("ln_w", "rwkv_ln_w"), ("ln_b", "rwkv_ln_b"), ("r_k", "rwkv_r_k")]:
            put(l, n, col(inp[k][l].reshape(-1)))
        put(l, "ikn_w", col(inp["idx_k_norm_w"][l]))
        put(l, "ikn_b", col(inp["idx_k_norm_b"][l]))
    return cv

def const_mats():
    k = np.arange(128)
    ident = np.eye(128, dtype=np.float32)
    triu = (k[:, None] <= k[None, :]).astype(np.float32)
    su = (k[:, None] > k[None, :]).astype(np.float32)
    ones = np.ones((128, 128), np.float32)
    z = np.zeros((128, 128), np.float32)
    perm = np.zeros((128, 128), np.float32)
    for m in range(128):
        perm[(m // 64) * 64 + ((m % 64) + 32) % 64, m] = 1.0
    negm = np.where(k[None, :] > k[:, None], -1e30, 0.0).astype(np.float32)
    misc = np.zeros((128, 128), np.float32)
    inv = 10000.0 ** (-(np.arange(32, dtype=np.float32) * 2.0 / 64))
    misc[:, 0] = inv[k % 32]
    misc[:, 1] = np.where((k % 64) < 32, -1.0, 1.0)
    misc[:, 2] = np.pi / 2
    misc[:, 3] = -0.5
    misc[:, 4] = 64e-5
    blk = (k[:, None] // 64 == k[None, :] // 64).astype(np.float32)
    s_ = (k % 64)[:, None]
    t_ = (k % 64)[None, :]
    mask4 = np.where(k[None, :] < 64, s_ < t_, s_ <= t_).astype(np.float32)
    m9 = np.zeros((128, 128), np.float32)
    k64 = np.arange(64)
    m9[:, 0:64] = ((k % 64)[:, None] > k64[None, :]).astype(np.float32)
    m9[:, 64:128] = ((k % 64)[:, None] == k64[None, :]).astype(np.float32)
    return np.concatenate([ident, triu, su, ones, perm, negm, misc, blk, mask4, m9], axis=1)


_NC_CACHE = {}


def build_program(T, L):
    nc = bass.Bass("TRN2", target_bir_lowering=False)
    es = ExitStack()
    with es:
        p = Prog(nc, es, nsem=100)
        c = Ctx()
        c.T = T
        c.xT0 = p.dram("xT0", [2048, T], F32, kind="ExternalInput")
        c.w_in_l = [p.dram("w_in%d" % l, [2048, NCP], F32, kind="ExternalInput") for l in range(L)]
        c.w_ba = p.dram("w_ba", [L, 1024, 2048], F32, kind="ExternalInput")
        c.w_bb = p.dram("w_bb", [L, 1024, 2048], F32, kind="ExternalInput")
        c.w_bc = p.dram("w_bc", [L, 1024, 2048], F32, kind="ExternalInput")
        c.w_out = p.dram("w_out", [L, 2048, 2048], F32, kind="ExternalInput")
        c.cvec_d = p.dram("cvec", [128, L * NV], F32, kind="ExternalInput")
        c.cmat_d = p.dram("cmat", [128, 1280], F32, kind="ExternalInput")
        c.pos_d = p.dram("pos", [128, T], I32, kind="ExternalInput")
        c.rw2 = p.dram("rw2", [L, 64, 1024], F32, kind="ExternalInput")
        c.ra2 = p.dram("ra2", [L, 64, 1024], F32, kind="ExternalInput")
        c.xo = p.dram("xo", [2048, T], F32, kind="ExternalOutput")
        xA = p.dram("xA", [2048, T], F32)
        xB = p.dram("xB", [2048, T], F32)
        c.projT = p.dram("projT", [G0, T], F32)
        c.projG = p.dram("projG", [NCP - G0, T], F32)
        c.xbcT = p.dram("xbcT", [2048, T], BF16)
        c.dsaT = p.dram("dsaT", [2432, T], BF16)
        c.yaT = p.dram("yaT", [1024, T], BF16)
        c.ybT = p.dram("ybT", [1024, T], BF16)
        c.ycT = p.dram("ycT", [1024, T], BF16)
        c.vD = p.dram("vD", [1024, T], F32)
        c.rkD = p.dram("rkD", [1024, T], F32)
        c.yD = p.dram("yD", [1024, T], F32)
        alloc_consts(p, c, L)
        p.barrier()
        xs = [c.xT0] + [xA if (l % 2 == 0) else xB for l in range(L - 1)] + [c.xo]
        for l in range(L):
            phase_p1(p, c, l, xs[l])
            phase_ssd(p, c, l)
            phase_rwkv2(p, c, l)
            phase_dsa(p, c, l)
            phase_p3(p, c, l, xs[l], xs[l + 1])
        p.barrier()
    return nc


def kernel(**inputs):
    inp = {k: np.asarray(v) for k, v in inputs.items()}
    x = inp["x"].astype(np.float32, copy=False)
    Bsz, T, _ = x.shape
    L = inp["w_in"].shape[0]
    key = (T, L)
    if key not in _NC_CACHE:
        _NC_CACHE[key] = build_program(T, L)
    nc = _NC_CACHE[key]
    cmat = const_mats()
    cvec = pack_cvec(inp, L)
    w_in_p = [pad_w_in(inp["w_in"][l]) for l in range(L)]
    maps = []
    for b in range(Bsz):
        m = {"xT0": np.ascontiguousarray(x[b].T), "w_ba": inp["w_branch_a"], "w_bb": inp["w_branch_b"],
             "w_bc": inp["w_branch_c"], "w_out": inp["w_out"], "cvec": cvec, "cmat": cmat,
             "pos": np.ascontiguousarray(np.broadcast_to(inp["positions"][b][None, :], (128, T))).astype(np.int32),
             "rw2": inp["rwkv_w2"], "ra2": inp["rwkv_a2"]}
        for l in range(L):
            m["w_in%d" % l] = w_in_p[l]
        maps.append(m)
    res = run_bass_kernel_spmd(nc, maps, core_ids=list(range(Bsz)))
    return np.stack([np.asarray(res.results[b]["xo"]).T for b in range(Bsz)]).astype(np.float32)
```

```python
import numpy as np
from contextlib import ExitStack
import concourse.bass as bass
import concourse.mybir as mybir
from concourse.bass_utils import run_bass_kernel_spmd

F32 = mybir.dt.float32
BF16 = mybir.dt.bfloat16
I32 = mybir.dt.int32
AF = mybir.ActivationFunctionType
ALU = mybir.AluOpType
AX = mybir.AxisListType

EPOCH = 24000
ENGS = ("pe", "dve", "act", "pool", "sp")


class Tl:
    def __init__(self, h, name):
        self.h = h
        self.name = name
        self.dram = False
        self.psum = False
        self.acc = {}
        self.w = None
        self.r = {}

    def __getitem__(self, k):
        return self.h[k]


class Prog:
    def __init__(self, nc, es, nsem=120, same_engine_sync=True):
        self.nc = nc
        self.es = es
        self.q = {e: [] for e in ENGS}
        self.n = {e: 0 for e in ENGS}
        self.waited = {e: {} for e in ENGS}
        self.lane_n = {}
        self.sem_pool = [es.enter_context(nc.semaphore("s%d" % i)) for i in range(nsem)]
        self.sem_map = {}
        self.sem_next = 0
        self.lane_sem = {}
        self.retired = []
        self.same = same_engine_sync
        self.ntile = 0

    def sb(self, name, shape, dt):
        self.ntile += 1
        h = self.es.enter_context(self.nc.sbuf_tensor("%s_%d" % (name, self.ntile), list(shape), dt))
        return Tl(h, name)

    def ps(self, name, shape, dt=F32):
        self.ntile += 1
        h = self.es.enter_context(self.nc.psum_tensor("%s_%d" % (name, self.ntile), list(shape), dt))
        t = Tl(h, name)
        t.psum = True
        return t

    def dram(self, name, shape, dt, kind="Internal"):
        h = self.nc.dram_tensor(name, list(shape), dt, kind=kind)
        t = Tl(h, name)
        t.dram = True
        return t

    def _fresh(self):
        if self.sem_next < len(self.sem_pool):
            self.sem_next += 1
            return self.sem_pool[self.sem_next - 1], 0
        self.retired.sort(key=lambda x: x[0])
        r, sem = self.retired.pop(0)
        return sem, r

    def _sem(self, src, val):
        if src in ENGS:
            ep = (val - 1) // EPOCH if val > 0 else 0
            key = (src, ep)
            if key not in self.sem_map:
                self.sem_map[key] = self._fresh()
            sem, r0 = self.sem_map[key]
            return sem, val - ep * EPOCH + r0
        if src not in self.lane_sem:
            sem, r = self._fresh()
            self.lane_sem[src] = (sem, r - (val - 16))
        sem, delta = self.lane_sem[src]
        assert 0 < val + delta < 30000, (src, val, delta)
        return sem, val + delta

    def retire_lanes(self):
        for lane, (sem, delta) in self.lane_sem.items():
            self.retired.append((self.lane_n[lane] + delta, sem))
        self.lane_sem = {}

    def _collect(self, eng, reads, writes):
        deps = {}

        def add(d):
            if d is None:
                return
            s, v = d
            if deps.get(s, 0) < v:
                deps[s] = v

        reads = [t for t in reads if not t.dram]
        writes = [t for t in writes if not t.dram]
        for t in list(reads) + list(writes):
            if t.psum:
                for s_, v_ in t.acc.items():
                    if s_ != eng:
                        add((s_, v_))
        for t in reads:
            add(t.w)
        for t in writes:
            add(t.w)
            for s, v in t.r.items():
                add((s, v))
        waits = []
        for s, v in deps.items():
            if s == eng:
                if eng == "pe" or not self.same:
                    continue
            if self.waited[eng].get(s, 0) >= v:
                continue
            self.waited[eng][s] = v
            waits.append((s, v))
        return waits

    def op(self, eng, fn, reads=(), writes=()):
        waits = self._collect(eng, reads, writes)
        self.n[eng] += 1
        me = (eng, self.n[eng])
        self._emit_now(eng, waits, fn, me)
        reads = [t for t in reads if not t.dram]
        writes = [t for t in writes if not t.dram]
        for t in list(reads) + list(writes):
            if t.psum:
                t.acc[eng] = me[1]
        for t in reads:
            if t.r.get(eng, 0) < me[1]:
                t.r[eng] = me[1]
        for t in writes:
            t.w = me
            t.r = {}

    def dma(self, qeng, out_ap, in_ap, reads=(), writes=(), lane=None):
        reads = [t for t in reads if not t.dram]
        writes = [t for t in writes if not t.dram]
        if lane is None:
            lane = "L_" + (writes[0].name if writes else reads[0].name)
        waits = self._collect(qeng, reads, writes)
        k = self.lane_n.get(lane, 0) + 16
        self.lane_n[lane] = k
        me = (lane, k)
        self._emit_now(qeng, waits, lambda e: e.dma_start(out=out_ap, in_=in_ap), me)
        for t in reads:
            if t.r.get(lane, 0) < k:
                t.r[lane] = k
        for t in writes:
            t.w = me
            t.r = {}

    def barrier(self):
        for e in ENGS:
            waits = []
            for s in list(ENGS) + list(self.lane_n):
                v = self.n[s] if s in ENGS else self.lane_n[s]
                if v == 0 or self.waited[e].get(s, 0) >= v:
                    continue
                self.waited[e][s] = v
                waits.append((s, v))
            self._emit_now(e, waits, None, None)

    def scope(self):
        prog = self

        class _S:
            def __enter__(s):
                s.old = prog.es
                s.st = ExitStack()
                prog.es = s.st
                return s

            def __exit__(s, *a):
                prog.barrier()
                prog.retire_lanes()
                s.st.close()
                prog.es = s.old
                return False
        return _S()

    def finish(self, out_tiles, eng="sp"):
        waits = self._collect(eng, out_tiles, ())
        self._emit_now(eng, waits, None, None)

    def _emit_now(self, eng, waits, fn, me):
        e = {"pe": self.nc.tensor, "dve": self.nc.vector, "act": self.nc.scalar, "pool": self.nc.gpsimd, "sp": self.nc.sync}[eng]
        for s, v in waits:
            sem, sv = self._sem(s, v)
            e.wait_ge(sem, sv)
        if fn is None:
            return
        ins = fn(e)
        src, val = me
        sem, sv = self._sem(src, val)
        ins.then_inc(sem, 1 if src in ENGS else 16)

    def _replay(self, eng, e):
        for waits, fn, me in self.q[eng]:
            for s, v in waits:
                sem, sv = self._sem(s, v)
                e.wait_ge(sem, sv)
            if fn is None:
                continue
            ins = fn(e)
            src, val = me
            sem, sv = self._sem(src, val)
            if src in ENGS:
                ins.then_inc(sem, 1)
            else:
                ins.then_inc(sem, 16)

    def emit(self):
        return
        nc = self.nc
        with nc.Block() as block:
            @block.sync
            def _(e):
                self._replay("sp", e)

            @block.tensor
            def _(e):
                self._replay("pe", e)

            @block.vector
            def _(e):
                self._replay("dve", e)

            @block.scalar
            def _(e):
                self._replay("act", e)

            @block.gpsimd
            def _(e):
                self._replay("pool", e)


def _mm(self, out, lhsT, rhs, start=True, stop=True, reads=(), writes=()):
    self.op("pe", lambda e: e.matmul(out, lhsT, rhs, start=start, stop=stop), reads, writes)

def _tr(self, out, in_, ident, reads=(), writes=()):
    self.op("pe", lambda e: e.transpose(out, in_, ident), reads, writes)

def _act(self, out, in_, func, reads=(), writes=(), bias=None, scale=1.0):
    if bias is None:
        self.op("act", lambda e: e.activation(out=out, in_=in_, func=func, scale=scale), reads, writes)
    else:
        self.op("act", lambda e: e.activation(out=out, in_=in_, func=func, bias=bias, scale=scale), reads, writes)

def _tt(self, eng, out, in0, in1, op, reads=(), writes=()):
    self.op(eng, lambda e: e.tensor_tensor(out=out, in0=in0, in1=in1, op=op), reads, writes)

def _ts(self, eng, out, in0, s1, op0, reads=(), writes=(), s2=None, op1=None):
    if op1 is None:
        self.op(eng, lambda e: e.tensor_scalar(out=out, in0=in0, scalar1=s1, scalar2=None, op0=op0), reads, writes)
    else:
        self.op(eng, lambda e: e.tensor_scalar(out=out, in0=in0, scalar1=s1, scalar2=s2, op0=op0, op1=op1), reads, writes)

def _stt(self, out, in0, scalar, in1, op0, op1, reads=(), writes=()):
    self.op("dve", lambda e: e.scalar_tensor_tensor(out=out, in0=in0, scalar=scalar, in1=in1, op0=op0, op1=op1), reads, writes)

def _copy(self, eng, out, in_, reads=(), writes=()):
    if eng == "act":
        self.op("act", lambda e: e.copy(out=out, in_=in_), reads, writes)
    else:
        self.op(eng, lambda e: e.tensor_copy(out=out, in_=in_), reads, writes)

def _memset(self, eng, ap, val, writes=()):
    self.op(eng, lambda e: e.memset(ap, val), (), writes)

Prog.mm = _mm
Prog.tr = _tr
Prog.act = _act
Prog.tt = _tt
Prog.ts = _ts
Prog.stt = _stt
Prog.copy = _copy
Prog.memset = _memset


D = 2048
NCP = 17280
A_Z, A_XBC, A_DT = 0, 1024, 3072
B0 = 3200
B_R, B_K, B_V, B_G, B_WA = B0, B0 + 1024, B0 + 2048, B0 + 3072, B0 + 4096
C0 = B0 + 4224
C_Q, C_K, C_V, C_G, C_IQ, C_IK = C0, C0 + 1024, C0 + 1280, C0 + 1536, C0 + 2560, C0 + 3584
G0 = C0 + 3712
assert G0 + 6144 == NCP
EPS = 1e-6

CV = {}
_o = 0
for _n, _w in [("pre_g", 16), ("post_g", 16), ("bg", 48), ("conv_w", 64), ("conv_b", 16), ("dt_bias", 1),
               ("a_log", 1), ("d_exp", 8), ("ssm_norm", 8), ("mu_r", 8), ("mu_k", 8), ("mu_v", 8), ("mu_wa", 1),
               ("w0", 8), ("a0", 8), ("k_k", 8), ("k_a", 8), ("ln_w", 8), ("ln_b", 8), ("r_k", 8),
               ("ikn_w", 1), ("ikn_b", 1)]:
    CV[_n] = (_o, _w)
    _o += _w
NV = _o


class Ctx:
    pass


def cvcol(c, l, name, j=0, rows=128):
    o, w = CV[name]
    return c.cv[0:rows, l * NV + o + j: l * NV + o + j + 1]


def alloc_consts(p, c, L):
    c.cv = p.sb("cv", [128, L * NV], F32)
    p.dma("sp", c.cv[:], c.cvec_d[:], reads=[c.cvec_d], writes=[c.cv])
    c.ones_bf = p.sb("ones_bf", [128, 128], BF16)
    p.memset("dve", c.ones_bf[:], 1.0, writes=[c.ones_bf])
    c.cm = p.sb("cm", [128, 11 * 128], F32)
    p.dma("sp", c.cm[:], c.cmat_d[:], reads=[c.cmat_d], writes=[c.cm])
    c.cm_bf = p.sb("cm_bf", [128, 11 * 128], BF16)
    p.copy("dve", c.cm_bf[:], c.cm[:], [c.cm], [c.cm_bf])


def rstd_from_ss(p, rs, ss, n, width):
    p.ts("dve", rs[:, 0:width], ss[:, 0:width], 1.0 / n, ALU.mult, [ss], [rs], s2=EPS, op1=ALU.add)
    p.act(rs[:, 0:width], rs[:, 0:width], AF.Sqrt, [rs], [rs])
    p.op("dve", lambda e: e.reciprocal(out=rs[:, 0:width], in_=rs[:, 0:width]), [rs], [rs])


def phase_p1(p, c, l, xin):
    T = c.T
    TT = min(T, 2048)
    NJ = TT // 512
    with p.scope():
        xg = p.sb("xg", [128, 16, TT], BF16)
        xst = [p.sb("xst%d" % i, [128, TT], F32) for i in range(2)]
        sq = [p.sb("sq%d" % i, [128, TT], BF16) for i in range(2)]
        rs = p.sb("rs", [128, TT], F32)
        wst = [p.sb("wst%d" % i, [128, 16, 256], F32) for i in range(2)]
        wbf = [p.sb("wbf%d" % i, [128, 16, 256], BF16) for i in range(2)]
        osb = [p.sb("osb%d" % i, [128, TT], F32) for i in range(2)]
        ss = p.ps("ss", [128, TT], F32)
        acc = [p.ps("acc%d" % i, [128, 512], F32) for i in range(8 - NJ)]
        for tt in range(T // TT):
            t0 = tt * TT
            for k in range(16):
                xs = xst[k % 2]
                p.dma("sp", xs[:], xin[k * 128:(k + 1) * 128, t0:t0 + TT], reads=[xin], writes=[xs])
                p.ts("dve", xg[:, k, :], xs[:], cvcol(c, l, "pre_g", k), ALU.mult, [xs, c.cv], [xg])
                p.tt("pool", sq[k % 2][:], xs[:], xs[:], ALU.mult, [xs], [sq[k % 2]])
                for j in range(NJ):
                    p.mm(ss[:, j * 512:(j + 1) * 512], c.ones_bf[:], sq[k % 2][:, j * 512:(j + 1) * 512],
                         start=(k == 0), stop=(k == 15), reads=[c.ones_bf, sq[k % 2]], writes=[ss])
            rstd_from_ss(p, rs, ss, D, TT)
            ng = (NCP + 255) // 256
            ai = 0
            oi = 0
            wv = c.w_in_l[l].h.rearrange("(k p) c -> p k c", p=128)
            for g in range(ng):
                c0 = g * 256
                cw = min(256, NCP - c0)
                ws, wb = wst[g % 2], wbf[g % 2]
                p.dma("sp", ws[:, :, 0:cw], wv[:, :, c0:c0 + cw], reads=[c.w_in_l[l]], writes=[ws])
                p.copy("pool" if g % 2 == 0 else "act", wb[:, :, 0:cw], ws[:, :, 0:cw], [ws], [wb])
                for ct in range(cw // 128):
                    ob = osb[oi % 2]
                    oi += 1
                    for j in range(NJ):
                        a = acc[ai % len(acc)]
                        ai += 1
                        for k in range(16):
                            p.mm(a[:], wb[:, k, ct * 128:(ct + 1) * 128], xg[:, k, j * 512:(j + 1) * 512],
                                 start=(k == 0), stop=(k == 15), reads=[wb, xg], writes=[a])
                        p.tt("dve", ob[:, j * 512:(j + 1) * 512], a[:], rs[:, j * 512:(j + 1) * 512], ALU.mult,
                             [a, rs], [ob])
                    r0 = c0 + ct * 128
                    if r0 < G0:
                        p.dma("pool", c.projT[r0:r0 + 128, t0:t0 + TT], ob[:], reads=[ob], writes=[c.projT])
                    else:
                        p.dma("pool", c.projG[r0 - G0:r0 - G0 + 128, t0:t0 + TT], ob[:], reads=[ob], writes=[c.projG])


def phase_p3(p, c, l, xin, xout):
    T = c.T
    TT = 512
    wbr = [c.w_ba, c.w_bb, c.w_bc]
    ybr = [c.yaT, c.ybT, c.ycT]
    with p.scope():
        ysb = [p.sb("ysb%d" % i, [128, 8, TT], BF16) for i in range(3)]
        mg = p.sb("mg", [128, 16, TT], BF16)
        osb = p.sb("o3", [128, 16, TT], F32)
        wst = [p.sb("w3st%d" % i, [128, 8, 128], F32) for i in range(2)]
        wb = [p.sb("w3bf%d" % i, [128, 8, 128], BF16) for i in range(2)]
        wst2 = [p.sb("wost%d" % i, [128, 16, 128], F32) for i in range(2)]
        wb2 = [p.sb("wobf%d" % i, [128, 16, 128], BF16) for i in range(2)]
        pg = [p.sb("pg%d" % i, [128, TT], F32) for i in range(2)]
        gt = [p.sb("gt%d" % i, [128, TT], F32) for i in range(2)]
        macc = p.sb("macc", [128, TT], F32)
        tmp = p.sb("tmp3", [128, TT], F32)
        sq = [p.sb("sq3%d" % i, [128, TT], BF16) for i in range(2)]
        rs = p.sb("rs3", [128, TT], F32)
        xt = [p.sb("xt3%d" % i, [128, TT], F32) for i in range(2)]
        xo = [p.sb("xo3%d" % i, [128, TT], F32) for i in range(2)]
        acc = [p.ps("acc3%d" % i, [128, 512], F32) for i in range(4)]
        ss = p.ps("ss3", [128, 512], F32)
        ai = 0
        wi = 0
        import os
        STOP = int(os.environ.get("P3STOP", "99"))
        for tt in range(T // TT):
            t0 = tt * TT
            for i in range(3):
                yv = ybr[i].h.rearrange("(k p) t -> p k t", p=128)
                p.dma("sp", ysb[i][:], yv[:, :, t0:t0 + TT], reads=[ybr[i]], writes=[ysb[i]])
            for ct in range(16):
                for i in range(3):
                    ws, wbb = wst[wi % 2], wb[wi % 2]
                    pgt, gtt = pg[wi % 2], gt[wi % 2]
                    wi += 1
                    wv = wbr[i].h[l].rearrange("(k p) c -> p k c", p=128)
                    p.dma("sp", ws[:], wv[:, :, ct * 128:(ct + 1) * 128], reads=[wbr[i]], writes=[ws])
                    p.copy("pool", wbb[:], ws[:], [ws], [wbb])
                    a = acc[ai % 4]
                    ai += 1
                    for k in range(8):
                        p.mm(a[:], wbb[:, k, :], ysb[i][:, k, :], start=(k == 0), stop=(k == 7),
                             reads=[wbb, ysb[i]], writes=[a])
                    r0 = i * 2048 + ct * 128
                    p.dma("sp", pgt[:], c.projG[r0:r0 + 128, t0:t0 + TT], reads=[c.projG], writes=[pgt])
                    p.act(gtt[:], pgt[:], AF.Sigmoid, [pgt, c.cv], [gtt], bias=cvcol(c, l, "bg", i * 16 + ct))
                    if i == 0:
                        p.tt("dve", macc[:], a[:], gtt[:], ALU.mult, [a, gtt], [macc])
                    else:
                        p.tt("dve", tmp[:], a[:], gtt[:], ALU.mult, [a, gtt], [tmp])
                        if i == 1:
                            p.tt("dve", macc[:], macc[:], tmp[:], ALU.add, [macc, tmp], [macc])
                        else:
                            p.tt("dve", mg[:, ct, :], macc[:], tmp[:], ALU.add, [macc, tmp], [mg])
            if STOP <= 1:
                continue
            for c2 in range(16):
                ws, wbb = wst2[c2 % 2], wb2[c2 % 2]
                wv = c.w_out.h[l].rearrange("(k p) c -> p k c", p=128)
                p.dma("sp", ws[:], wv[:, :, c2 * 128:(c2 + 1) * 128], reads=[c.w_out], writes=[ws])
                p.copy("pool", wbb[:], ws[:], [ws], [wbb])
                a = acc[ai % 4]
                ai += 1
                for k in range(16):
                    p.mm(a[:], wbb[:, k, :], mg[:, k, :], start=(k == 0), stop=(k == 15), reads=[wbb, mg], writes=[a])
                p.copy("dve", osb[:, c2, :], a[:], [a], [osb])
                p.act(sq[c2 % 2][:], a[:], AF.Square, [a], [sq[c2 % 2]])
                p.mm(ss[:], c.ones_bf[:], sq[c2 % 2][:], start=(c2 == 0), stop=(c2 == 15),
                     reads=[c.ones_bf, sq[c2 % 2]], writes=[ss])
            if STOP <= 2:
                continue
            rstd_from_ss(p, rs, ss, D, TT)
            if STOP <= 3:
                continue
            for c2 in range(16):
                xx, xn = xt[c2 % 2], xo[c2 % 2]
                p.dma("sp", xx[:], xin[c2 * 128:(c2 + 1) * 128, t0:t0 + TT], reads=[xin], writes=[xx])
                p.tt("pool", tmp[:], osb[:, c2, :], rs[:], ALU.mult, [osb, rs], [tmp])
                p.stt(xn[:], tmp[:], cvcol(c, l, "post_g", c2), xx[:], ALU.mult, ALU.add, [tmp, xx, c.cv], [xn])
                p.dma("pool", xout[c2 * 128:(c2 + 1) * 128, t0:t0 + TT], xn[:], reads=[xn], writes=[xout])


def phase_ssd(p, c, l):
    T = c.T
    NCH = T // 128
    IDb = c.cm_bf[:, 0:128]
    ID = c.cm
    TRIU = c.cm[:, 128:256]
    SU = c.cm[:, 256:384]
    ONESF = c.cm[:, 384:512]
    ONEC = c.cm[:, 384:385]
    with p.scope():
        dtT = p.sb("dtT", [16, T], F32)
        adtT = p.sb("adtT", [16, T], F32)
        negA = p.sb("negA", [16, 1], F32)
        with p.scope():
            dtr = p.sb("dtr", [16, T], F32)
            p.dma("sp", dtr[:], c.projT[A_DT:A_DT + 16, 0:T], reads=[c.projT], writes=[dtr])
            p.act(dtr[:], dtr[:], AF.Exp, [dtr, c.cv], [dtr], bias=cvcol(c, l, "dt_bias", 0, 16))
            p.act(dtT[:], dtr[:], AF.Ln, [dtr, c.cm], [dtT], bias=c.cm[0:16, 384:385])
            p.act(negA[:], cvcol(c, l, "a_log", 0, 16), AF.Exp, [c.cv], [negA])
            p.ts("dve", negA[:], negA[:], -1.0, ALU.mult, [negA], [negA])
            p.ts("dve", adtT[:], dtT[:], negA[:, 0:1], ALU.mult, [dtT, negA], [adtT])
            xr = [p.sb("xr%d" % i, [128, T + 3], F32) for i in range(2)]
            ca = [p.sb("ca%d" % i, [128, T], F32) for i in range(2)]
            xo = [p.sb("cxo%d" % i, [128, T], BF16) for i in range(2)]
            for i in range(2):
                p.memset("pool", xr[i][:, 0:3], 0.0, writes=[xr[i]])
            for ci in range(16):
                x_, a_, o_ = xr[ci % 2], ca[ci % 2], xo[ci % 2]
                r0 = A_XBC + ci * 128
                p.dma("sp", x_[:, 3:T + 3], c.projT[r0:r0 + 128, 0:T], reads=[c.projT], writes=[x_])
                p.ts("dve", a_[:], x_[:, 3:T + 3], cvcol(c, l, "conv_w", 3 * 16 + ci), ALU.mult, [x_, c.cv], [a_],
                     s2=cvcol(c, l, "conv_b", ci), op1=ALU.add)
                for j in (2, 1, 0):
                    p.stt(a_[:], x_[:, j:T + j], cvcol(c, l, "conv_w", j * 16 + ci), a_[:], ALU.mult, ALU.add,
                          [x_, a_, c.cv], [a_])
                p.act(o_[:], a_[:], AF.Silu, [a_], [o_])
                p.dma("pool", c.xbcT[ci * 128:(ci + 1) * 128, 0:T], o_[:], reads=[o_], writes=[c.xbcT])
        xbc = [p.sb("xbc%d" % i, [128, 16, 128], BF16) for i in range(2)]
        zt = [p.sb("zt%d" % i, [128, 8, 128], F32) for i in range(2)]
        dtk = p.sb("dtk", [128, 32], F32)
        sm = p.sb("ssm_sm", [128, 64], F32)
        Xdt = p.sb("Xdt", [128, 1024], BF16)
        Xds = p.sb("Xds", [128, 1024], BF16)
        Btok = p.sb("Btok", [128, 512], BF16)
        adx = p.sb("adx", [128, 1024], F32)
        lall = p.sb("lall", [128, 16, 128], F32)
        E = p.sb("E", [128, 1024], F32)
        Lm = p.sb("Lm", [128, 2048], F32)
        CBm = p.sb("CBm", [128, 512], F32)
        M = p.sb("M", [128, 16, 128], BF16)
        hT = p.sb("hT", [128, 1024], F32)
        hTb = p.sb("hTb", [128, 1024], BF16)
        Y = p.sb("Y", [128, 1024], F32)
        t1 = p.sb("t1", [128, 1024], F32)
        sz = p.sb("sz", [128, 1024], F32)
        sqy = p.sb("sqy", [128, 1024], BF16)
        rsy = p.sb("rsy", [128, 512], F32)
        yo = [p.sb("yo%d" % i, [128, 8, 128], BF16) for i in range(2)]
        P0 = p.ps("P0", [128, 2048], BF16)
        P1 = p.ps("P1", [128, 2048], F32)
        P2 = p.ps("P2", [128, 1024], F32)
        p.memset("dve", hT[:], 0.0, writes=[hT])
        p.memset("pool", hTb[:], 0.0, writes=[hTb])
        xv = c.xbcT.h.rearrange("(k p) t -> p k t", p=128)
        zv = c.projT.h[A_Z:A_Z + 1024, :].rearrange("(k p) t -> p k t", p=128)
        yav = c.yaT.h.rearrange("(k p) t -> p k t", p=128)
        for ch in range(NCH):
            t0 = ch * 128
            xb, zz, yy = xbc[ch % 2], zt[ch % 2], yo[ch % 2]
            p.dma("sp", xb[:], xv[:, :, t0:t0 + 128], reads=[c.xbcT], writes=[xb])
            p.dma("sp", zz[:], zv[:, :, t0:t0 + 128], reads=[c.projT], writes=[zz])
            p.tr(P2[:, 0:16], dtT[:, t0:t0 + 128], ID[0:16, 0:16], [dtT, c.cm], [P2])
            p.tr(P2[:, 16:32], adtT[:, t0:t0 + 128], ID[0:16, 0:16], [adtT, c.cm], [P2])
            p.copy("act", dtk[:], P2[:, 0:32], [P2], [dtk])
            p.mm(P2[:, 64:80], SU, dtk[:, 16:32], reads=[c.cm, dtk], writes=[P2])
            p.mm(P2[:, 128:144], ONESF, dtk[:, 16:32], reads=[c.cm, dtk], writes=[P2])
            p.act(sm[:, 0:16], P2[:, 64:80], AF.Exp, [P2], [sm])
            p.act(sm[:, 16:32], P2[:, 128:144], AF.Exp, [P2], [sm])
            p.tt("dve", sm[:, 32:48], sm[:, 0:16], dtk[:, 0:16], ALU.mult, [sm, dtk], [sm])
            for j in range(12):
                p.tr(P0[:, j * 128:(j + 1) * 128], xb[:, j, :], IDb, [xb, c.cm_bf], [P0])
            px = P0[:, 0:1024].rearrange("p (h d) -> p h d", d=64)
            p.tt("dve", Xdt[:].rearrange("p (h d) -> p h d", d=64), px,
                 dtk[:, 0:16].unsqueeze(2).to_broadcast([128, 16, 64]), ALU.mult, [P0, dtk], [Xdt])
            p.tt("dve", Xds[:].rearrange("p (h d) -> p h d", d=64), px,
                 sm[:, 32:48].unsqueeze(2).to_broadcast([128, 16, 64]), ALU.mult, [P0, sm], [Xds])
            p.copy("act", Btok[:], P0[:, 1024:1536], [P0], [Btok])
            p.copy("pool", adx[:].rearrange("p (h d) -> p h d", d=64),
                   dtk[:, 16:32].unsqueeze(2).to_broadcast([128, 16, 64]), [dtk], [adx])
            p.tt("pool", lall[:], dtk[:, 16:32].unsqueeze(2).to_broadcast([128, 16, 128]),
                 SU.unsqueeze(1).to_broadcast([128, 16, 128]), ALU.mult, [dtk, c.cm], [lall])
            for h in range(16):
                p.mm(P1[:, h * 128:(h + 1) * 128], lall[:, h, :], TRIU, reads=[lall, c.cm], writes=[P1])
            p.act(Lm[:], P1[:], AF.Exp, [P1], [Lm])
            for j in range(8):
                p.mm(P2[:, j * 128:(j + 1) * 128], adx[:, j * 128:(j + 1) * 128], TRIU, reads=[adx, c.cm], writes=[P2])
            p.act(E[:], P2[:], AF.Exp, [P2], [E])
            for g in range(4):
                p.mm(P2[:, g * 128:(g + 1) * 128], xb[:, 8 + g, :], xb[:, 12 + g, :], reads=[xb], writes=[P2])
            p.tt("dve", CBm[:].rearrange("p (g l) -> p g l", l=128), P2[:, 0:512].rearrange("p (g l) -> p g l", l=128),
                 TRIU.unsqueeze(1).to_broadcast([128, 4, 128]), ALU.mult, [P2, c.cm], [CBm])
            p.tt("dve", M[:].rearrange("p (g r) l -> p g r l", r=4), Lm[:].rearrange("p (g r l) -> p g r l", r=4, l=128),
                 CBm[:].rearrange("p (g l) -> p g l", l=128).unsqueeze(2).to_broadcast([128, 4, 4, 128]), ALU.mult,
                 [Lm, CBm], [M])
            for h in range(16):
                p.mm(P1[(h % 2) * 64:(h % 2) * 64 + 64, (h // 2) * 128:(h // 2) * 128 + 128], Xdt[:, h * 64:(h + 1) * 64],
                     M[:, h, :], reads=[Xdt, M], writes=[P1])
            for j in range(8):
                p.mm(P2[:, j * 128:(j + 1) * 128], hTb[:, j * 128:(j + 1) * 128], xb[:, 12 + j // 2, :],
                     reads=[hTb, xb], writes=[P2])
            p.tt("dve", t1[:], P2[:], E[:], ALU.mult, [P2, E], [t1])
            p.tt("dve", Y[:], t1[:], P1[:, 0:1024], ALU.add, [t1, P1], [Y])
            for g in range(4):
                p.mm(P1[:, 1024 + g * 256:1024 + (g + 1) * 256], Btok[:, g * 128:(g + 1) * 128],
                     Xds[:, g * 256:(g + 1) * 256], reads=[Btok, Xds], writes=[P1])
            p.tt("pool", hT[:].rearrange("p (h d) -> p h d", d=64), hT[:].rearrange("p (h d) -> p h d", d=64),
                 sm[:, 16:32].unsqueeze(2).to_broadcast([128, 16, 64]), ALU.mult, [hT, sm], [hT])
            p.tt("dve", hT[:], hT[:], P1[:, 1024:2048], ALU.add, [hT, P1], [hT])
            p.copy("act", hTb[:], hT[:], [hT], [hTb])
            o_d, _ = CV["d_exp"]
            o_n, _ = CV["ssm_norm"]
            dcol = c.cv[:, l * NV + o_d:l * NV + o_d + 8].unsqueeze(2).to_broadcast([128, 8, 128])
            ncol = c.cv[:, l * NV + o_n:l * NV + o_n + 8].unsqueeze(2).to_broadcast([128, 8, 128])
            Y3 = Y[:].rearrange("p (j l) -> p j l", l=128)
            t13 = t1[:].rearrange("p (j l) -> p j l", l=128)
            p.tt("pool", t13, xb[:, 0:8, :], dcol, ALU.mult, [xb, c.cv], [t1])
            p.tt("pool", Y[:], Y[:], t1[:], ALU.add, [Y, t1], [Y])
            p.act(sz[:], zz[:].rearrange("p j l -> p (j l)"), AF.Silu, [zz], [sz])
            p.tt("dve", Y[:], Y[:], sz[:], ALU.mult, [Y, sz], [Y])
            p.tt("pool", sqy[:], Y[:], Y[:], ALU.mult, [Y], [sqy])
            for g in range(4):
                for i2 in range(2):
                    j = 2 * g + i2
                    p.mm(P2[:, g * 128:(g + 1) * 128], c.ones_bf[:], sqy[:, j * 128:(j + 1) * 128],
                         start=(i2 == 0), stop=(i2 == 1), reads=[c.ones_bf, sqy], writes=[P2])
            rstd_from_ss(p, rsy, P2, 256, 512)
            p.tt("dve", Y[:].rearrange("p (g i l) -> p g i l", i=2, l=128), Y[:].rearrange("p (g i l) -> p g i l", i=2, l=128),
                 rsy[:].rearrange("p (g l) -> p g l", l=128).unsqueeze(2).to_broadcast([128, 4, 2, 128]), ALU.mult,
                 [Y, rsy], [Y])
            p.tt("dve", yy[:], Y3, ncol, ALU.mult, [Y, c.cv], [yy])
            p.dma("pool", yav[:, :, t0:t0 + 128], yy[:], reads=[yy], writes=[c.yaT])


R_Q, R_IQ, R_K, R_IK = 0, 1024, 2048, 2304
MAGIC = 12582912.0
NEGBIG = -1.0e30


def phase_dsa(p, c, l):
    T = c.T
    NQ = T // 128
    ID = c.cm[:, 0:128]
    IDb = c.cm_bf[:, 0:128]
    ONESF = c.cm[:, 384:512]
    PERM = c.cm[:, 512:640]
    NEGM = c.cm[:, 640:768]
    INV = c.cm[:, 768:769]
    SGN = c.cm[:, 769:770]
    HPI = c.cm[:, 770:771]
    TWO_PI = 6.283185307179586
    C1 = 6.28125
    C2 = TWO_PI - C1
    with p.scope():
        Ct = p.sb("ropeC", [128, T], F32)
        St = p.sb("ropeS", [128, T], F32)
        with p.scope():
            pi_ = p.sb("posi", [128, T], I32)
            ang = p.sb("ang", [128, T], F32)
            kk = p.sb("angk", [128, T], F32)
            r = p.sb("angr", [128, T], F32)
            p.dma("sp", pi_[:], c.pos_d[:], reads=[c.pos_d], writes=[pi_])
            p.copy("dve", ang[:], pi_[:], [pi_], [ang])
            p.ts("dve", ang[:], ang[:], INV, ALU.mult, [ang, c.cm], [ang])
            p.ts("dve", kk[:], ang[:], 1.0 / TWO_PI, ALU.mult, [ang], [kk], s2=MAGIC, op1=ALU.add)
            p.ts("dve", kk[:], kk[:], MAGIC, ALU.subtract, [kk], [kk])
            p.stt(r[:], kk[:], -C1, ang[:], ALU.mult, ALU.add, [kk, ang], [r])
            p.stt(r[:], kk[:], -C2, r[:], ALU.mult, ALU.add, [kk, r], [r])
            p.ts("dve", r[:], r[:], 3.14159, ALU.min, [r], [r], s2=-3.14159, op1=ALU.max)
            p.act(St[:], r[:], AF.Sin, [r], [St])
            p.ts("dve", St[:], St[:], SGN, ALU.mult, [St, c.cm], [St])
            p.ts("dve", kk[:], r[:], -1.0, ALU.mult, [r], [kk])
            p.tt("dve", kk[:], kk[:], r[:], ALU.max, [kk, r], [kk])
            p.act(Ct[:], kk[:], AF.Sin, [kk, c.cm], [Ct], bias=HPI, scale=-1.0)
        with p.scope():
            xs = [p.sb("rx%d" % i, [128, T], F32) for i in range(2)]
            ob = [p.sb("rob%d" % i, [128, T], BF16) for i in range(2)]
            tmp = p.sb("rtmp", [128, 512], F32)
            o1 = p.sb("ro1", [128, 512], F32)
            sqk = p.sb("rsq", [128, 512], F32)
            rsk = p.sb("rrs", [128, 512], F32)
            PS = [p.ps("rps%d" % i, [128, 512], F32) for i in range(4)]
            tiles = [(C_Q + i * 128, R_Q + i * 128, 128) for i in range(8)] + \
                    [(C_IQ + i * 128, R_IQ + i * 128, 128) for i in range(8)] + \
                    [(C_K + i * 128, R_K + i * 128, 128) for i in range(2)] + [(C_IK, R_IK, 64)]
            for ti, (src, dst, nr) in enumerate(tiles):
                x_, o_ = xs[ti % 2], ob[ti % 2]
                p.dma("sp", x_[0:nr, :], c.projT[src:src + nr, 0:T], reads=[c.projT], writes=[x_])
                for j in range(T // 512):
                    sl = slice(j * 512, (j + 1) * 512)
                    if nr == 64:
                        ps = PS[j % 2]
                        p.mm(ps[0:64, :], ONESF[0:64, 0:64], x_[0:64, sl], reads=[c.cm, x_], writes=[ps])
                        p.stt(x_[0:64, sl], ps[0:64, :], -1.0 / 64, x_[0:64, sl], ALU.mult, ALU.add, [ps, x_], [x_])
                        p.tt("pool", sqk[0:64, :], x_[0:64, sl], x_[0:64, sl], ALU.mult, [x_], [sqk])
                        p.mm(ps[0:64, :], ONESF[0:64, 0:64], sqk[0:64, :], reads=[c.cm, sqk], writes=[ps])
                        p.ts("dve", rsk[0:64, :], ps[0:64, :], 1.0 / 64, ALU.mult, [ps], [rsk], s2=EPS, op1=ALU.add)
                        p.act(rsk[0:64, :], rsk[0:64, :], AF.Sqrt, [rsk], [rsk])
                        p.op("dve", lambda e, a=rsk[0:64, :]: e.reciprocal(out=a, in_=a), [rsk], [rsk])
                        p.tt("dve", x_[0:64, sl], x_[0:64, sl], rsk[0:64, :], ALU.mult, [x_, rsk], [x_])
                        p.ts("dve", x_[0:64, sl], x_[0:64, sl], cvcol(c, l, "ikn_w", 0, 64), ALU.mult, [x_, c.cv], [x_],
                             s2=cvcol(c, l, "ikn_b", 0, 64), op1=ALU.add)
                    ps = PS[2 + j % 2]
                    p.mm(ps[0:nr, :], PERM[0:nr, 0:nr], x_[0:nr, sl], reads=[c.cm, x_], writes=[ps])
                    p.tt("dve", tmp[0:nr, :], ps[0:nr, :], St[0:nr, sl], ALU.mult, [ps, St], [tmp])
                    p.tt("pool", o1[0:nr, :], x_[0:nr, sl], Ct[0:nr, sl], ALU.mult, [x_, Ct], [o1])
                    p.tt("dve", o_[0:nr, sl], o1[0:nr, :], tmp[0:nr, :], ALU.add, [o1, tmp], [o_])
                p.dma("pool", c.dsaT[dst:dst + nr, 0:T], o_[0:nr, :], reads=[o_], writes=[c.dsaT])
    with p.scope():
        ik2 = p.sb("ik2", [128, T], BF16)
        kb = p.sb("kb", [64, 4, T], BF16)
        vtok = p.sb("vtok", [128, NQ, 256], BF16)
        P0 = p.ps("dP0", [128, 1024], BF16)
        PLa = [p.ps("dPLa%d" % i, [128, 512], F32) for i in range(2)]
        PLb = [p.ps("dPLb%d" % i, [128, 512], F32) for i in range(3)]
        PO = p.ps("dPO", [64, 512], F32)
        PR = p.ps("dPR", [64, 512], F32)
        for b in (0, 64):
            p.dma("sp", ik2[b:b + 64, :], c.dsaT[R_IK:R_IK + 64, 0:T], reads=[c.dsaT], writes=[ik2])
        p.dma("sp", kb[:], c.dsaT.h[R_K:R_K + 256, :].rearrange("(g d) t -> d g t", d=64), reads=[c.dsaT], writes=[kb])
        with p.scope():
            vf = p.sb("vf", [128, 2, T], F32)
            vb = p.sb("vb", [128, 2, T], BF16)
            p.dma("sp", vf[:], c.projT.h[C_V:C_V + 256, :].rearrange("(k p) t -> p k t", p=128), reads=[c.projT], writes=[vf])
            p.copy("pool", vb[:], vf[:], [vf], [vb])
            for b0 in range(0, NQ, 4):
                for blk in range(b0, b0 + 4):
                    for t2 in range(2):
                        i = (blk - b0) * 2 + t2
                        p.tr(P0[:, i * 128:(i + 1) * 128], vb[:, t2, blk * 128:(blk + 1) * 128], IDb, [vb, c.cm_bf], [P0])
                p.copy("dve", vtok[:, b0:b0 + 4, :].rearrange("p b c -> p (b c)"), P0[:, 0:1024], [P0], [vtok])
        Ss = [p.sb("dS%d" % i, [128, T], F32) for i in range(2)]
        msk = p.sb("dmsk", [128, T], BF16)
        mskTs = [p.sb("dmskT%d" % i, [128, NQ, 128], BF16) for i in range(2)]
        qsb = [p.sb("dq%d" % i, [64, 16, 128], BF16) for i in range(2)]
        iqs = [p.sb("diq%d" % i, [128, 8, 128], BF16) for i in range(2)]
        gts = [p.sb("dg%d" % i, [64, 16, 128], F32) for i in range(2)]
        iwc = [p.sb("diw%d" % i, [16, 128], F32) for i in range(2)]
        rl = [p.sb("drl%d" % i, [128, 512], F32) for i in range(3)]
        PT = [p.sb("dPT%d" % i, [128, 512], BF16) for i in range(3)]
        PTm = [p.sb("dPTm%d" % i, [128, 512], BF16) for i in range(3)]
        iwk = p.sb("diwk", [128, 16], F32)
        bs = p.sb("dbs", [128, 8], F32)
        Wt = p.sb("dWt", [128, 32], F32)
        junk = p.sb("djunk", [128, T], BF16)
        thr = p.sb("dthr", [128, 1], F32)
        rec = p.sb("drec", [64, 512], F32)
        ov = p.sb("dov", [64, 2048], F32)
        sg = p.sb("dsg", [64, 2048], F32)
        yo = [p.sb("dyo%d" % i, [64, 16, 128], BF16) for i in range(2)]
        qv = c.dsaT.h[R_Q:R_Q + 1024, :].rearrange("(h d) t -> d h t", d=64)
        iqv = c.dsaT.h[R_IQ:R_IQ + 1024, :].rearrange("(k p) t -> p k t", p=128)
        gv = c.projT.h[C_G:C_G + 1024, :].rearrange("(h d) t -> d h t", d=64)
        ycv = c.ycT.h.rearrange("(h d) t -> d h t", d=64)
        cnt = {"a": 0, "b": 0}

        def stageA(qi):
            q0 = qi * 128
            n = q0 + 128
            qs_, iq_, g_, iw_ = qsb[qi % 2], iqs[qi % 2], gts[qi % 2], iwc[qi % 2]
            S, mskT = Ss[qi % 2], mskTs[qi % 2]
            p.dma("sp", qs_[:], qv[:, :, q0:n], reads=[c.dsaT], writes=[qs_])
            p.dma("sp", iq_[:], iqv[:, :, q0:n], reads=[c.dsaT], writes=[iq_])
            p.dma("sp", g_[:], gv[:, :, q0:n], reads=[c.projT], writes=[g_])
            p.dma("sp", iw_[:], c.projT[C_IK + 64:C_IK + 80, q0:n], reads=[c.projT], writes=[iw_])
            p.tr(PLa[0][:, 0:16], iw_[:], ID[0:16, 0:16], [iw_, c.cm], [PLa[0]])
            p.copy("act", iwk[:], PLa[0][:, 0:16], [PLa[0]], [iwk])
            yield
            steps = [(s0, min(512, n - s0), h) for s0 in range(0, n, 512) for h in range(16)]
            NS = len(steps)
            for j in range(NS + 2):
                if j < NS:
                    s0, w, h = steps[j]
                    b = (h % 2) * 64
                    p.mm(PLa[j % 2][:, 0:w], iq_[b:b + 64, h // 2, :], ik2[b:b + 64, s0:s0 + w], reads=[iq_, ik2], writes=[PLa[j % 2]])
                if 0 <= j - 1 < NS:
                    s0, w, h = steps[j - 1]
                    p.act(rl[(j - 1) % 3][:, 0:w], PLa[(j - 1) % 2][:, 0:w], AF.Relu, [PLa[(j - 1) % 2]], [rl[(j - 1) % 3]])
                if 0 <= j - 2 < NS:
                    s0, w, h = steps[j - 2]
                    r_ = rl[(j - 2) % 3]
                    if h == 0:
                        p.ts("dve", S[:, s0:s0 + w], r_[:, 0:w], iwk[:, 0:1], ALU.mult, [r_, iwk], [S])
                    else:
                        p.stt(S[:, s0:s0 + w], r_[:, 0:w], iwk[:, h:h + 1], S[:, s0:s0 + w], ALU.mult, ALU.add, [r_, iwk, S], [S])
                yield
            if qi >= 2:
                p.op("dve", lambda e, a=S[:, 0:n]: e.tensor_reduce(out=bs[:, 0:1], in_=a, op=ALU.min, axis=AX.X), [S], [bs])
                p.op("dve", lambda e, a=S[:, 0:n]: e.tensor_reduce(out=bs[:, 1:2], in_=a, op=ALU.max, axis=AX.X), [S], [bs])
                yield
            p.tt("dve", S[:, q0:n], S[:, q0:n], NEGM, ALU.add, [S, c.cm], [S])
            if qi >= 2:
                NIT = 28
                p.tt("dve", bs[:, 2:3], bs[:, 1:2], bs[:, 0:1], ALU.subtract, [bs], [bs])
                p.ts("dve", bs[:, 2:3], bs[:, 2:3], 1.0001, ALU.mult, [bs], [bs], s2=1e-6, op1=ALU.add)
                p.ts("dve", Wt[:, 0:NIT], c.cm[:, 1280:1280 + NIT], bs[:, 2:3], ALU.mult, [c.cm, bs], [Wt])
                p.ts("dve", bs[:, 3:4], bs[:, 0:1], -1.0, ALU.mult, [bs], [bs])
                p.tt("dve", bs[:, 4:5], bs[:, 3:4], Wt[:, 0:1], ALU.subtract, [bs, Wt], [bs])
                yield
                for k in range(NIT):
                    p.op("act", lambda e, a=S[:, 0:n], o=junk[:, 0:n]: e.activation(out=o, in_=a, func=AF.Sign, bias=bs[:, 4:5],
                                                                                     scale=1.0, accum_out=bs[:, 5:6]),
                         [S, bs], [junk, bs])
                    yield
                    p.stt(bs[:, 6:7], bs[:, 5:6], float(512 - n), Wt[:, k:k + 1], ALU.is_ge, ALU.mult, [bs, Wt], [bs])
                    p.tt("dve", bs[:, 3:4], bs[:, 3:4], bs[:, 6:7], ALU.subtract, [bs], [bs])
                    if k + 1 < NIT:
                        p.tt("dve", bs[:, 4:5], bs[:, 3:4], Wt[:, k + 1:k + 2], ALU.subtract, [bs, Wt], [bs])
                    yield
                p.ts("dve", thr[:], bs[:, 3:4], -1.0, ALU.mult, [bs], [thr])
            else:
                p.memset("dve", thr[:], -1.0e29, writes=[thr])
            p.ts("dve", msk[:, 0:n], S[:, 0:n], thr[:, 0:1], ALU.is_ge, [S, thr], [msk])
            yield
            for b0 in range(0, qi + 1, 8):
                nb = min(8, qi + 1 - b0)
                for j in range(nb):
                    p.tr(P0[:, j * 128:(j + 1) * 128], msk[:, (b0 + j) * 128:(b0 + j + 1) * 128], IDb, [msk, c.cm_bf], [P0])
                p.copy("act", mskT[:, b0:b0 + nb, :].rearrange("p b c -> p (b c)"), P0[:, 0:nb * 128], [P0], [mskT])
                yield

        def stageB(qi):
            q0 = qi * 128
            n = q0 + 128
            qs_, g_, y_ = qsb[qi % 2], gts[qi % 2], yo[qi % 2]
            mskT = mskTs[qi % 2]
            groups = [(g, sb) for g in range(4) for sb in range(qi + 1)]

            def qk(gi):
                g, sb = groups[gi]
                pl = PLb[gi % 3]
                p.mm(pl[:, :], kb[:, g, sb * 128:(sb + 1) * 128], qs_[:, 4 * g:4 * g + 4, :].rearrange("p h q -> p (h q)"),
                     reads=[kb, qs_], writes=[pl])

            NGp = len(groups)
            for i in range(NGp + 3):
                if i < NGp:
                    qk(i)
                if 0 <= i - 1 < NGp:
                    gi = i - 1
                    p.act(PT[gi % 3][:], PLb[gi % 3][:], AF.Exp, [PLb[gi % 3]], [PT[gi % 3]], scale=0.125)
                if 0 <= i - 2 < NGp:
                    gi = i - 2
                    g, sb = groups[gi]
                    p.tt("pool", PTm[gi % 3][:].rearrange("p (h q) -> p h q", q=128), PT[gi % 3][:].rearrange("p (h q) -> p h q", q=128),
                         mskT[:, sb:sb + 1, :].to_broadcast([128, 4, 128]), ALU.mult, [PT[gi % 3], mskT], [PTm[gi % 3]])
                if 0 <= i - 3 < NGp:
                    gi = i - 3
                    g, sb = groups[gi]
                    ptm = PTm[gi % 3]
                    p.mm(PO[:], vtok[:, sb, g * 64:(g + 1) * 64], ptm[:], start=(sb == 0), stop=(sb == qi),
                         reads=[vtok, ptm], writes=[PO])
                    p.mm(PR[:], c.ones_bf[:, 0:64], ptm[:], start=(sb == 0), stop=(sb == qi),
                         reads=[c.ones_bf, ptm], writes=[PR])
                    if sb == qi:
                        p.op("dve", lambda e: e.reciprocal(out=rec[:], in_=PR[:]), [PR], [rec])
                        p.tt("dve", ov[:, g * 512:(g + 1) * 512], PO[:], rec[:], ALU.mult, [PO, rec], [ov])
                yield
            p.act(sg[:], g_[:].rearrange("p h l -> p (h l)"), AF.Silu, [g_], [sg])
            p.tt("pool", y_[:].rearrange("p h l -> p (h l)"), ov[:], sg[:], ALU.mult, [ov, sg], [y_])
            p.dma("pool", ycv[:, :, q0:n], y_[:], reads=[y_], writes=[c.ycT])
            yield

        for _ in stageA(0):
            pass
        for qi in range(NQ):
            gb = stageB(qi)
            ga = stageA(qi + 1) if qi + 1 < NQ else iter(())
            la, lb = True, True
            while la or lb:
                for _ in range(2):
                    if la:
                        try:
                            next(ga)
                        except StopIteration:
                            la = False
                if lb:
                    try:
                        next(gb)
                    except StopIteration:
                        lb = False


def phase_rwkv(p, c, l):
    T = c.T
    NB = T // 128
    ID = c.cm[:, 0:128]
    ONEC = c.cm[:, 384:385]
    NHALF = c.cm[:, 771:772]
    GNEPS = c.cm[:, 772:773]
    BLK = c.cm[:, 896:1024]
    rows5 = c.rowsD.h.rearrange("t (f c) -> t f c", f=5)
    with p.scope():
        negw0 = p.sb("negw0", [128, 8], F32)
        o_w0, _ = CV["w0"]
        p.ts("dve", negw0[:], c.cv[:, l * NV + o_w0:l * NV + o_w0 + 8], -1.0, ALU.mult, [c.cv], [negw0])
        w2s = p.sb("w2s", [128, 1024], F32)
        w2b = p.sb("w2b", [128, 1024], BF16)
        p.dma("sp", w2s[0:64, :], c.rw2.h[l], reads=[c.rw2], writes=[w2s])
        p.dma("sp", w2s[64:128, :], c.ra2.h[l], reads=[c.ra2], writes=[w2s])
        p.copy("dve", w2b[:], w2s[:], [w2s], [w2b])
        xr = [p.sb("bxr%d" % i, [128, T + 1], F32) for i in range(3)]
        mx = {n: p.sb("bm_" + n, [128, T], F32) for n in ("wa", "r", "k", "v")}
        wab = p.sb("wab", [128, T], BF16)
        dec = p.sb("bdec", [128, T], F32)
        av = p.sb("bav", [128, T], F32)
        kkn = p.sb("bkkn", [128, T], F32)
        t2 = p.sb("bt2", [128, 512], F32)
        t3 = p.sb("bt3", [128, 512], F32)
        tro = [p.sb("btro%d" % i, [128, 5, 128], F32) for i in range(2)]
        PS = [p.ps("bps%d" % i, [128, 512], F32) for i in range(4)]
        PTt = [p.ps("bpt%d" % i, [128, 5 * 128], F32) for i in range(2)]
        for i in range(3):
            p.memset("pool", xr[i][:, 0:1], 0.0, writes=[xr[i]])
        xi = [0]

        def mix(src, mu_name, mu_j, dst):
            x_ = xr[xi[0] % 3]
            xi[0] += 1
            p.dma("sp", x_[:, 1:T + 1], c.projT[src:src + 128, 0:T], reads=[c.projT], writes=[x_])
            p.tt("pool", dst[:], x_[:, 0:T], x_[:, 1:T + 1], ALU.subtract, [x_], [dst])
            p.stt(dst[:], dst[:], cvcol(c, l, mu_name, mu_j), x_[:, 1:T + 1], ALU.mult, ALU.add, [dst, x_, c.cv], [dst])

        mix(B_WA, "mu_wa", 0, mx["wa"])
        p.act(mx["wa"][0:64, :], mx["wa"][0:64, :], AF.Tanh, [mx["wa"]], [mx["wa"]])
        p.copy("dve", wab[:], mx["wa"][:], [mx["wa"]], [wab])
        for j in range(8):
            mix(B_R + j * 128, "mu_r", j, mx["r"])
            mix(B_K + j * 128, "mu_k", j, mx["k"])
            mix(B_V + j * 128, "mu_v", j, mx["v"])
            p.dma("pool", c.vD[j * 128:(j + 1) * 128, 0:T], mx["v"][:], reads=[mx["v"]], writes=[c.vD])
            for q in range(T // 512):
                sl = slice(q * 512, (q + 1) * 512)
                pw, pa, pk, pr = PS
                p.mm(pw[:], w2b[0:64, j * 128:(j + 1) * 128], wab[0:64, sl], reads=[w2b, wab], writes=[pw])
                p.mm(pa[:], w2b[64:128, j * 128:(j + 1) * 128], wab[64:128, sl], reads=[w2b, wab], writes=[pa])
                p.act(t2[:], pw[:], AF.Exp, [pw, negw0], [t2], bias=negw0[:, j:j + 1], scale=-1.0)
                p.act(t2[:], t2[:], AF.Ln, [t2, c.cm], [t2], bias=ONEC)
                p.act(t2[:], t2[:], AF.Exp, [t2, c.cm], [t2], bias=NHALF, scale=-1.0)
                p.act(dec[:, sl], t2[:], AF.Exp, [t2], [dec], scale=-1.0)
                p.act(av[:, sl], pa[:], AF.Sigmoid, [pa, c.cv], [av], bias=cvcol(c, l, "a0", j))
                p.ts("dve", kkn[:, sl], mx["k"][:, sl], cvcol(c, l, "k_k", j), ALU.mult, [mx["k"], c.cv], [kkn])
                p.tt("pool", t3[:], kkn[:, sl], kkn[:, sl], ALU.mult, [kkn], [t3])
                p.mm(pk[:], BLK, t3[:], reads=[c.cm, t3], writes=[pk])
                p.act(t3[:], pk[:], AF.Sqrt, [pk], [t3])
                p.ts("dve", t3[:], t3[:], 1e-12, ALU.max, [t3], [t3])
                p.op("dve", lambda e, a=t3[:]: e.reciprocal(out=a, in_=a), [t3], [t3])
                p.tt("dve", kkn[:, sl], kkn[:, sl], t3[:], ALU.mult, [kkn, t3], [kkn])
                p.ts("dve", t3[:], av[:, sl], -1.0, ALU.add, [av], [t3], s2=cvcol(c, l, "k_a", j), op1=ALU.mult)
                p.ts("dve", t3[:], t3[:], 1.0, ALU.add, [t3], [t3])
                p.tt("dve", mx["k"][:, sl], mx["k"][:, sl], t3[:], ALU.mult, [mx["k"], t3], [mx["k"]])
                p.tt("pool", av[:, sl], av[:, sl], kkn[:, sl], ALU.mult, [av, kkn], [av])
                p.tt("pool", t3[:], mx["r"][:, sl], mx["k"][:, sl], ALU.mult, [mx["r"], mx["k"]], [t3])
                p.ts("dve", t3[:], t3[:], cvcol(c, l, "r_k", j), ALU.mult, [t3, c.cv], [t3])
                p.mm(pr[:], BLK, t3[:], reads=[c.cm, t3], writes=[pr])
                p.copy("act", t2[:], pr[:], [pr], [t2])
                p.dma("pool", c.rkD[j * 128:(j + 1) * 128, sl], t2[:], reads=[t2], writes=[c.rkD])
            srcs = [kkn, dec, av, mx["k"], mx["r"]]
            for bi in range(NB):
                pt = PTt[bi % 2]
                to = tro[bi % 2]
                for f in range(5):
                    p.tr(pt[:, f * 128:(f + 1) * 128], srcs[f][:, bi * 128:(bi + 1) * 128], ID, [srcs[f], c.cm], [pt])
                p.copy("act" if bi % 2 else "dve", to[:].rearrange("p f c -> p (f c)"), pt[:], [pt], [to])
                p.dma("pool", rows5[bi * 128:(bi + 1) * 128, :, j * 128:(j + 1) * 128], to[:], reads=[to], writes=[c.rowsD])
    TB = 128
    with p.scope():
        S = p.sb("rS", [64, 1024], F32)
        tmp = p.sb("rtmp", [64, 1024], F32)
        tmp2 = p.sb("rtmp2", [64, 1024], F32)
        sa = p.sb("rsa", [64, 16], F32)
        Vc = [p.sb("rVc%d" % i, [64, 16, TB], F32) for i in range(2)]
        Yc = [p.sb("rYc%d" % i, [64, 16, TB], F32) for i in range(2)]
        NBUF = 4
        rb = [p.sb("rrow%d" % i, [64, 5, 1024], F32) for i in range(NBUF)]
        p.memset("dve", S[:], 0.0, writes=[S])
        vv = c.vD.h.rearrange("(h v) t -> v h t", v=64)
        yv = c.yD.h.rearrange("(h v) t -> v h t", v=64)
        S3 = S[:].rearrange("p (h k) -> p h k", k=64)
        T3 = tmp[:].rearrange("p (h k) -> p h k", k=64)
        U3 = tmp2[:].rearrange("p (h k) -> p h k", k=64)
        for t in range(T):
            blk, tl = divmod(t, TB)
            vc, yc = Vc[blk % 2], Yc[blk % 2]
            if tl == 0:
                p.dma("sp", vc[:], vv[:, :, blk * TB:(blk + 1) * TB], reads=[c.vD], writes=[vc])
            r_ = rb[t % NBUF]
            p.dma("sp", r_[:].rearrange("p f c -> p (f c)"), c.rowsD[t:t + 1, :].to_broadcast([64, 5 * 1024]),
                  reads=[c.rowsD], writes=[r_])
            KK, W, B_, K_, R_ = [r_[:, f, :] for f in range(5)]
            p.tt("dve", tmp[:], S[:], KK, ALU.mult, [S, r_], [tmp])
            p.op("dve", lambda e, o=sa[:], i=T3: e.tensor_reduce(out=o, in_=i, op=ALU.add, axis=AX.X, negate=True), [tmp], [sa])
            p.tt("pool", S[:], S[:], W, ALU.mult, [S, r_], [S])
            p.tt("pool", U3, vc[:, :, tl:tl + 1].to_broadcast([64, 16, 64]), K_.rearrange("p (h k) -> p h k", k=64), ALU.mult,
                 [vc, r_], [tmp2])
            p.tt("pool", S[:], S[:], tmp2[:], ALU.add, [S, tmp2], [S])
            p.tt("dve", T3, sa[:].unsqueeze(2).to_broadcast([64, 16, 64]), B_.rearrange("p (h k) -> p h k", k=64), ALU.mult,
                 [sa, r_], [tmp])
            p.tt("dve", S[:], S[:], tmp[:], ALU.add, [S, tmp], [S])
            p.tt("dve", tmp[:], S[:], R_, ALU.mult, [S, r_], [tmp])
            p.op("dve", lambda e, o=yc[:, :, tl], i=T3: e.tensor_reduce(out=o, in_=i, op=ALU.add, axis=AX.X), [tmp], [yc])
            if tl == TB - 1:
                p.dma("pool", yv[:, :, blk * TB:(blk + 1) * TB], yc[:], reads=[yc], writes=[c.yD])
    with p.scope():
        yt = [p.sb("cy%d" % i, [128, T], F32) for i in range(2)]
        vt = [p.sb("cv_%d" % i, [128, T], F32) for i in range(2)]
        rkt = [p.sb("crk%d" % i, [128, T], F32) for i in range(2)]
        gt = [p.sb("cg%d" % i, [128, T], F32) for i in range(2)]
        ob = [p.sb("cob%d" % i, [128, T], BF16) for i in range(2)]
        t2 = p.sb("ct2", [128, 512], F32)
        t3 = p.sb("ct3", [128, 512], F32)
        PS = [p.ps("cps%d" % i, [128, 512], F32) for i in range(4)]
        for j in range(8):
            y_, v_, rk_, g_, o_ = yt[j % 2], vt[j % 2], rkt[j % 2], gt[j % 2], ob[j % 2]
            rs_ = slice(j * 128, (j + 1) * 128)
            p.dma("sp", y_[:], c.yD[rs_, 0:T], reads=[c.yD], writes=[y_])
            p.dma("sp", v_[:], c.vD[rs_, 0:T], reads=[c.vD], writes=[v_])
            p.dma("sp", rk_[:], c.rkD[rs_, 0:T], reads=[c.rkD], writes=[rk_])
            p.dma("sp", g_[:], c.projT[B_G + j * 128:B_G + (j + 1) * 128, 0:T], reads=[c.projT], writes=[g_])
            for q in range(T // 512):
                sl = slice(q * 512, (q + 1) * 512)
                pm, pv = PS[(2 * q) % 4], PS[(2 * q + 1) % 4]
                p.mm(pm[:], BLK, y_[:, sl], reads=[c.cm, y_], writes=[pm])
                p.stt(y_[:, sl], pm[:], -1.0 / 64, y_[:, sl], ALU.mult, ALU.add, [pm, y_], [y_])
                p.tt("pool", t2[:], y_[:, sl], y_[:, sl], ALU.mult, [y_], [t2])
                p.mm(pv[:], BLK, t2[:], reads=[c.cm, t2], writes=[pv])
                p.ts("dve", t3[:], pv[:], 1.0 / 64, ALU.mult, [pv], [t3], s2=64e-5, op1=ALU.add)
                p.act(t3[:], t3[:], AF.Sqrt, [t3], [t3])
                p.op("dve", lambda e, a=t3[:]: e.reciprocal(out=a, in_=a), [t3], [t3])
                p.tt("dve", y_[:, sl], y_[:, sl], t3[:], ALU.mult, [y_, t3], [y_])
                p.ts("dve", y_[:, sl], y_[:, sl], cvcol(c, l, "ln_w", j), ALU.mult, [y_, c.cv], [y_],
                     s2=cvcol(c, l, "ln_b", j), op1=ALU.add)
                p.tt("pool", t2[:], rk_[:, sl], v_[:, sl], ALU.mult, [rk_, v_], [t2])
                p.tt("dve", y_[:, sl], y_[:, sl], t2[:], ALU.add, [y_, t2], [y_])
                p.act(t2[:], g_[:, sl], AF.Silu, [g_], [t2])
                p.tt("dve", o_[:, sl], y_[:, sl], t2[:], ALU.mult, [y_, t2], [o_])
            p.dma("pool", c.ybT[rs_, 0:T], o_[:], reads=[o_], writes=[c.ybT])


def phase_rwkv2(p, c, l):
    T = c.T
    G = 512
    NG = T // G
    IDb = c.cm_bf[0:64, 0:64]
    ONEC = c.cm[0:64, 384:385]
    NHALF = c.cm[0:64, 771:772]
    BLK = c.cm[0:64, 896:960]
    MASK4 = c.cm[0:64, 1024:1152]
    MLOW = c.cm[0:64, 1152:1216]
    I2 = c.cm[0:64, 1216:1280]

    def hcol(name, hd):
        o, w = CV[name]
        return c.cv[(hd % 2) * 64:(hd % 2) * 64 + 64, l * NV + o + hd // 2:l * NV + o + hd // 2 + 1]

    with p.scope():
        names = ["mu_r", "mu_k", "mu_v", "w0", "a0", "k_k", "k_a", "r_k"]
        hv = p.sb("hv", [64, len(names), 16], F32)
        for ni, nm in enumerate(names):
            o, w = CV[nm]
            src = c.cvec_d.h[:, l * NV + o:l * NV + o + 8]
            p.dma("sp", hv[:, ni, 0:8], src[0:64, :], reads=[c.cvec_d], writes=[hv])
            p.dma("sp", hv[:, ni, 8:16], src[64:128, :], reads=[c.cvec_d], writes=[hv])
        hidx = lambda hd: (hd % 2) * 8 + hd // 2
        hvc = lambda nm, hd: hv[:, names.index(nm), hidx(hd):hidx(hd) + 1]
        negw0 = p.sb("negw0", [64, 16], F32)
        p.ts("dve", negw0[:], hv[:, names.index("w0"), :], -1.0, ALU.mult, [hv], [negw0])
        muw = p.sb("muw", [64, 4], F32)
        o, w = CV["mu_wa"]
        p.dma("sp", muw[:, 0:2], c.cvec_d.h[0:64, l * NV + o:l * NV + o + 2], reads=[c.cvec_d], writes=[muw])
        p.dma("sp", muw[:, 2:4], c.cvec_d.h[64:128, l * NV + o:l * NV + o + 2], reads=[c.cvec_d], writes=[muw])
        w2s = p.sb("w2s", [64, 2048], F32)
        w2b = p.sb("w2b", [64, 2048], BF16)
        p.dma("sp", w2s[:, 0:1024], c.rw2.h[l], reads=[c.rw2], writes=[w2s])
        p.dma("sp", w2s[:, 1024:2048], c.ra2.h[l], reads=[c.ra2], writes=[w2s])
        p.copy("dve", w2b[:], w2s[:], [w2s], [w2b])
        mreset = p.sb("mreset", [64, G], F32)
        p.memset("dve", mreset[:], 1.0, writes=[mreset])
        p.memset("dve", mreset[:].rearrange("p (c j) -> p c j", j=64)[:, :, 0:1], 0.0, writes=[mreset])
        Sf = [p.sb("Sf%d" % j, [64, 64], F32) for j in range(16)]
        Sb = [p.sb("Sb%d" % j, [64, 64], BF16) for j in range(16)]
        for j in range(16):
            p.memset("pool", Sf[j][:], 0.0, writes=[Sf[j]])
            p.memset("pool", Sb[j][:], 0.0, writes=[Sb[j]])
        xwd = [p.sb("cxwd%d" % i, [64, G + 1], F32) for i in range(2)]
        xad = [p.sb("cxad%d" % i, [64, G + 1], F32) for i in range(2)]
        xin = {n: [p.sb("cx%s%d" % (n, i), [64, G + 1], F32) for i in range(2)] for n in "rkv"}
        mwd = p.sb("cmwd", [64, G], F32)
        mad = p.sb("cmad", [64, G], F32)
        wdb = p.sb("cwdb", [64, G], BF16)
        adb = p.sb("cadb", [64, G], BF16)
        mr = p.sb("cmr", [64, G], F32)
        mk = p.sb("cmk", [64, G], F32)
        mv = [p.sb("cmv%d" % i, [64, G], F32) for i in range(2)]
        lwn = p.sb("clwn", [64, G], F32)
        cs = p.sb("ccs", [64, G], F32)
        e1 = p.sb("ce1", [64, G], F32)
        e2 = p.sb("ce2", [64, G], F32)
        e3 = p.sb("ce3", [64, G], F32)
        e4 = p.sb("ce4", [64, G], F32)
        av = p.sb("cav", [64, G], F32)
        kkn = p.sb("ckkn", [64, G], F32)
        t3 = p.sb("ct3_", [64, G], F32)
        rko = [p.sb("crko%d" % i, [64, G], F32) for i in range(2)]
        gCs = [p.sb("cgC%d" % i, [64, 8], F32) for i in range(2)]
        ARs = [p.sb("cAR%d" % i, [64, 8, 128], BF16) for i in range(2)]
        BK = p.sb("cBK", [64, 8, 128], BF16)
        Bg = p.sb("cBg", [64, G], BF16)
        Kg = p.sb("cKg", [64, G], BF16)
        Vb = p.sb("cVb", [64, G], BF16)
        ABs = [p.sb("cAB%d" % i, [64, 8, 128], BF16) for i in range(2)]
        AKs = [p.sb("cAK%d" % i, [64, 8, 128], BF16) for i in range(2)]
        Pm = [p.sb("cP%d" % i, [64, 8, 64], BF16) for i in range(2)]
        Qm = [p.sb("cQ%d" % i, [64, 8, 64], BF16) for i in range(2)]
        TTm = [p.sb("cTT%d" % i, [64, 8, 64], BF16) for i in range(4)]
        BgTs = [p.sb("cBgT%d" % i, [64, 8, 64], BF16) for i in range(2)]
        KgTs = [p.sb("cKgT%d" % i, [64, 8, 64], BF16) for i in range(2)]
        Vts = [p.sb("cVt%d" % i, [64, 8, 64], BF16) for i in range(2)]
        TAts = [p.sb("cTAt%d" % i, [64, 8, 64], BF16) for i in range(2)]
        UVss = [p.sb("cUV%d" % i, [64, 8, 64], F32) for i in range(2)]
        AtT = p.sb("cAtT", [64, 8, 64], BF16)
        W2sb = p.sb("cW2sb", [64, 8, 64], BF16)
        Usb = p.sb("cUsb", [64, 64], BF16)
        Ysb = [p.sb("cYsb%d" % i, [64, G], F32) for i in range(2)]
        PB = p.ps("cPB", [64, 1024], F32)
        PA = p.ps("cPA", [64, 512], F32)
        PY = p.ps("cPY", [64, 512], F32)
        PI = [p.ps("cPI%d" % i, [64, 512], F32) for i in range(3)]
        PT = p.ps("cPT", [64, 1024], BF16)
        v3 = lambda a: a.rearrange("p (c j) -> p c j", j=64)
        f2 = lambda a: a.rearrange("p c m -> p (c m)")

        def load(x_, row, t0):
            if t0 == 0:
                p.memset("pool", x_[:, 0:1], 0.0, writes=[x_])
                p.dma("sp", x_[:, 1:G + 1], c.projT[row:row + 64, 0:G], reads=[c.projT], writes=[x_])
            else:
                p.dma("sp", x_[:, 0:G + 1], c.projT[row:row + 64, t0 - 1:t0 + G], reads=[c.projT], writes=[x_])

        def mix(x_, mu_ap, dst):
            p.tt("pool", dst[:], x_[:, 0:G], x_[:, 1:G + 1], ALU.subtract, [x_], [dst])
            p.stt(dst[:], dst[:], mu_ap, x_[:, 1:G + 1], ALU.mult, ALU.add, [dst, x_, hv, muw], [dst])

        def prep(it, g, hd):
            t0 = g * G
            st = it % 2
            AR, AB, AK, gC = ARs[st], ABs[st], AKs[st], gCs[st]
            BgT, KgT, Vt, TAt, UVs = BgTs[st], KgTs[st], Vts[st], TAts[st], UVss[st]
            if hd == 0:
                load(xwd[g % 2], B_WA, t0)
                load(xad[g % 2], B_WA + 64, t0)
                mix(xwd[g % 2], muw[:, 0:1], mwd)
                mix(xad[g % 2], muw[:, 2:3], mad)
                p.act(wdb[:], mwd[:], AF.Tanh, [mwd], [wdb])
                p.copy("dve", adb[:], mad[:], [mad], [adb])
                yield
            xr_, xk_, xv_ = [xin[n][it % 2] for n in "rkv"]
            mv_ = mv[it % 2]
            rko_ = rko[it % 2]
            load(xr_, B_R + hd * 64, t0)
            load(xk_, B_K + hd * 64, t0)
            load(xv_, B_V + hd * 64, t0)
            mix(xr_, hvc("mu_r", hd), mr)
            yield
            mix(xk_, hvc("mu_k", hd), mk)
            mix(xv_, hvc("mu_v", hd), mv_)
            yield
            p.dma("pool", c.vD[hd * 64:(hd + 1) * 64, t0:t0 + G], mv_[:], reads=[mv_], writes=[c.vD])
            p.copy("act", Vb[:], mv_[:], [mv_], [Vb])
            pw, pa, pk = PI
            p.mm(pw[:], w2b[:, hd * 64:(hd + 1) * 64], wdb[:], reads=[w2b, wdb], writes=[pw])
            p.mm(pa[:], w2b[:, 1024 + hd * 64:1024 + (hd + 1) * 64], adb[:], reads=[w2b, adb], writes=[pa])
            yield
            p.act(e1[:], pw[:], AF.Exp, [pw, negw0], [e1], bias=negw0[:, hidx(hd):hidx(hd) + 1], scale=-1.0)
            p.act(e1[:], e1[:], AF.Ln, [e1, c.cm], [e1], bias=ONEC)
            yield
            p.act(lwn[:], e1[:], AF.Exp, [e1, c.cm], [lwn], bias=NHALF, scale=-1.0)
            p.act(av[:], pa[:], AF.Sigmoid, [pa, hv], [av], bias=hvc("a0", hd))
            yield
            p.op("dve", lambda e: e.tensor_tensor_scan(out=cs[:], data0=mreset[:], data1=lwn[:], initial=0.0,
                                                       op0=ALU.mult, op1=ALU.add), [mreset, lwn], [cs])
            cs3 = v3(cs[:])
            p.act(e1[:], cs[:], AF.Exp, [cs], [e1])
            yield
            p.act(e2[:], cs[:], AF.Exp, [cs], [e2], scale=-1.0)
            p.tt("pool", e3[:], cs[:], lwn[:], ALU.subtract, [cs, lwn], [e3])
            yield
            p.act(e3[:], e3[:], AF.Exp, [e3], [e3], scale=-1.0)
            p.tt("pool", v3(e4[:]), cs3, cs3[:, :, 63:64].to_broadcast([64, 8, 64]), ALU.subtract, [cs], [e4])
            yield
            p.act(e4[:], e4[:], AF.Exp, [e4], [e4])
            p.act(gC[:], cs3[:, :, 63], AF.Exp, [cs], [gC], scale=-1.0)
            yield
            p.ts("dve", kkn[:], mk[:], hvc("k_k", hd), ALU.mult, [mk, hv], [kkn])
            p.tt("pool", t3[:], kkn[:], kkn[:], ALU.mult, [kkn], [t3])
            p.mm(pk[:], BLK, t3[:], reads=[c.cm, t3], writes=[pk])
            yield
            p.act(t3[:], pk[:], AF.Sqrt, [pk], [t3])
            p.ts("dve", t3[:], t3[:], 1e-12, ALU.max, [t3], [t3])
            yield
            p.op("dve", lambda e, a=t3[:]: e.reciprocal(out=a, in_=a), [t3], [t3])
            p.tt("dve", kkn[:], kkn[:], t3[:], ALU.mult, [kkn, t3], [kkn])
            yield
            p.ts("dve", t3[:], av[:], -1.0, ALU.add, [av, hv], [t3], s2=hvc("k_a", hd), op1=ALU.mult)
            p.ts("dve", t3[:], t3[:], 1.0, ALU.add, [t3], [t3])
            yield
            p.tt("dve", mk[:], mk[:], t3[:], ALU.mult, [mk, t3], [mk])
            p.tt("pool", av[:], av[:], kkn[:], ALU.mult, [av, kkn], [av])
            yield
            p.tt("pool", t3[:], mr[:], mk[:], ALU.mult, [mr, mk], [t3])
            p.ts("dve", t3[:], t3[:], hvc("r_k", hd), ALU.mult, [t3, hv], [t3])
            p.mm(pk[:], BLK, t3[:], reads=[c.cm, t3], writes=[pk])
            yield
            p.copy("act", rko_[:], pk[:], [pk], [rko_])
            p.dma("pool", c.rkD[hd * 64:(hd + 1) * 64, t0:t0 + G], rko_[:], reads=[rko_], writes=[c.rkD])
            p.stt(AR[:, :, 0:64], v3(kkn[:]), -1.0, v3(e3[:]), ALU.mult, ALU.mult, [kkn, e3], [AR])
            yield
            p.tt("pool", AR[:, :, 64:128], v3(mr[:]), v3(e2[:]), ALU.mult, [mr, e2], [AR])
            p.tt("dve", BK[:, :, 0:64], v3(av[:]), v3(e1[:]), ALU.mult, [av, e1], [BK])
            yield
            p.tt("pool", BK[:, :, 64:128], v3(mk[:]), v3(e1[:]), ALU.mult, [mk, e1], [BK])
            p.tt("dve", Bg[:], av[:], e4[:], ALU.mult, [av, e4], [Bg])
            p.tt("pool", Kg[:], mk[:], e4[:], ALU.mult, [mk, e4], [Kg])
            yield
            for ci in range(8):
                p.mm(PB[:, ci * 128:(ci + 1) * 128], BK[:, ci, 0:64], AR[:, ci, :], reads=[BK, AR], writes=[PB])
                if ci % 4 == 3:
                    yield
            p.tt("dve", AB[:], PB[:].rearrange("p (c m) -> p c m", m=128), MASK4.unsqueeze(1).to_broadcast([64, 8, 128]),
                 ALU.mult, [PB, c.cm], [AB])
            yield
            for ci in range(8):
                p.mm(PB[:, ci * 128:(ci + 1) * 128], BK[:, ci, 64:128], AR[:, ci, :], reads=[BK, AR], writes=[PB])
                if ci % 4 == 3:
                    yield
            p.tt("dve", AK[:], PB[:].rearrange("p (c m) -> p c m", m=128), MASK4.unsqueeze(1).to_broadcast([64, 8, 128]),
                 ALU.mult, [PB, c.cm], [AK])
            yield
            for ci in range(8):
                p.mm(PI[0][:, ci * 64:(ci + 1) * 64], AR[:, ci, 0:64], BK[:, ci, 0:64], reads=[AR, BK], writes=[PI[0]])
            yield
            Pc, Qc, Tc = Pm[0], Qm[0], TTm[2 * st]
            p.tt("dve", Pc[:], PI[0][:].rearrange("p (c m) -> p c m", m=64), MLOW.unsqueeze(1).to_broadcast([64, 8, 64]),
                 ALU.mult, [PI[0], c.cm], [Pc])
            p.copy("pool", Qc[:], AB[:, :, 0:64], [AB], [Qc])
            p.tt("pool", Tc[:], AB[:, :, 0:64], I2.unsqueeze(1).to_broadcast([64, 8, 64]), ALU.add, [AB, c.cm], [Tc])
            yield
            cur = 0
            for lev in range(5):
                Pn, Qn, Tn = Pm[1 - cur], Qm[1 - cur], TTm[2 * st + 1 - cur]
                Pc, Qc, Tc = Pm[cur], Qm[cur], TTm[2 * st + cur]
                for ci in range(8):
                    cc = slice(ci * 64, (ci + 1) * 64)
                    p.mm(PI[0][:, cc], Qc[:, ci, :], Pc[:, ci, :], reads=[Qc, Pc], writes=[PI[0]])
                    if ci % 4 == 3:
                        yield
                if lev < 4:
                    for ci in range(8):
                        cc = slice(ci * 64, (ci + 1) * 64)
                        p.mm(PI[1][:, cc], Pc[:, ci, :], Qc[:, ci, :], reads=[Qc, Pc], writes=[PI[1]])
                        if ci % 4 == 3:
                            yield
                p.copy("act", f2(Pn[:]), PI[0][:], [PI[0]], [Pn])
                if lev < 4:
                    p.copy("dve", f2(Qn[:]), PI[1][:], [PI[1]], [Qn])
                yield
                for ci in range(8):
                    cc = slice(ci * 64, (ci + 1) * 64)
                    p.mm(PI[2][:, cc], Pn[:, ci, :], Tc[:, ci, :], reads=[Pn, Tc], writes=[PI[2]])
                    if ci % 4 == 3:
                        yield
                p.tt("dve", f2(Tn[:]), f2(Tc[:]), PI[2][:], ALU.add, [Tc, PI[2]], [Tn])
                yield
                cur = 1 - cur
            TT = TTm[2 * st + cur]
            assert cur == 1
            for (src_, dst_, w0_) in ((Bg, BgT, None), (Kg, KgT, None), (Vb, Vt, None), (AR, AtT, 0)):
                for ci in range(8):
                    in_ = src_[:, ci * 64:(ci + 1) * 64] if w0_ is None else src_[:, ci, 0:64]
                    p.tr(PT[:, ci * 64:(ci + 1) * 64], in_, IDb, [src_, c.cm_bf], [PT])
                    if ci % 4 == 3:
                        yield
                p.copy("act", f2(dst_[:]), PT[:, 0:512], [PT], [dst_])
                yield
            for ci in range(8):
                p.mm(PI[0][:, ci * 64:(ci + 1) * 64], AtT[:, ci, :], TT[:, ci, :], reads=[AtT, TT], writes=[PI[0]])
                if ci % 4 == 3:
                    yield
            p.copy("act", f2(TAt[:]), PI[0][:], [PI[0]], [TAt])
            for ci in range(8):
                p.mm(PI[1][:, ci * 64:(ci + 1) * 64], AK[:, ci, 0:64], Vt[:, ci, :], reads=[AK, Vt], writes=[PI[1]])
                if ci % 4 == 3:
                    yield
            p.copy("dve", f2(W2sb[:]), PI[1][:], [PI[1]], [W2sb])
            yield
            for ci in range(8):
                p.mm(PI[2][:, ci * 64:(ci + 1) * 64], TT[:, ci, :], W2sb[:, ci, :], reads=[TT, W2sb], writes=[PI[2]])
                if ci % 4 == 3:
                    yield
            p.copy("act", f2(UVs[:]), PI[2][:], [PI[2]], [UVs])
            yield

        def seq(it, g, hd):
            t0 = g * G
            st = it % 2
            AR, AB, AK, gC = ARs[st], ABs[st], AKs[st], gCs[st]
            BgT, KgT, Vt, TAt, UVs = BgTs[st], KgTs[st], Vts[st], TAts[st], UVss[st]
            S_f, S_b = Sf[hd], Sb[hd]
            ys_ = Ysb[it % 2]
            for ci in range(8):
                yc = slice(ci * 64, (ci + 1) * 64)
                p.mm(PA[:, 0:64], TAt[:, ci, :], S_b[:], reads=[TAt, S_b], writes=[PA])
                p.mm(PY[:, yc], S_b[:], AR[:, ci, 64:128], start=True, stop=False, reads=[S_b, AR], writes=[PY])
                p.mm(PY[:, yc], Vt[:, ci, :], AK[:, ci, 64:128], start=False, stop=False, reads=[Vt, AK], writes=[PY])
                p.mm(PA[:, 128:192], KgT[:, ci, :], Vt[:, ci, :], start=True, stop=False, reads=[KgT, Vt], writes=[PA])
                yield
                p.tt("dve", Usb[:], PA[:, 0:64], UVs[:, ci, :], ALU.add, [PA, UVs], [Usb])
                yield
                p.mm(PA[:, 128:192], BgT[:, ci, :], Usb[:], start=False, stop=True, reads=[BgT, Usb], writes=[PA])
                p.mm(PY[:, yc], Usb[:], AB[:, ci, 64:128], start=False, stop=True, reads=[Usb, AB], writes=[PY])
                yield
                p.stt(S_f[:], S_f[:], gC[:, ci:ci + 1], PA[:, 128:192], ALU.mult, ALU.add, [S_f, gC, PA], [S_f])
                yield
                p.copy("act", S_b[:], S_f[:], [S_f], [S_b])
                yield
            p.copy("dve", ys_[:], PY[:, 0:512], [PY], [ys_])
            p.dma("pool", c.yD[hd * 64:(hd + 1) * 64, t0:t0 + G], ys_[:], reads=[ys_], writes=[c.yD])
            yield

        work = [(g, hd) for g in range(NG) for hd in range(16)]
        for _ in prep(0, *work[0]):
            pass
        for it, (g, hd) in enumerate(work):
            gs = seq(it, g, hd)
            gp = prep(it + 1, *work[it + 1]) if it + 1 < len(work) else iter(())
            alive_s, alive_p = True, True
            while alive_s or alive_p:
                if alive_s:
                    try:
                        next(gs)
                    except StopIteration:
                        alive_s = False
                for _ in range(2):
                    if alive_p:
                        try:
                            next(gp)
                        except StopIteration:
                            alive_p = False
    _rwkv_b3(p, c, l)


def _rwkv_b3(p, c, l):
    T = c.T
    BLK = c.cm[:, 896:1024]
    with p.scope():
        yt = [p.sb("cy%d" % i, [128, T], F32) for i in range(2)]
        vt = [p.sb("cv_%d" % i, [128, T], F32) for i in range(2)]
        rkt = [p.sb("crk%d" % i, [128, T], F32) for i in range(2)]
        gt = [p.sb("cg%d" % i, [128, T], F32) for i in range(2)]
        ob = [p.sb("cob%d" % i, [128, T], BF16) for i in range(2)]
        t2 = p.sb("ct2", [128, 512], F32)
        t3 = p.sb("ct3", [128, 512], F32)
        PS = [p.ps("cps%d" % i, [128, 512], F32) for i in range(4)]
        for j in range(8):
            y_, v_, rk_, g_, o_ = yt[j % 2], vt[j % 2], rkt[j % 2], gt[j % 2], ob[j % 2]
            rs_ = slice(j * 128, (j + 1) * 128)
            p.dma("sp", y_[:], c.yD[rs_, 0:T], reads=[c.yD], writes=[y_])
            p.dma("sp", v_[:], c.vD[rs_, 0:T], reads=[c.vD], writes=[v_])
            p.dma("sp", rk_[:], c.rkD[rs_, 0:T], reads=[c.rkD], writes=[rk_])
            p.dma("sp", g_[:], c.projT[B_G + j * 128:B_G + (j + 1) * 128, 0:T], reads=[c.projT], writes=[g_])
            for q in range(T // 512):
                sl = slice(q * 512, (q + 1) * 512)
                pm, pv = PS[(2 * q) % 4], PS[(2 * q + 1) % 4]
                p.mm(pm[:], BLK, y_[:, sl], reads=[c.cm, y_], writes=[pm])
                p.stt(y_[:, sl], pm[:], -1.0 / 64, y_[:, sl], ALU.mult, ALU.add, [pm, y_], [y_])
                p.tt("pool", t2[:], y_[:, sl], y_[:, sl], ALU.mult, [y_], [t2])
                p.mm(pv[:], BLK, t2[:], reads=[c.cm, t2], writes=[pv])
                p.ts("dve", t3[:], pv[:], 1.0 / 64, ALU.mult, [pv], [t3], s2=64e-5, op1=ALU.add)
                p.act(t3[:], t3[:], AF.Sqrt, [t3], [t3])
                p.op("dve", lambda e, a=t3[:]: e.reciprocal(out=a, in_=a), [t3], [t3])
                p.tt("dve", y_[:, sl], y_[:, sl], t3[:], ALU.mult, [y_, t3], [y_])
                p.ts("dve", y_[:, sl], y_[:, sl], cvcol(c, l, "ln_w", j), ALU.mult, [y_, c.cv], [y_],
                     s2=cvcol(c, l, "ln_b", j), op1=ALU.add)
                p.tt("pool", t2[:], rk_[:, sl], v_[:, sl], ALU.mult, [rk_, v_], [t2])
                p.tt("dve", y_[:, sl], y_[:, sl], t2[:], ALU.add, [y_, t2], [y_])
                p.act(t2[:], g_[:, sl], AF.Silu, [g_], [t2])
                p.tt("dve", o_[:, sl], y_[:, sl], t2[:], ALU.mult, [y_, t2], [o_])
            p.dma("pool", c.ybT[rs_, 0:T], o_[:], reads=[o_], writes=[c.ybT])


def pad_w_in(w):
    o = np.zeros((2048, NCP), np.float32)
    o[:, 0:3088] = w[:, 0:3088]
    b = 3088
    o[:, B_R:B_R + 1024] = w[:, b:b + 1024]
    o[:, B_WA:B_WA + 64] = w[:, b + 1024:b + 1088]
    o[:, B_K:B_K + 1024] = w[:, b + 1088:b + 2112]
    o[:, B_V:B_V + 1024] = w[:, b + 2112:b + 3136]
    o[:, B_WA + 64:B_WA + 128] = w[:, b + 3136:b + 3200]
    o[:, B_G:B_G + 1024] = w[:, b + 3200:b + 4224]
    cc = 7312
    o[:, C_Q:C_Q + 1024] = w[:, cc:cc + 1024]
    o[:, C_K:C_K + 256] = w[:, cc + 1024:cc + 1280]
    o[:, C_V:C_V + 256] = w[:, cc + 1280:cc + 1536]
    o[:, C_G:C_G + 1024] = w[:, cc + 1536:cc + 2560]
    o[:, C_IQ:C_IQ + 1024] = w[:, cc + 2560:cc + 3584]
    o[:, C_IK:C_IK + 80] = w[:, cc + 3584:cc + 3664]
    o[:, G0:G0 + 6144] = w[:, 10976:17120]
    return o

def col(v):
    v = np.asarray(v, np.float32).reshape(-1)
    n = (v.size + 127) // 128
    o = np.zeros((n * 128,), np.float32)
    o[:v.size] = v
    return o.reshape(n, 128).T

def pack_cvec(inp, L):
    cv = np.zeros((128, L * NV), np.float32)
    def put(l, name, arr):
        o, w = CV[name]
        assert arr.shape == (128, w), (name, arr.shape, w)
        cv[:, l * NV + o:l * NV + o + w] = arr
    for l in range(L):
        put(l, "pre_g", col(inp["pre_norm"][l]))
        put(l, "post_g", col(inp["post_norm"][l]))
        put(l, "bg", col(inp["b_gate"][l]))
        put(l, "conv_w", np.concatenate([col(inp["ssm_conv_w"][l][j]) for j in range(4)], axis=1))
        put(l, "conv_b", col(inp["ssm_conv_b"][l]))
        put(l, "dt_bias", col(inp["ssm_dt_bias"][l]))
        put(l, "a_log", col(inp["ssm_a_log"][l]))
        put(l, "d_exp", col(np.repeat(inp["ssm_d"][l], 64)))
        put(l, "ssm_norm", col(inp["ssm_norm"][l]))
        mu = inp["rwkv_mu"][l]
        put(l, "mu_r", col(mu[0:1024]))
        put(l, "mu_k", col(mu[1088:2112]))
        put(l, "mu_v", col(mu[2112:3136]))
        put(l, "mu_wa", col(np.concatenate([mu[1024:1088], mu[3136:3200]])))
        for n, k in [("w0", "rwkv_w0"), ("a0", "rwkv_a0"), ("k_k", "rwkv_k_k"), ("k_a", "rwkv_k_a"),
                     ("ln_w", "rwkv_ln_w"), ("ln_b", "rwkv_ln_b"), ("r_k", "rwkv_r_k")]:
            put(l, n, col(inp[k][l].reshape(-1)))
        put(l, "ikn_w", col(inp["idx_k_norm_w"][l]))
        put(l, "ikn_b", col(inp["idx_k_norm_b"][l]))
    return cv

def const_mats():
    k = np.arange(128)
    ident = np.eye(128, dtype=np.float32)
    triu = (k[:, None] <= k[None, :]).astype(np.float32)
    su = (k[:, None] > k[None, :]).astype(np.float32)
    ones = np.ones((128, 128), np.float32)
    z = np.zeros((128, 128), np.float32)
    perm = np.zeros((128, 128), np.float32)
    for m in range(128):
        perm[(m // 64) * 64 + ((m % 64) + 32) % 64, m] = 1.0
    negm = np.where(k[None, :] > k[:, None], -1e30, 0.0).astype(np.float32)
    misc = np.zeros((128, 128), np.float32)
    inv = 10000.0 ** (-(np.arange(32, dtype=np.float32) * 2.0 / 64))
    misc[:, 0] = inv[k % 32]
    misc[:, 1] = np.where((k % 64) < 32, -1.0, 1.0)
    misc[:, 2] = np.pi / 2
    misc[:, 3] = -0.5
    misc[:, 4] = 64e-5
    blk = (k[:, None] // 64 == k[None, :] // 64).astype(np.float32)
    s_ = (k % 64)[:, None]
    t_ = (k % 64)[None, :]
    mask4 = np.where(k[None, :] < 64, s_ < t_, s_ <= t_).astype(np.float32)
    m9 = np.zeros((128, 128), np.float32)
    k64 = np.arange(64)
    m9[:, 0:64] = ((k % 64)[:, None] > k64[None, :]).astype(np.float32)
    m9[:, 64:128] = ((k % 64)[:, None] == k64[None, :]).astype(np.float32)
    m10 = np.zeros((128, 128), np.float32)
    m10[:, 0:32] = (0.5 ** np.arange(1, 33, dtype=np.float64)).astype(np.float32)[None, :]
    return np.concatenate([ident, triu, su, ones, perm, negm, misc, blk, mask4, m9, m10], axis=1)


_NC_CACHE = {}


def build_program(T, L):
    nc = bass.Bass("TRN2", target_bir_lowering=False)
    es = ExitStack()
    with es:
        p = Prog(nc, es, nsem=100)
        c = Ctx()
        c.T = T
        c.xT0 = p.dram("xT0", [2048, T], F32, kind="ExternalInput")
        c.w_in_l = [p.dram("w_in%d" % l, [2048, NCP], F32, kind="ExternalInput") for l in range(L)]
        c.w_ba = p.dram("w_ba", [L, 1024, 2048], F32, kind="ExternalInput")
        c.w_bb = p.dram("w_bb", [L, 1024, 2048], F32, kind="ExternalInput")
        c.w_bc = p.dram("w_bc", [L, 1024, 2048], F32, kind="ExternalInput")
        c.w_out = p.dram("w_out", [L, 2048, 2048], F32, kind="ExternalInput")
        c.cvec_d = p.dram("cvec", [128, L * NV], F32, kind="ExternalInput")
        c.cmat_d = p.dram("cmat", [128, 1408], F32, kind="ExternalInput")
        c.pos_d = p.dram("pos", [128, T], I32, kind="ExternalInput")
        c.rw2 = p.dram("rw2", [L, 64, 1024], F32, kind="ExternalInput")
        c.ra2 = p.dram("ra2", [L, 64, 1024], F32, kind="ExternalInput")
        c.xo = p.dram("xo", [2048, T], F32, kind="ExternalOutput")
        xA = p.dram("xA", [2048, T], F32)
        xB = p.dram("xB", [2048, T], F32)
        c.projT = p.dram("projT", [G0, T], F32)
        c.projG = p.dram("projG", [NCP - G0, T], F32)
        c.xbcT = p.dram("xbcT", [2048, T], BF16)
        c.dsaT = p.dram("dsaT", [2432, T], BF16)
        c.yaT = p.dram("yaT", [1024, T], BF16)
        c.ybT = p.dram("ybT", [1024, T], BF16)
        c.ycT = p.dram("ycT", [1024, T], BF16)
        c.vD = p.dram("vD", [1024, T], F32)
        c.rkD = p.dram("rkD", [1024, T], F32)
        c.yD = p.dram("yD", [1024, T], F32)
        alloc_consts(p, c, L)
        p.barrier()
        xs = [c.xT0] + [xA if (l % 2 == 0) else xB for l in range(L - 1)] + [c.xo]
        for l in range(L):
            phase_p1(p, c, l, xs[l])
            phase_ssd(p, c, l)
            phase_rwkv2(p, c, l)
            phase_dsa(p, c, l)
            phase_p3(p, c, l, xs[l], xs[l + 1])
        p.barrier()
    return nc


def kernel(**inputs):
    inp = {k: np.asarray(v) for k, v in inputs.items()}
    x = inp["x"].astype(np.float32, copy=False)
    Bsz, T, _ = x.shape
    L = inp["w_in"].shape[0]
    key = (T, L)
    if key not in _NC_CACHE:
        _NC_CACHE[key] = build_program(T, L)
    nc = _NC_CACHE[key]
    cmat = const_mats()
    cvec = pack_cvec(inp, L)
    w_in_p = [pad_w_in(inp["w_in"][l]) for l in range(L)]
    maps = []
    for b in range(Bsz):
        m = {"xT0": np.ascontiguousarray(x[b].T), "w_ba": inp["w_branch_a"], "w_bb": inp["w_branch_b"],
             "w_bc": inp["w_branch_c"], "w_out": inp["w_out"], "cvec": cvec, "cmat": cmat,
             "pos": np.ascontiguousarray(np.broadcast_to(inp["positions"][b][None, :], (128, T))).astype(np.int32),
             "rw2": inp["rwkv_w2"], "ra2": inp["rwkv_a2"]}
        for l in range(L):
            m["w_in%d" % l] = w_in_p[l]
        maps.append(m)
    res = run_bass_kernel_spmd(nc, maps, core_ids=list(range(Bsz)))
    return np.stack([np.asarray(res.results[b]["xo"]).T for b in range(Bsz)]).astype(np.float32)
```

```python
import numpy as np
from contextlib import ExitStack
import concourse.bass as bass
import concourse.mybir as mybir
from concourse.bass_utils import run_bass_kernel_spmd

F32 = mybir.dt.float32
BF16 = mybir.dt.bfloat16
I32 = mybir.dt.int32
AF = mybir.ActivationFunctionType
ALU = mybir.AluOpType
AX = mybir.AxisListType

EPOCH = 24000
ENGS = ("pe", "dve", "act", "pool", "sp")


class Tl:
    def __init__(self, h, name):
        self.h = h
        self.name = name
        self.dram = False
        self.psum = False
        self.acc = {}
        self.w = None
        self.r = {}

    def __getitem__(self, k):
        return self.h[k]


class Prog:
    def __init__(self, nc, es, nsem=120, same_engine_sync=True):
        self.nc = nc
        self.es = es
        self.q = {e: [] for e in ENGS}
        self.n = {e: 0 for e in ENGS}
        self.waited = {e: {} for e in ENGS}
        self.lane_n = {}
        self.sem_pool = [es.enter_context(nc.semaphore("s%d" % i)) for i in range(nsem)]
        self.sem_map = {}
        self.sem_next = 0
        self.lane_sem = {}
        self.retired = []
        self.same = same_engine_sync
        self.ntile = 0

    def sb(self, name, shape, dt):
        self.ntile += 1
        h = self.es.enter_context(self.nc.sbuf_tensor("%s_%d" % (name, self.ntile), list(shape), dt))
        return Tl(h, name)

    def ps(self, name, shape, dt=F32):
        self.ntile += 1
        h = self.es.enter_context(self.nc.psum_tensor("%s_%d" % (name, self.ntile), list(shape), dt))
        t = Tl(h, name)
        t.psum = True
        return t

    def dram(self, name, shape, dt, kind="Internal"):
        h = self.nc.dram_tensor(name, list(shape), dt, kind=kind)
        t = Tl(h, name)
        t.dram = True
        return t

    def _fresh(self):
        if self.sem_next < len(self.sem_pool):
            self.sem_next += 1
            return self.sem_pool[self.sem_next - 1], 0
        self.retired.sort(key=lambda x: x[0])
        r, sem = self.retired.pop(0)
        return sem, r

    def _sem(self, src, val):
        if src in ENGS:
            ep = (val - 1) // EPOCH if val > 0 else 0
            key = (src, ep)
            if key not in self.sem_map:
                self.sem_map[key] = self._fresh()
            sem, r0 = self.sem_map[key]
            return sem, val - ep * EPOCH + r0
        if src not in self.lane_sem:
            sem, r = self._fresh()
            self.lane_sem[src] = (sem, r - (val - 16))
        sem, delta = self.lane_sem[src]
        assert 0 < val + delta < 30000, (src, val, delta)
        return sem, val + delta

    def retire_lanes(self):
        for lane, (sem, delta) in self.lane_sem.items():
            self.retired.append((self.lane_n[lane] + delta, sem))
        self.lane_sem = {}

    def _collect(self, eng, reads, writes):
        deps = {}

        def add(d):
            if d is None:
                return
            s, v = d
            if deps.get(s, 0) < v:
                deps[s] = v

        reads = [t for t in reads if not t.dram]
        writes = [t for t in writes if not t.dram]
        for t in list(reads) + list(writes):
            if t.psum:
                for s_, v_ in t.acc.items():
                    if s_ != eng:
                        add((s_, v_))
        for t in reads:
            add(t.w)
        for t in writes:
            add(t.w)
            for s, v in t.r.items():
                add((s, v))
        waits = []
        for s, v in deps.items():
            if s == eng:
                if eng == "pe" or not self.same:
                    continue
            if self.waited[eng].get(s, 0) >= v:
                continue
            self.waited[eng][s] = v
            waits.append((s, v))
        return waits

    def op(self, eng, fn, reads=(), writes=()):
        waits = self._collect(eng, reads, writes)
        self.n[eng] += 1
        me = (eng, self.n[eng])
        self._emit_now(eng, waits, fn, me)
        reads = [t for t in reads if not t.dram]
        writes = [t for t in writes if not t.dram]
        for t in list(reads) + list(writes):
            if t.psum:
                t.acc[eng] = me[1]
        for t in reads:
            if t.r.get(eng, 0) < me[1]:
                t.r[eng] = me[1]
        for t in writes:
            t.w = me
            t.r = {}

    def dma(self, qeng, out_ap, in_ap, reads=(), writes=(), lane=None):
        reads = [t for t in reads if not t.dram]
        writes = [t for t in writes if not t.dram]
        if lane is None:
            lane = "L_" + (writes[0].name if writes else reads[0].name)
        waits = self._collect(qeng, reads, writes)
        k = self.lane_n.get(lane, 0) + 16
        self.lane_n[lane] = k
        me = (lane, k)
        self._emit_now(qeng, waits, lambda e: e.dma_start(out=out_ap, in_=in_ap), me)
        for t in reads:
            if t.r.get(lane, 0) < k:
                t.r[lane] = k
        for t in writes:
            t.w = me
            t.r = {}

    def barrier(self):
        for e in ENGS:
            waits = []
            for s in list(ENGS) + list(self.lane_n):
                v = self.n[s] if s in ENGS else self.lane_n[s]
                if v == 0 or self.waited[e].get(s, 0) >= v:
                    continue
                self.waited[e][s] = v
                waits.append((s, v))
            self._emit_now(e, waits, None, None)

    def scope(self):
        prog = self

        class _S:
            def __enter__(s):
                s.old = prog.es
                s.st = ExitStack()
                prog.es = s.st
                return s

            def __exit__(s, *a):
                prog.barrier()
                prog.retire_lanes()
                s.st.close()
                prog.es = s.old
                return False
        return _S()

    def finish(self, out_tiles, eng="sp"):
        waits = self._collect(eng, out_tiles, ())
        self._emit_now(eng, waits, None, None)

    def _emit_now(self, eng, waits, fn, me):
        e = {"pe": self.nc.tensor, "dve": self.nc.vector, "act": self.nc.scalar, "pool": self.nc.gpsimd, "sp": self.nc.sync}[eng]
        for s, v in waits:
            sem, sv = self._sem(s, v)
            e.wait_ge(sem, sv)
        if fn is None:
            return
        ins = fn(e)
        src, val = me
        sem, sv = self._sem(src, val)
        ins.then_inc(sem, 1 if src in ENGS else 16)

    def _replay(self, eng, e):
        for waits, fn, me in self.q[eng]:
            for s, v in waits:
                sem, sv = self._sem(s, v)
                e.wait_ge(sem, sv)
            if fn is None:
                continue
            ins = fn(e)
            src, val = me
            sem, sv = self._sem(src, val)
            if src in ENGS:
                ins.then_inc(sem, 1)
            else:
                ins.then_inc(sem, 16)

    def emit(self):
        return
        nc = self.nc
        with nc.Block() as block:
            @block.sync
            def _(e):
                self._replay("sp", e)

            @block.tensor
            def _(e):
                self._replay("pe", e)

            @block.vector
            def _(e):
                self._replay("dve", e)

            @block.scalar
            def _(e):
                self._replay("act", e)

            @block.gpsimd
            def _(e):
                self._replay("pool", e)


def _mm(self, out, lhsT, rhs, start=True, stop=True, reads=(), writes=()):
    self.op("pe", lambda e: e.matmul(out, lhsT, rhs, start=start, stop=stop), reads, writes)

def _tr(self, out, in_, ident, reads=(), writes=()):
    self.op("pe", lambda e: e.transpose(out, in_, ident), reads, writes)

def _act(self, out, in_, func, reads=(), writes=(), bias=None, scale=1.0):
    if bias is None:
        self.op("act", lambda e: e.activation(out=out, in_=in_, func=func, scale=scale), reads, writes)
    else:
        self.op("act", lambda e: e.activation(out=out, in_=in_, func=func, bias=bias, scale=scale), reads, writes)

def _tt(self, eng, out, in0, in1, op, reads=(), writes=()):
    self.op(eng, lambda e: e.tensor_tensor(out=out, in0=in0, in1=in1, op=op), reads, writes)

def _ts(self, eng, out, in0, s1, op0, reads=(), writes=(), s2=None, op1=None):
    if op1 is None:
        self.op(eng, lambda e: e.tensor_scalar(out=out, in0=in0, scalar1=s1, scalar2=None, op0=op0), reads, writes)
    else:
        self.op(eng, lambda e: e.tensor_scalar(out=out, in0=in0, scalar1=s1, scalar2=s2, op0=op0, op1=op1), reads, writes)

def _stt(self, out, in0, scalar, in1, op0, op1, reads=(), writes=()):
    self.op("dve", lambda e: e.scalar_tensor_tensor(out=out, in0=in0, scalar=scalar, in1=in1, op0=op0, op1=op1), reads, writes)

def _copy(self, eng, out, in_, reads=(), writes=()):
    if eng == "act":
        self.op("act", lambda e: e.copy(out=out, in_=in_), reads, writes)
    else:
        self.op(eng, lambda e: e.tensor_copy(out=out, in_=in_), reads, writes)

def _memset(self, eng, ap, val, writes=()):
    self.op(eng, lambda e: e.memset(ap, val), (), writes)

Prog.mm = _mm
Prog.tr = _tr
Prog.act = _act
Prog.tt = _tt
Prog.ts = _ts
Prog.stt = _stt
Prog.copy = _copy
Prog.memset = _memset


D = 2048
NCP = 17280
A_Z, A_XBC, A_DT = 0, 1024, 3072
B0 = 3200
B_R, B_K, B_V, B_G, B_WA = B0, B0 + 1024, B0 + 2048, B0 + 3072, B0 + 4096
C0 = B0 + 4224
C_Q, C_K, C_V, C_G, C_IQ, C_IK = C0, C0 + 1024, C0 + 1280, C0 + 1536, C0 + 2560, C0 + 3584
G0 = C0 + 3712
assert G0 + 6144 == NCP
EPS = 1e-6

CV = {}
_o = 0
for _n, _w in [("pre_g", 16), ("post_g", 16), ("bg", 48), ("conv_w", 64), ("conv_b", 16), ("dt_bias", 1),
               ("a_log", 1), ("d_exp", 8), ("ssm_norm", 8), ("mu_r", 8), ("mu_k", 8), ("mu_v", 8), ("mu_wa", 1),
               ("w0", 8), ("a0", 8), ("k_k", 8), ("k_a", 8), ("ln_w", 8), ("ln_b", 8), ("r_k", 8),
               ("ikn_w", 1), ("ikn_b", 1)]:
    CV[_n] = (_o, _w)
    _o += _w
NV = _o


class Ctx:
    pass


def cvcol(c, l, name, j=0, rows=128):
    o, w = CV[name]
    return c.cv[0:rows, l * NV + o + j: l * NV + o + j + 1]


def alloc_consts(p, c, L):
    c.cv = p.sb("cv", [128, L * NV], F32)
    p.dma("sp", c.cv[:], c.cvec_d[:], reads=[c.cvec_d], writes=[c.cv])
    c.ones_bf = p.sb("ones_bf", [128, 128], BF16)
    p.memset("dve", c.ones_bf[:], 1.0, writes=[c.ones_bf])
    c.cm = p.sb("cm", [128, 11 * 128], F32)
    p.dma("sp", c.cm[:], c.cmat_d[:], reads=[c.cmat_d], writes=[c.cm])
    c.cm_bf = p.sb("cm_bf", [128, 11 * 128], BF16)
    p.copy("dve", c.cm_bf[:], c.cm[:], [c.cm], [c.cm_bf])


def rstd_from_ss(p, rs, ss, n, width):
    p.ts("dve", rs[:, 0:width], ss[:, 0:width], 1.0 / n, ALU.mult, [ss], [rs], s2=EPS, op1=ALU.add)
    p.act(rs[:, 0:width], rs[:, 0:width], AF.Sqrt, [rs], [rs])
    p.op("dve", lambda e: e.reciprocal(out=rs[:, 0:width], in_=rs[:, 0:width]), [rs], [rs])


def phase_p1(p, c, l, xin):
    T = c.T
    TT = min(T, 2048)
    NJ = TT // 512
    with p.scope():
        xg = p.sb("xg", [128, 16, TT], BF16)
        xst = [p.sb("xst%d" % i, [128, TT], F32) for i in range(2)]
        sq = [p.sb("sq%d" % i, [128, TT], BF16) for i in range(2)]
        rs = p.sb("rs", [128, TT], F32)
        wst = [p.sb("wst%d" % i, [128, 16, 256], F32) for i in range(2)]
        wbf = [p.sb("wbf%d" % i, [128, 16, 256], BF16) for i in range(2)]
        osb = [p.sb("osb%d" % i, [128, TT], F32) for i in range(2)]
        ss = p.ps("ss", [128, TT], F32)
        acc = [p.ps("acc%d" % i, [128, 512], F32) for i in range(8 - NJ)]
        for tt in range(T // TT):
            t0 = tt * TT
            for k in range(16):
                xs = xst[k % 2]
                p.dma("sp", xs[:], xin[k * 128:(k + 1) * 128, t0:t0 + TT], reads=[xin], writes=[xs])
                p.ts("dve", xg[:, k, :], xs[:], cvcol(c, l, "pre_g", k), ALU.mult, [xs, c.cv], [xg])
                p.tt("pool", sq[k % 2][:], xs[:], xs[:], ALU.mult, [xs], [sq[k % 2]])
                for j in range(NJ):
                    p.mm(ss[:, j * 512:(j + 1) * 512], c.ones_bf[:], sq[k % 2][:, j * 512:(j + 1) * 512],
                         start=(k == 0), stop=(k == 15), reads=[c.ones_bf, sq[k % 2]], writes=[ss])
            rstd_from_ss(p, rs, ss, D, TT)
            ng = (NCP + 255) // 256
            ai = 0
            oi = 0
            wv = c.w_in_l[l].h.rearrange("(k p) c -> p k c", p=128)
            for g in range(ng):
                c0 = g * 256
                cw = min(256, NCP - c0)
                ws, wb = wst[g % 2], wbf[g % 2]
                p.dma("sp", ws[:, :, 0:cw], wv[:, :, c0:c0 + cw], reads=[c.w_in_l[l]], writes=[ws])
                p.copy("pool" if g % 2 == 0 else "act", wb[:, :, 0:cw], ws[:, :, 0:cw], [ws], [wb])
                for ct in range(cw // 128):
                    ob = osb[oi % 2]
                    oi += 1
                    for j in range(NJ):
                        a = acc[ai % len(acc)]
                        ai += 1
                        for k in range(16):
                            p.mm(a[:], wb[:, k, ct * 128:(ct + 1) * 128], xg[:, k, j * 512:(j + 1) * 512],
                                 start=(k == 0), stop=(k == 15), reads=[wb, xg], writes=[a])
                        p.tt("dve", ob[:, j * 512:(j + 1) * 512], a[:], rs[:, j * 512:(j + 1) * 512], ALU.mult,
                             [a, rs], [ob])
                    r0 = c0 + ct * 128
                    if r0 < G0:
                        p.dma("pool", c.projT[r0:r0 + 128, t0:t0 + TT], ob[:], reads=[ob], writes=[c.projT])
                    else:
                        p.dma("pool", c.projG[r0 - G0:r0 - G0 + 128, t0:t0 + TT], ob[:], reads=[ob], writes=[c.projG])


def phase_p3(p, c, l, xin, xout):
    T = c.T
    TT = 512
    wbr = [c.w_ba, c.w_bb, c.w_bc]
    ybr = [c.yaT, c.ybT, c.ycT]
    with p.scope():
        ysb = [p.sb("ysb%d" % i, [128, 8, TT], BF16) for i in range(3)]
        mg = p.sb("mg", [128, 16, TT], BF16)
        osb = p.sb("o3", [128, 16, TT], F32)
        wst = [p.sb("w3st%d" % i, [128, 8, 128], F32) for i in range(4)]
        wb = [p.sb("w3bf%d" % i, [128, 8, 128], BF16) for i in range(4)]
        wst2 = [p.sb("wost%d" % i, [128, 16, 128], F32) for i in range(3)]
        wb2 = [p.sb("wobf%d" % i, [128, 16, 128], BF16) for i in range(3)]
        pg = [p.sb("pg%d" % i, [128, TT], F32) for i in range(2)]
        gt = [p.sb("gt%d" % i, [128, TT], F32) for i in range(2)]
        macc = p.sb("macc", [128, TT], F32)
        tmp = p.sb("tmp3", [128, TT], F32)
        sq = [p.sb("sq3%d" % i, [128, TT], BF16) for i in range(2)]
        rs = p.sb("rs3", [128, TT], F32)
        xt = [p.sb("xt3%d" % i, [128, TT], F32) for i in range(2)]
        xo = [p.sb("xo3%d" % i, [128, TT], F32) for i in range(2)]
        acc = [p.ps("acc3%d" % i, [128, 512], F32) for i in range(4)]
        ss = p.ps("ss3", [128, 512], F32)
        ai = 0
        wi = 0
        import os
        STOP = int(os.environ.get("P3STOP", "99"))
        for tt in range(T // TT):
            t0 = tt * TT
            for i in range(3):
                yv = ybr[i].h.rearrange("(k p) t -> p k t", p=128)
                p.dma("sp", ysb[i][:], yv[:, :, t0:t0 + TT], reads=[ybr[i]], writes=[ysb[i]])
            for ct in range(16):
                for i in range(3):
                    ws, wbb = wst[wi % 4], wb[wi % 4]
                    pgt, gtt = pg[wi % 2], gt[wi % 2]
                    wi += 1
                    wv = wbr[i].h[l].rearrange("(k p) c -> p k c", p=128)
                    p.dma("sp", ws[:], wv[:, :, ct * 128:(ct + 1) * 128], reads=[wbr[i]], writes=[ws])
                    p.copy("act", wbb[:], ws[:], [ws], [wbb])
                    a = acc[ai % 4]
                    ai += 1
                    for k in range(8):
                        p.mm(a[:], wbb[:, k, :], ysb[i][:, k, :], start=(k == 0), stop=(k == 7),
                             reads=[wbb, ysb[i]], writes=[a])
                    r0 = i * 2048 + ct * 128
                    p.dma("sp", pgt[:], c.projG[r0:r0 + 128, t0:t0 + TT], reads=[c.projG], writes=[pgt])
                    p.act(gtt[:], pgt[:], AF.Sigmoid, [pgt, c.cv], [gtt], bias=cvcol(c, l, "bg", i * 16 + ct))
                    if i == 0:
                        p.tt("dve", macc[:], a[:], gtt[:], ALU.mult, [a, gtt], [macc])
                    else:
                        p.tt("dve", tmp[:], a[:], gtt[:], ALU.mult, [a, gtt], [tmp])
                        if i == 1:
                            p.tt("dve", macc[:], macc[:], tmp[:], ALU.add, [macc, tmp], [macc])
                        else:
                            p.tt("dve", mg[:, ct, :], macc[:], tmp[:], ALU.add, [macc, tmp], [mg])
            if STOP <= 1:
                continue
            for c2 in range(16):
                ws, wbb = wst2[c2 % 3], wb2[c2 % 3]
                wv = c.w_out.h[l].rearrange("(k p) c -> p k c", p=128)
                p.dma("sp", ws[:], wv[:, :, c2 * 128:(c2 + 1) * 128], reads=[c.w_out], writes=[ws])
                p.copy("act", wbb[:], ws[:], [ws], [wbb])
                a = acc[ai % 4]
                ai += 1
                for k in range(16):
                    p.mm(a[:], wbb[:, k, :], mg[:, k, :], start=(k == 0), stop=(k == 15), reads=[wbb, mg], writes=[a])
                p.copy("dve", osb[:, c2, :], a[:], [a], [osb])
                p.act(sq[c2 % 2][:], a[:], AF.Square, [a], [sq[c2 % 2]])
                p.mm(ss[:], c.ones_bf[:], sq[c2 % 2][:], start=(c2 == 0), stop=(c2 == 15),
                     reads=[c.ones_bf, sq[c2 % 2]], writes=[ss])
            if STOP <= 2:
                continue
            rstd_from_ss(p, rs, ss, D, TT)
            if STOP <= 3:
                continue
            for c2 in range(16):
                xx, xn = xt[c2 % 2], xo[c2 % 2]
                p.dma("sp", xx[:], xin[c2 * 128:(c2 + 1) * 128, t0:t0 + TT], reads=[xin], writes=[xx])
                p.tt("pool", tmp[:], osb[:, c2, :], rs[:], ALU.mult, [osb, rs], [tmp])
                p.stt(xn[:], tmp[:], cvcol(c, l, "post_g", c2), xx[:], ALU.mult, ALU.add, [tmp, xx, c.cv], [xn])
                p.dma("pool", xout[c2 * 128:(c2 + 1) * 128, t0:t0 + TT], xn[:], reads=[xn], writes=[xout])


def phase_ssd(p, c, l):
    T = c.T
    NCH = T // 128
    IDb = c.cm_bf[:, 0:128]
    ID = c.cm
    TRIU = c.cm[:, 128:256]
    SU = c.cm[:, 256:384]
    ONESF = c.cm[:, 384:512]
    ONEC = c.cm[:, 384:385]
    with p.scope():
        dtT = p.sb("dtT", [16, T], F32)
        adtT = p.sb("adtT", [16, T], F32)
        negA = p.sb("negA", [16, 1], F32)
        with p.scope():
            dtr = p.sb("dtr", [16, T], F32)
            p.dma("sp", dtr[:], c.projT[A_DT:A_DT + 16, 0:T], reads=[c.projT], writes=[dtr])
            p.act(dtr[:], dtr[:], AF.Exp, [dtr, c.cv], [dtr], bias=cvcol(c, l, "dt_bias", 0, 16))
            p.act(dtT[:], dtr[:], AF.Ln, [dtr, c.cm], [dtT], bias=c.cm[0:16, 384:385])
            p.act(negA[:], cvcol(c, l, "a_log", 0, 16), AF.Exp, [c.cv], [negA])
            p.ts("dve", negA[:], negA[:], -1.0, ALU.mult, [negA], [negA])
            p.ts("dve", adtT[:], dtT[:], negA[:, 0:1], ALU.mult, [dtT, negA], [adtT])
            xr = [p.sb("xr%d" % i, [128, T + 3], F32) for i in range(2)]
            ca = [p.sb("ca%d" % i, [128, T], F32) for i in range(2)]
            xo = [p.sb("cxo%d" % i, [128, T], BF16) for i in range(2)]
            for i in range(2):
                p.memset("pool", xr[i][:, 0:3], 0.0, writes=[xr[i]])
            for ci in range(16):
                x_, a_, o_ = xr[ci % 2], ca[ci % 2], xo[ci % 2]
                r0 = A_XBC + ci * 128
                p.dma("sp", x_[:, 3:T + 3], c.projT[r0:r0 + 128, 0:T], reads=[c.projT], writes=[x_])
                p.ts("dve", a_[:], x_[:, 3:T + 3], cvcol(c, l, "conv_w", 3 * 16 + ci), ALU.mult, [x_, c.cv], [a_],
                     s2=cvcol(c, l, "conv_b", ci), op1=ALU.add)
                for j in (2, 1, 0):
                    p.stt(a_[:], x_[:, j:T + j], cvcol(c, l, "conv_w", j * 16 + ci), a_[:], ALU.mult, ALU.add,
                          [x_, a_, c.cv], [a_])
                p.act(o_[:], a_[:], AF.Silu, [a_], [o_])
                p.dma("pool", c.xbcT[ci * 128:(ci + 1) * 128, 0:T], o_[:], reads=[o_], writes=[c.xbcT])
        xbc = [p.sb("xbc%d" % i, [128, 16, 128], BF16) for i in range(2)]
        zt = [p.sb("zt%d" % i, [128, 8, 128], F32) for i in range(2)]
        dtk = p.sb("dtk", [128, 32], F32)
        sm = p.sb("ssm_sm", [128, 64], F32)
        Xdt = p.sb("Xdt", [128, 1024], BF16)
        Xds = p.sb("Xds", [128, 1024], BF16)
        Btok = p.sb("Btok", [128, 512], BF16)
        adx = p.sb("adx", [128, 1024], F32)
        lall = p.sb("lall", [128, 16, 128], F32)
        E = p.sb("E", [128, 1024], F32)
        Lm = p.sb("Lm", [128, 2048], F32)
        CBm = p.sb("CBm", [128, 512], F32)
        M = p.sb("M", [128, 16, 128], BF16)
        hT = p.sb("hT", [128, 1024], F32)
        hTb = p.sb("hTb", [128, 1024], BF16)
        Y = p.sb("Y", [128, 1024], F32)
        t1 = p.sb("t1", [128, 1024], F32)
        sz = p.sb("sz", [128, 1024], F32)
        sqy = p.sb("sqy", [128, 1024], BF16)
        rsy = p.sb("rsy", [128, 512], F32)
        yo = [p.sb("yo%d" % i, [128, 8, 128], BF16) for i in range(2)]
        P0 = p.ps("P0", [128, 2048], BF16)
        P1 = p.ps("P1", [128, 2048], F32)
        P2 = p.ps("P2", [128, 1024], F32)
        p.memset("dve", hT[:], 0.0, writes=[hT])
        p.memset("pool", hTb[:], 0.0, writes=[hTb])
        xv = c.xbcT.h.rearrange("(k p) t -> p k t", p=128)
        zv = c.projT.h[A_Z:A_Z + 1024, :].rearrange("(k p) t -> p k t", p=128)
        yav = c.yaT.h.rearrange("(k p) t -> p k t", p=128)
        for ch in range(NCH):
            t0 = ch * 128
            xb, zz, yy = xbc[ch % 2], zt[ch % 2], yo[ch % 2]
            p.dma("sp", xb[:], xv[:, :, t0:t0 + 128], reads=[c.xbcT], writes=[xb])
            p.dma("sp", zz[:], zv[:, :, t0:t0 + 128], reads=[c.projT], writes=[zz])
            p.tr(P2[:, 0:16], dtT[:, t0:t0 + 128], ID[0:16, 0:16], [dtT, c.cm], [P2])
            p.tr(P2[:, 16:32], adtT[:, t0:t0 + 128], ID[0:16, 0:16], [adtT, c.cm], [P2])
            p.copy("act", dtk[:], P2[:, 0:32], [P2], [dtk])
            p.mm(P2[:, 64:80], SU, dtk[:, 16:32], reads=[c.cm, dtk], writes=[P2])
            p.mm(P2[:, 128:144], ONESF, dtk[:, 16:32], reads=[c.cm, dtk], writes=[P2])
            p.act(sm[:, 0:16], P2[:, 64:80], AF.Exp, [P2], [sm])
            p.act(sm[:, 16:32], P2[:, 128:144], AF.Exp, [P2], [sm])
            p.tt("dve", sm[:, 32:48], sm[:, 0:16], dtk[:, 0:16], ALU.mult, [sm, dtk], [sm])
            for j in range(12):
                p.tr(P0[:, j * 128:(j + 1) * 128], xb[:, j, :], IDb, [xb, c.cm_bf], [P0])
            px = P0[:, 0:1024].rearrange("p (h d) -> p h d", d=64)
            p.tt("dve", Xdt[:].rearrange("p (h d) -> p h d", d=64), px,
                 dtk[:, 0:16].unsqueeze(2).to_broadcast([128, 16, 64]), ALU.mult, [P0, dtk], [Xdt])
            p.tt("dve", Xds[:].rearrange("p (h d) -> p h d", d=64), px,
                 sm[:, 32:48].unsqueeze(2).to_broadcast([128, 16, 64]), ALU.mult, [P0, sm], [Xds])
            p.copy("act", Btok[:], P0[:, 1024:1536], [P0], [Btok])
            p.copy("pool", adx[:].rearrange("p (h d) -> p h d", d=64),
                   dtk[:, 16:32].unsqueeze(2).to_broadcast([128, 16, 64]), [dtk], [adx])
            p.tt("pool", lall[:], dtk[:, 16:32].unsqueeze(2).to_broadcast([128, 16, 128]),
                 SU.unsqueeze(1).to_broadcast([128, 16, 128]), ALU.mult, [dtk, c.cm], [lall])
            for h in range(16):
                p.mm(P1[:, h * 128:(h + 1) * 128], lall[:, h, :], TRIU, reads=[lall, c.cm], writes=[P1])
            p.act(Lm[:], P1[:], AF.Exp, [P1], [Lm])
            for j in range(8):
                p.mm(P2[:, j * 128:(j + 1) * 128], adx[:, j * 128:(j + 1) * 128], TRIU, reads=[adx, c.cm], writes=[P2])
            p.act(E[:], P2[:], AF.Exp, [P2], [E])
            for g in range(4):
                p.mm(P2[:, g * 128:(g + 1) * 128], xb[:, 8 + g, :], xb[:, 12 + g, :], reads=[xb], writes=[P2])
            p.tt("dve", CBm[:].rearrange("p (g l) -> p g l", l=128), P2[:, 0:512].rearrange("p (g l) -> p g l", l=128),
                 TRIU.unsqueeze(1).to_broadcast([128, 4, 128]), ALU.mult, [P2, c.cm], [CBm])
            p.tt("dve", M[:].rearrange("p (g r) l -> p g r l", r=4), Lm[:].rearrange("p (g r l) -> p g r l", r=4, l=128),
                 CBm[:].rearrange("p (g l) -> p g l", l=128).unsqueeze(2).to_broadcast([128, 4, 4, 128]), ALU.mult,
                 [Lm, CBm], [M])
            for h in range(16):
                p.mm(P1[(h % 2) * 64:(h % 2) * 64 + 64, (h // 2) * 128:(h // 2) * 128 + 128], Xdt[:, h * 64:(h + 1) * 64],
                     M[:, h, :], reads=[Xdt, M], writes=[P1])
            for j in range(8):
                p.mm(P2[:, j * 128:(j + 1) * 128], hTb[:, j * 128:(j + 1) * 128], xb[:, 12 + j // 2, :],
                     reads=[hTb, xb], writes=[P2])
            p.tt("dve", t1[:], P2[:], E[:], ALU.mult, [P2, E], [t1])
            p.tt("dve", Y[:], t1[:], P1[:, 0:1024], ALU.add, [t1, P1], [Y])
            for g in range(4):
                p.mm(P1[:, 1024 + g * 256:1024 + (g + 1) * 256], Btok[:, g * 128:(g + 1) * 128],
                     Xds[:, g * 256:(g + 1) * 256], reads=[Btok, Xds], writes=[P1])
            p.tt("pool", hT[:].rearrange("p (h d) -> p h d", d=64), hT[:].rearrange("p (h d) -> p h d", d=64),
                 sm[:, 16:32].unsqueeze(2).to_broadcast([128, 16, 64]), ALU.mult, [hT, sm], [hT])
            p.tt("dve", hT[:], hT[:], P1[:, 1024:2048], ALU.add, [hT, P1], [hT])
            p.copy("act", hTb[:], hT[:], [hT], [hTb])
            o_d, _ = CV["d_exp"]
            o_n, _ = CV["ssm_norm"]
            dcol = c.cv[:, l * NV + o_d:l * NV + o_d + 8].unsqueeze(2).to_broadcast([128, 8, 128])
            ncol = c.cv[:, l * NV + o_n:l * NV + o_n + 8].unsqueeze(2).to_broadcast([128, 8, 128])
            Y3 = Y[:].rearrange("p (j l) -> p j l", l=128)
            t13 = t1[:].rearrange("p (j l) -> p j l", l=128)
            p.tt("pool", t13, xb[:, 0:8, :], dcol, ALU.mult, [xb, c.cv], [t1])
            p.tt("pool", Y[:], Y[:], t1[:], ALU.add, [Y, t1], [Y])
            p.act(sz[:], zz[:].rearrange("p j l -> p (j l)"), AF.Silu, [zz], [sz])
            p.tt("dve", Y[:], Y[:], sz[:], ALU.mult, [Y, sz], [Y])
            p.tt("pool", sqy[:], Y[:], Y[:], ALU.mult, [Y], [sqy])
            for g in range(4):
                for i2 in range(2):
                    j = 2 * g + i2
                    p.mm(P2[:, g * 128:(g + 1) * 128], c.ones_bf[:], sqy[:, j * 128:(j + 1) * 128],
                         start=(i2 == 0), stop=(i2 == 1), reads=[c.ones_bf, sqy], writes=[P2])
            rstd_from_ss(p, rsy, P2, 256, 512)
            p.tt("dve", Y[:].rearrange("p (g i l) -> p g i l", i=2, l=128), Y[:].rearrange("p (g i l) -> p g i l", i=2, l=128),
                 rsy[:].rearrange("p (g l) -> p g l", l=128).unsqueeze(2).to_broadcast([128, 4, 2, 128]), ALU.mult,
                 [Y, rsy], [Y])
            p.tt("dve", yy[:], Y3, ncol, ALU.mult, [Y, c.cv], [yy])
            p.dma("pool", yav[:, :, t0:t0 + 128], yy[:], reads=[yy], writes=[c.yaT])


R_Q, R_IQ, R_K, R_IK = 0, 1024, 2048, 2304
MAGIC = 12582912.0
NEGBIG = -1.0e30


def phase_dsa(p, c, l):
    T = c.T
    NQ = T // 128
    ID = c.cm[:, 0:128]
    IDb = c.cm_bf[:, 0:128]
    ONESF = c.cm[:, 384:512]
    PERM = c.cm[:, 512:640]
    NEGM = c.cm[:, 640:768]
    INV = c.cm[:, 768:769]
    SGN = c.cm[:, 769:770]
    HPI = c.cm[:, 770:771]
    TWO_PI = 6.283185307179586
    C1 = 6.28125
    C2 = TWO_PI - C1
    with p.scope():
        Ct = p.sb("ropeC", [128, T], F32)
        St = p.sb("ropeS", [128, T], F32)
        with p.scope():
            pi_ = p.sb("posi", [128, T], I32)
            ang = p.sb("ang", [128, T], F32)
            kk = p.sb("angk", [128, T], F32)
            r = p.sb("angr", [128, T], F32)
            p.dma("sp", pi_[:], c.pos_d[:], reads=[c.pos_d], writes=[pi_])
            p.copy("dve", ang[:], pi_[:], [pi_], [ang])
            p.ts("dve", ang[:], ang[:], INV, ALU.mult, [ang, c.cm], [ang])
            p.ts("dve", kk[:], ang[:], 1.0 / TWO_PI, ALU.mult, [ang], [kk], s2=MAGIC, op1=ALU.add)
            p.ts("dve", kk[:], kk[:], MAGIC, ALU.subtract, [kk], [kk])
            p.stt(r[:], kk[:], -C1, ang[:], ALU.mult, ALU.add, [kk, ang], [r])
            p.stt(r[:], kk[:], -C2, r[:], ALU.mult, ALU.add, [kk, r], [r])
            p.ts("dve", r[:], r[:], 3.14159, ALU.min, [r], [r], s2=-3.14159, op1=ALU.max)
            p.act(St[:], r[:], AF.Sin, [r], [St])
            p.ts("dve", St[:], St[:], SGN, ALU.mult, [St, c.cm], [St])
            p.ts("dve", kk[:], r[:], -1.0, ALU.mult, [r], [kk])
            p.tt("dve", kk[:], kk[:], r[:], ALU.max, [kk, r], [kk])
            p.act(Ct[:], kk[:], AF.Sin, [kk, c.cm], [Ct], bias=HPI, scale=-1.0)
        with p.scope():
            xs = [p.sb("rx%d" % i, [128, T], F32) for i in range(2)]
            ob = [p.sb("rob%d" % i, [128, T], BF16) for i in range(2)]
            tmp = p.sb("rtmp", [128, 512], F32)
            o1 = p.sb("ro1", [128, 512], F32)
            sqk = p.sb("rsq", [128, 512], F32)
            rsk = p.sb("rrs", [128, 512], F32)
            PS = [p.ps("rps%d" % i, [128, 512], F32) for i in range(4)]
            tiles = [(C_Q + i * 128, R_Q + i * 128, 128) for i in range(8)] + \
                    [(C_IQ + i * 128, R_IQ + i * 128, 128) for i in range(8)] + \
                    [(C_K + i * 128, R_K + i * 128, 128) for i in range(2)] + [(C_IK, R_IK, 64)]
            for ti, (src, dst, nr) in enumerate(tiles):
                x_, o_ = xs[ti % 2], ob[ti % 2]
                p.dma("sp", x_[0:nr, :], c.projT[src:src + nr, 0:T], reads=[c.projT], writes=[x_])
                for j in range(T // 512):
                    sl = slice(j * 512, (j + 1) * 512)
                    if nr == 64:
                        ps = PS[j % 2]
                        p.mm(ps[0:64, :], ONESF[0:64, 0:64], x_[0:64, sl], reads=[c.cm, x_], writes=[ps])
                        p.stt(x_[0:64, sl], ps[0:64, :], -1.0 / 64, x_[0:64, sl], ALU.mult, ALU.add, [ps, x_], [x_])
                        p.tt("pool", sqk[0:64, :], x_[0:64, sl], x_[0:64, sl], ALU.mult, [x_], [sqk])
                        p.mm(ps[0:64, :], ONESF[0:64, 0:64], sqk[0:64, :], reads=[c.cm, sqk], writes=[ps])
                        p.ts("dve", rsk[0:64, :], ps[0:64, :], 1.0 / 64, ALU.mult, [ps], [rsk], s2=EPS, op1=ALU.add)
                        p.act(rsk[0:64, :], rsk[0:64, :], AF.Sqrt, [rsk], [rsk])
                        p.op("dve", lambda e, a=rsk[0:64, :]: e.reciprocal(out=a, in_=a), [rsk], [rsk])
                        p.tt("dve", x_[0:64, sl], x_[0:64, sl], rsk[0:64, :], ALU.mult, [x_, rsk], [x_])
                        p.ts("dve", x_[0:64, sl], x_[0:64, sl], cvcol(c, l, "ikn_w", 0, 64), ALU.mult, [x_, c.cv], [x_],
                             s2=cvcol(c, l, "ikn_b", 0, 64), op1=ALU.add)
                    ps = PS[2 + j % 2]
                    p.mm(ps[0:nr, :], PERM[0:nr, 0:nr], x_[0:nr, sl], reads=[c.cm, x_], writes=[ps])
                    p.tt("dve", tmp[0:nr, :], ps[0:nr, :], St[0:nr, sl], ALU.mult, [ps, St], [tmp])
                    p.tt("pool", o1[0:nr, :], x_[0:nr, sl], Ct[0:nr, sl], ALU.mult, [x_, Ct], [o1])
                    p.tt("dve", o_[0:nr, sl], o1[0:nr, :], tmp[0:nr, :], ALU.add, [o1, tmp], [o_])
                p.dma("pool", c.dsaT[dst:dst + nr, 0:T], o_[0:nr, :], reads=[o_], writes=[c.dsaT])
    with p.scope():
        ik2 = p.sb("ik2", [128, T], BF16)
        kb = p.sb("kb", [64, 4, T], BF16)
        vtok = p.sb("vtok", [128, NQ, 256], BF16)
        P0 = p.ps("dP0", [128, 1024], BF16)
        PLa = [p.ps("dPLa%d" % i, [128, 512], F32) for i in range(2)]
        PLb = [p.ps("dPLb%d" % i, [128, 512], F32) for i in range(3)]
        PO = p.ps("dPO", [64, 512], F32)
        PR = p.ps("dPR", [64, 512], F32)
        for b in (0, 64):
            p.dma("sp", ik2[b:b + 64, :], c.dsaT[R_IK:R_IK + 64, 0:T], reads=[c.dsaT], writes=[ik2])
        p.dma("sp", kb[:], c.dsaT.h[R_K:R_K + 256, :].rearrange("(g d) t -> d g t", d=64), reads=[c.dsaT], writes=[kb])
        with p.scope():
            vf = p.sb("vf", [128, 2, T], F32)
            vb = p.sb("vb", [128, 2, T], BF16)
            p.dma("sp", vf[:], c.projT.h[C_V:C_V + 256, :].rearrange("(k p) t -> p k t", p=128), reads=[c.projT], writes=[vf])
            p.copy("pool", vb[:], vf[:], [vf], [vb])
            for b0 in range(0, NQ, 4):
                for blk in range(b0, b0 + 4):
                    for t2 in range(2):
                        i = (blk - b0) * 2 + t2
                        p.tr(P0[:, i * 128:(i + 1) * 128], vb[:, t2, blk * 128:(blk + 1) * 128], IDb, [vb, c.cm_bf], [P0])
                p.copy("dve", vtok[:, b0:b0 + 4, :].rearrange("p b c -> p (b c)"), P0[:, 0:1024], [P0], [vtok])
        Ss = [p.sb("dS%d" % i, [128, T], F32) for i in range(2)]
        msk = p.sb("dmsk", [128, T], BF16)
        mskTs = [p.sb("dmskT%d" % i, [128, NQ, 128], BF16) for i in range(2)]
        qsb = [p.sb("dq%d" % i, [64, 16, 128], BF16) for i in range(2)]
        iqs = [p.sb("diq%d" % i, [128, 8, 128], BF16) for i in range(2)]
        gts = [p.sb("dg%d" % i, [64, 16, 128], F32) for i in range(2)]
        iwc = [p.sb("diw%d" % i, [16, 128], F32) for i in range(2)]
        rl = [p.sb("drl%d" % i, [128, 512], F32) for i in range(3)]
        PT = [p.sb("dPT%d" % i, [128, 512], BF16) for i in range(3)]
        PTm = [p.sb("dPTm%d" % i, [128, 512], BF16) for i in range(3)]
        iwk = p.sb("diwk", [128, 16], F32)
        bs = p.sb("dbs", [128, 8], F32)
        Wt = p.sb("dWt", [128, 32], F32)
        junk = p.sb("djunk", [128, T], BF16)
        thr = p.sb("dthr", [128, 1], F32)
        rec = p.sb("drec", [64, 512], F32)
        ov = p.sb("dov", [64, 2048], F32)
        sg = p.sb("dsg", [64, 2048], F32)
        yo = [p.sb("dyo%d" % i, [64, 16, 128], BF16) for i in range(2)]
        qv = c.dsaT.h[R_Q:R_Q + 1024, :].rearrange("(h d) t -> d h t", d=64)
        iqv = c.dsaT.h[R_IQ:R_IQ + 1024, :].rearrange("(k p) t -> p k t", p=128)
        gv = c.projT.h[C_G:C_G + 1024, :].rearrange("(h d) t -> d h t", d=64)
        ycv = c.ycT.h.rearrange("(h d) t -> d h t", d=64)
        cnt = {"a": 0, "b": 0}

        def stageA(qi):
            q0 = qi * 128
            n = q0 + 128
            qs_, iq_, g_, iw_ = qsb[qi % 2], iqs[qi % 2], gts[qi % 2], iwc[qi % 2]
            S, mskT = Ss[qi % 2], mskTs[qi % 2]
            p.dma("sp", qs_[:], qv[:, :, q0:n], reads=[c.dsaT], writes=[qs_])
            p.dma("sp", iq_[:], iqv[:, :, q0:n], reads=[c.dsaT], writes=[iq_])
            p.dma("sp", g_[:], gv[:, :, q0:n], reads=[c.projT], writes=[g_])
            p.dma("sp", iw_[:], c.projT[C_IK + 64:C_IK + 80, q0:n], reads=[c.projT], writes=[iw_])
            p.tr(PLa[0][:, 0:16], iw_[:], ID[0:16, 0:16], [iw_, c.cm], [PLa[0]])
            p.copy("act", iwk[:], PLa[0][:, 0:16], [PLa[0]], [iwk])
            yield
            steps = [(s0, min(512, n - s0), h) for s0 in range(0, n, 512) for h in range(16)]
            NS = len(steps)
            for j in range(NS + 2):
                if j < NS:
                    s0, w, h = steps[j]
                    b = (h % 2) * 64
                    p.mm(PLa[j % 2][:, 0:w], iq_[b:b + 64, h // 2, :], ik2[b:b + 64, s0:s0 + w], reads=[iq_, ik2], writes=[PLa[j % 2]])
                if 0 <= j - 1 < NS:
                    s0, w, h = steps[j - 1]
                    p.act(rl[(j - 1) % 3][:, 0:w], PLa[(j - 1) % 2][:, 0:w], AF.Relu, [PLa[(j - 1) % 2]], [rl[(j - 1) % 3]])
                if 0 <= j - 2 < NS:
                    s0, w, h = steps[j - 2]
                    r_ = rl[(j - 2) % 3]
                    if h == 0:
                        p.ts("dve", S[:, s0:s0 + w], r_[:, 0:w], iwk[:, 0:1], ALU.mult, [r_, iwk], [S])
                    else:
                        p.stt(S[:, s0:s0 + w], r_[:, 0:w], iwk[:, h:h + 1], S[:, s0:s0 + w], ALU.mult, ALU.add, [r_, iwk, S], [S])
                yield
            if qi >= 2:
                p.op("dve", lambda e, a=S[:, 0:n]: e.tensor_reduce(out=bs[:, 0:1], in_=a, op=ALU.min, axis=AX.X), [S], [bs])
                p.op("dve", lambda e, a=S[:, 0:n]: e.tensor_reduce(out=bs[:, 1:2], in_=a, op=ALU.max, axis=AX.X), [S], [bs])
                yield
            p.tt("dve", S[:, q0:n], S[:, q0:n], NEGM, ALU.add, [S, c.cm], [S])
            if qi >= 2:
                NIT = 28
                p.tt("dve", bs[:, 2:3], bs[:, 1:2], bs[:, 0:1], ALU.subtract, [bs], [bs])
                p.ts("dve", bs[:, 2:3], bs[:, 2:3], 1.0001, ALU.mult, [bs], [bs], s2=1e-6, op1=ALU.add)
                p.ts("dve", Wt[:, 0:NIT], c.cm[:, 1280:1280 + NIT], bs[:, 2:3], ALU.mult, [c.cm, bs], [Wt])
                p.ts("dve", bs[:, 3:4], bs[:, 0:1], -1.0, ALU.mult, [bs], [bs])
                p.tt("dve", bs[:, 4:5], bs[:, 3:4], Wt[:, 0:1], ALU.subtract, [bs, Wt], [bs])
                yield
                for k in range(NIT):
                    p.op("act", lambda e, a=S[:, 0:n], o=junk[:, 0:n]: e.activation(out=o, in_=a, func=AF.Sign, bias=bs[:, 4:5],
                                                                                     scale=1.0, accum_out=bs[:, 5:6]),
                         [S, bs], [junk, bs])
                    yield
                    p.stt(bs[:, 6:7], bs[:, 5:6], float(512 - n), Wt[:, k:k + 1], ALU.is_ge, ALU.mult, [bs, Wt], [bs])
                    p.tt("dve", bs[:, 3:4], bs[:, 3:4], bs[:, 6:7], ALU.subtract, [bs], [bs])
                    if k + 1 < NIT:
                        p.tt("dve", bs[:, 4:5], bs[:, 3:4], Wt[:, k + 1:k + 2], ALU.subtract, [bs, Wt], [bs])
                    yield
                p.ts("dve", thr[:], bs[:, 3:4], -1.0, ALU.mult, [bs], [thr])
            else:
                p.memset("dve", thr[:], -1.0e29, writes=[thr])
            p.ts("dve", msk[:, 0:n], S[:, 0:n], thr[:, 0:1], ALU.is_ge, [S, thr], [msk])
            yield
            for b0 in range(0, qi + 1, 8):
                nb = min(8, qi + 1 - b0)
                for j in range(nb):
                    p.tr(P0[:, j * 128:(j + 1) * 128], msk[:, (b0 + j) * 128:(b0 + j + 1) * 128], IDb, [msk, c.cm_bf], [P0])
                p.copy("act", mskT[:, b0:b0 + nb, :].rearrange("p b c -> p (b c)"), P0[:, 0:nb * 128], [P0], [mskT])
                yield

        def stageB(qi):
            q0 = qi * 128
            n = q0 + 128
            qs_, g_, y_ = qsb[qi % 2], gts[qi % 2], yo[qi % 2]
            mskT = mskTs[qi % 2]
            groups = [(g, sb) for g in range(4) for sb in range(qi + 1)]

            def qk(gi):
                g, sb = groups[gi]
                pl = PLb[gi % 3]
                p.mm(pl[:, :], kb[:, g, sb * 128:(sb + 1) * 128], qs_[:, 4 * g:4 * g + 4, :].rearrange("p h q -> p (h q)"),
                     reads=[kb, qs_], writes=[pl])

            NGp = len(groups)
            for i in range(NGp + 3):
                if i < NGp:
                    qk(i)
                if 0 <= i - 1 < NGp:
                    gi = i - 1
                    p.act(PT[gi % 3][:], PLb[gi % 3][:], AF.Exp, [PLb[gi % 3]], [PT[gi % 3]], scale=0.125)
                if 0 <= i - 2 < NGp:
                    gi = i - 2
                    g, sb = groups[gi]
                    p.tt("pool", PTm[gi % 3][:].rearrange("p (h q) -> p h q", q=128), PT[gi % 3][:].rearrange("p (h q) -> p h q", q=128),
                         mskT[:, sb:sb + 1, :].to_broadcast([128, 4, 128]), ALU.mult, [PT[gi % 3], mskT], [PTm[gi % 3]])
                if 0 <= i - 3 < NGp:
                    gi = i - 3
                    g, sb = groups[gi]
                    ptm = PTm[gi % 3]
                    p.mm(PO[:], vtok[:, sb, g * 64:(g + 1) * 64], ptm[:], start=(sb == 0), stop=(sb == qi),
                         reads=[vtok, ptm], writes=[PO])
                    p.mm(PR[:], c.ones_bf[:, 0:64], ptm[:], start=(sb == 0), stop=(sb == qi),
                         reads=[c.ones_bf, ptm], writes=[PR])
                    if sb == qi:
                        p.op("dve", lambda e: e.reciprocal(out=rec[:], in_=PR[:]), [PR], [rec])
                        p.tt("dve", ov[:, g * 512:(g + 1) * 512], PO[:], rec[:], ALU.mult, [PO, rec], [ov])
                yield
            p.act(sg[:], g_[:].rearrange("p h l -> p (h l)"), AF.Silu, [g_], [sg])
            p.tt("pool", y_[:].rearrange("p h l -> p (h l)"), ov[:], sg[:], ALU.mult, [ov, sg], [y_])
            p.dma("pool", ycv[:, :, q0:n], y_[:], reads=[y_], writes=[c.ycT])
            yield

        for _ in stageA(0):
            pass
        for qi in range(NQ):
            gb = stageB(qi)
            ga = stageA(qi + 1) if qi + 1 < NQ else iter(())
            la, lb = True, True
            while la or lb:
                for _ in range(2):
                    if la:
                        try:
                            next(ga)
                        except StopIteration:
                            la = False
                if lb:
                    try:
                        next(gb)
                    except StopIteration:
                        lb = False


def phase_rwkv(p, c, l):
    T = c.T
    NB = T // 128
    ID = c.cm[:, 0:128]
    ONEC = c.cm[:, 384:385]
    NHALF = c.cm[:, 771:772]
    GNEPS = c.cm[:, 772:773]
    BLK = c.cm[:, 896:1024]
    rows5 = c.rowsD.h.rearrange("t (f c) -> t f c", f=5)
    with p.scope():
        negw0 = p.sb("negw0", [128, 8], F32)
        o_w0, _ = CV["w0"]
        p.ts("dve", negw0[:], c.cv[:, l * NV + o_w0:l * NV + o_w0 + 8], -1.0, ALU.mult, [c.cv], [negw0])
        w2s = p.sb("w2s", [128, 1024], F32)
        w2b = p.sb("w2b", [128, 1024], BF16)
        p.dma("sp", w2s[0:64, :], c.rw2.h[l], reads=[c.rw2], writes=[w2s])
        p.dma("sp", w2s[64:128, :], c.ra2.h[l], reads=[c.ra2], writes=[w2s])
        p.copy("dve", w2b[:], w2s[:], [w2s], [w2b])
        xr = [p.sb("bxr%d" % i, [128, T + 1], F32) for i in range(3)]
        mx = {n: p.sb("bm_" + n, [128, T], F32) for n in ("wa", "r", "k", "v")}
        wab = p.sb("wab", [128, T], BF16)
        dec = p.sb("bdec", [128, T], F32)
        av = p.sb("bav", [128, T], F32)
        kkn = p.sb("bkkn", [128, T], F32)
        t2 = p.sb("bt2", [128, 512], F32)
        t3 = p.sb("bt3", [128, 512], F32)
        tro = [p.sb("btro%d" % i, [128, 5, 128], F32) for i in range(2)]
        PS = [p.ps("bps%d" % i, [128, 512], F32) for i in range(4)]
        PTt = [p.ps("bpt%d" % i, [128, 5 * 128], F32) for i in range(2)]
        for i in range(3):
            p.memset("pool", xr[i][:, 0:1], 0.0, writes=[xr[i]])
        xi = [0]

        def mix(src, mu_name, mu_j, dst):
            x_ = xr[xi[0] % 3]
            xi[0] += 1
            p.dma("sp", x_[:, 1:T + 1], c.projT[src:src + 128, 0:T], reads=[c.projT], writes=[x_])
            p.tt("pool", dst[:], x_[:, 0:T], x_[:, 1:T + 1], ALU.subtract, [x_], [dst])
            p.stt(dst[:], dst[:], cvcol(c, l, mu_name, mu_j), x_[:, 1:T + 1], ALU.mult, ALU.add, [dst, x_, c.cv], [dst])

        mix(B_WA, "mu_wa", 0, mx["wa"])
        p.act(mx["wa"][0:64, :], mx["wa"][0:64, :], AF.Tanh, [mx["wa"]], [mx["wa"]])
        p.copy("dve", wab[:], mx["wa"][:], [mx["wa"]], [wab])
        for j in range(8):
            mix(B_R + j * 128, "mu_r", j, mx["r"])
            mix(B_K + j * 128, "mu_k", j, mx["k"])
            mix(B_V + j * 128, "mu_v", j, mx["v"])
            p.dma("pool", c.vD[j * 128:(j + 1) * 128, 0:T], mx["v"][:], reads=[mx["v"]], writes=[c.vD])
            for q in range(T // 512):
                sl = slice(q * 512, (q + 1) * 512)
                pw, pa, pk, pr = PS
                p.mm(pw[:], w2b[0:64, j * 128:(j + 1) * 128], wab[0:64, sl], reads=[w2b, wab], writes=[pw])
                p.mm(pa[:], w2b[64:128, j * 128:(j + 1) * 128], wab[64:128, sl], reads=[w2b, wab], writes=[pa])
                p.act(t2[:], pw[:], AF.Exp, [pw, negw0], [t2], bias=negw0[:, j:j + 1], scale=-1.0)
                p.act(t2[:], t2[:], AF.Ln, [t2, c.cm], [t2], bias=ONEC)
                p.act(t2[:], t2[:], AF.Exp, [t2, c.cm], [t2], bias=NHALF, scale=-1.0)
                p.act(dec[:, sl], t2[:], AF.Exp, [t2], [dec], scale=-1.0)
                p.act(av[:, sl], pa[:], AF.Sigmoid, [pa, c.cv], [av], bias=cvcol(c, l, "a0", j))
                p.ts("dve", kkn[:, sl], mx["k"][:, sl], cvcol(c, l, "k_k", j), ALU.mult, [mx["k"], c.cv], [kkn])
                p.tt("pool", t3[:], kkn[:, sl], kkn[:, sl], ALU.mult, [kkn], [t3])
                p.mm(pk[:], BLK, t3[:], reads=[c.cm, t3], writes=[pk])
                p.act(t3[:], pk[:], AF.Sqrt, [pk], [t3])
                p.ts("dve", t3[:], t3[:], 1e-12, ALU.max, [t3], [t3])
                p.op("dve", lambda e, a=t3[:]: e.reciprocal(out=a, in_=a), [t3], [t3])
                p.tt("dve", kkn[:, sl], kkn[:, sl], t3[:], ALU.mult, [kkn, t3], [kkn])
                p.ts("dve", t3[:], av[:, sl], -1.0, ALU.add, [av], [t3], s2=cvcol(c, l, "k_a", j), op1=ALU.mult)
                p.ts("dve", t3[:], t3[:], 1.0, ALU.add, [t3], [t3])
                p.tt("dve", mx["k"][:, sl], mx["k"][:, sl], t3[:], ALU.mult, [mx["k"], t3], [mx["k"]])
                p.tt("pool", av[:, sl], av[:, sl], kkn[:, sl], ALU.mult, [av, kkn], [av])
                p.tt("pool", t3[:], mx["r"][:, sl], mx["k"][:, sl], ALU.mult, [mx["r"], mx["k"]], [t3])
                p.ts("dve", t3[:], t3[:], cvcol(c, l, "r_k", j), ALU.mult, [t3, c.cv], [t3])
                p.mm(pr[:], BLK, t3[:], reads=[c.cm, t3], writes=[pr])
                p.copy("act", t2[:], pr[:], [pr], [t2])
                p.dma("pool", c.rkD[j * 128:(j + 1) * 128, sl], t2[:], reads=[t2], writes=[c.rkD])
            srcs = [kkn, dec, av, mx["k"], mx["r"]]
            for bi in range(NB):
                pt = PTt[bi % 2]
                to = tro[bi % 2]
                for f in range(5):
                    p.tr(pt[:, f * 128:(f + 1) * 128], srcs[f][:, bi * 128:(bi + 1) * 128], ID, [srcs[f], c.cm], [pt])
                p.copy("act" if bi % 2 else "dve", to[:].rearrange("p f c -> p (f c)"), pt[:], [pt], [to])
                p.dma("pool", rows5[bi * 128:(bi + 1) * 128, :, j * 128:(j + 1) * 128], to[:], reads=[to], writes=[c.rowsD])
    TB = 128
    with p.scope():
        S = p.sb("rS", [64, 1024], F32)
        tmp = p.sb("rtmp", [64, 1024], F32)
        tmp2 = p.sb("rtmp2", [64, 1024], F32)
        sa = p.sb("rsa", [64, 16], F32)
        Vc = [p.sb("rVc%d" % i, [64, 16, TB], F32) for i in range(2)]
        Yc = [p.sb("rYc%d" % i, [64, 16, TB], F32) for i in range(2)]
        NBUF = 4
        rb = [p.sb("rrow%d" % i, [64, 5, 1024], F32) for i in range(NBUF)]
        p.memset("dve", S[:], 0.0, writes=[S])
        vv = c.vD.h.rearrange("(h v) t -> v h t", v=64)
        yv = c.yD.h.rearrange("(h v) t -> v h t", v=64)
        S3 = S[:].rearrange("p (h k) -> p h k", k=64)
        T3 = tmp[:].rearrange("p (h k) -> p h k", k=64)
        U3 = tmp2[:].rearrange("p (h k) -> p h k", k=64)
        for t in range(T):
            blk, tl = divmod(t, TB)
            vc, yc = Vc[blk % 2], Yc[blk % 2]
            if tl == 0:
                p.dma("sp", vc[:], vv[:, :, blk * TB:(blk + 1) * TB], reads=[c.vD], writes=[vc])
            r_ = rb[t % NBUF]
            p.dma("sp", r_[:].rearrange("p f c -> p (f c)"), c.rowsD[t:t + 1, :].to_broadcast([64, 5 * 1024]),
                  reads=[c.rowsD], writes=[r_])
            KK, W, B_, K_, R_ = [r_[:, f, :] for f in range(5)]
            p.tt("dve", tmp[:], S[:], KK, ALU.mult, [S, r_], [tmp])
            p.op("dve", lambda e, o=sa[:], i=T3: e.tensor_reduce(out=o, in_=i, op=ALU.add, axis=AX.X, negate=True), [tmp], [sa])
            p.tt("pool", S[:], S[:], W, ALU.mult, [S, r_], [S])
            p.tt("pool", U3, vc[:, :, tl:tl + 1].to_broadcast([64, 16, 64]), K_.rearrange("p (h k) -> p h k", k=64), ALU.mult,
                 [vc, r_], [tmp2])
            p.tt("pool", S[:], S[:], tmp2[:], ALU.add, [S, tmp2], [S])
            p.tt("dve", T3, sa[:].unsqueeze(2).to_broadcast([64, 16, 64]), B_.rearrange("p (h k) -> p h k", k=64), ALU.mult,
                 [sa, r_], [tmp])
            p.tt("dve", S[:], S[:], tmp[:], ALU.add, [S, tmp], [S])
            p.tt("dve", tmp[:], S[:], R_, ALU.mult, [S, r_], [tmp])
            p.op("dve", lambda e, o=yc[:, :, tl], i=T3: e.tensor_reduce(out=o, in_=i, op=ALU.add, axis=AX.X), [tmp], [yc])
            if tl == TB - 1:
                p.dma("pool", yv[:, :, blk * TB:(blk + 1) * TB], yc[:], reads=[yc], writes=[c.yD])
    with p.scope():
        yt = [p.sb("cy%d" % i, [128, T], F32) for i in range(2)]
        vt = [p.sb("cv_%d" % i, [128, T], F32) for i in range(2)]
        rkt = [p.sb("crk%d" % i, [128, T], F32) for i in range(2)]
        gt = [p.sb("cg%d" % i, [128, T], F32) for i in range(2)]
        ob = [p.sb("cob%d" % i, [128, T], BF16) for i in range(2)]
        t2 = p.sb("ct2", [128, 512], F32)
        t3 = p.sb("ct3", [128, 512], F32)
        PS = [p.ps("cps%d" % i, [128, 512], F32) for i in range(4)]
        for j in range(8):
            y_, v_, rk_, g_, o_ = yt[j % 2], vt[j % 2], rkt[j % 2], gt[j % 2], ob[j % 2]
            rs_ = slice(j * 128, (j + 1) * 128)
            p.dma("sp", y_[:], c.yD[rs_, 0:T], reads=[c.yD], writes=[y_])
            p.dma("sp", v_[:], c.vD[rs_, 0:T], reads=[c.vD], writes=[v_])
            p.dma("sp", rk_[:], c.rkD[rs_, 0:T], reads=[c.rkD], writes=[rk_])
            p.dma("sp", g_[:], c.projT[B_G + j * 128:B_G + (j + 1) * 128, 0:T], reads=[c.projT], writes=[g_])
            for q in range(T // 512):
                sl = slice(q * 512, (q + 1) * 512)
                pm, pv = PS[(2 * q) % 4], PS[(2 * q + 1) % 4]
                p.mm(pm[:], BLK, y_[:, sl], reads=[c.cm, y_], writes=[pm])
                p.stt(y_[:, sl], pm[:], -1.0 / 64, y_[:, sl], ALU.mult, ALU.add, [pm, y_], [y_])
                p.tt("pool", t2[:], y_[:, sl], y_[:, sl], ALU.mult, [y_], [t2])
                p.mm(pv[:], BLK, t2[:], reads=[c.cm, t2], writes=[pv])
                p.ts("dve", t3[:], pv[:], 1.0 / 64, ALU.mult, [pv], [t3], s2=64e-5, op1=ALU.add)
                p.act(t3[:], t3[:], AF.Sqrt, [t3], [t3])
                p.op("dve", lambda e, a=t3[:]: e.reciprocal(out=a, in_=a), [t3], [t3])
                p.tt("dve", y_[:, sl], y_[:, sl], t3[:], ALU.mult, [y_, t3], [y_])
                p.ts("dve", y_[:, sl], y_[:, sl], cvcol(c, l, "ln_w", j), ALU.mult, [y_, c.cv], [y_],
                     s2=cvcol(c, l, "ln_b", j), op1=ALU.add)
                p.tt("pool", t2[:], rk_[:, sl], v_[:, sl], ALU.mult, [rk_, v_], [t2])
                p.tt("dve", y_[:, sl], y_[:, sl], t2[:], ALU.add, [y_, t2], [y_])
                p.act(t2[:], g_[:, sl], AF.Silu, [g_], [t2])
                p.tt("dve", o_[:, sl], y_[:, sl], t2[:], ALU.mult, [y_, t2], [o_])
            p.dma("pool", c.ybT[rs_, 0:T], o_[:], reads=[o_], writes=[c.ybT])


def phase_rwkv2(p, c, l):
    T = c.T
    G = 512
    NG = T // G
    IDb = c.cm_bf[0:64, 0:64]
    ONEC = c.cm[0:64, 384:385]
    NHALF = c.cm[0:64, 771:772]
    BLK = c.cm[0:64, 896:960]
    MASK4 = c.cm[0:64, 1024:1152]
    MLOW = c.cm[0:64, 1152:1216]
    I2 = c.cm[0:64, 1216:1280]

    def hcol(name, hd):
        o, w = CV[name]
        return c.cv[(hd % 2) * 64:(hd % 2) * 64 + 64, l * NV + o + hd // 2:l * NV + o + hd // 2 + 1]

    with p.scope():
        names = ["mu_r", "mu_k", "mu_v", "w0", "a0", "k_k", "k_a", "r_k"]
        hv = p.sb("hv", [64, len(names), 16], F32)
        for ni, nm in enumerate(names):
            o, w = CV[nm]
            src = c.cvec_d.h[:, l * NV + o:l * NV + o + 8]
            p.dma("sp", hv[:, ni, 0:8], src[0:64, :], reads=[c.cvec_d], writes=[hv])
            p.dma("sp", hv[:, ni, 8:16], src[64:128, :], reads=[c.cvec_d], writes=[hv])
        hidx = lambda hd: (hd % 2) * 8 + hd // 2
        hvc = lambda nm, hd: hv[:, names.index(nm), hidx(hd):hidx(hd) + 1]
        negw0 = p.sb("negw0", [64, 16], F32)
        p.ts("dve", negw0[:], hv[:, names.index("w0"), :], -1.0, ALU.mult, [hv], [negw0])
        muw = p.sb("muw", [64, 4], F32)
        o, w = CV["mu_wa"]
        p.dma("sp", muw[:, 0:2], c.cvec_d.h[0:64, l * NV + o:l * NV + o + 2], reads=[c.cvec_d], writes=[muw])
        p.dma("sp", muw[:, 2:4], c.cvec_d.h[64:128, l * NV + o:l * NV + o + 2], reads=[c.cvec_d], writes=[muw])
        w2s = p.sb("w2s", [64, 2048], F32)
        w2b = p.sb("w2b", [64, 2048], BF16)
        p.dma("sp", w2s[:, 0:1024], c.rw2.h[l], reads=[c.rw2], writes=[w2s])
        p.dma("sp", w2s[:, 1024:2048], c.ra2.h[l], reads=[c.ra2], writes=[w2s])
        p.copy("dve", w2b[:], w2s[:], [w2s], [w2b])
        mreset = p.sb("mreset", [64, G], F32)
        p.memset("dve", mreset[:], 1.0, writes=[mreset])
        p.memset("dve", mreset[:].rearrange("p (c j) -> p c j", j=64)[:, :, 0:1], 0.0, writes=[mreset])
        Sf = [p.sb("Sf%d" % j, [64, 64], F32) for j in range(16)]
        Sb = [p.sb("Sb%d" % j, [64, 64], BF16) for j in range(16)]
        for j in range(16):
            p.memset("pool", Sf[j][:], 0.0, writes=[Sf[j]])
            p.memset("pool", Sb[j][:], 0.0, writes=[Sb[j]])
        xwd = [p.sb("cxwd%d" % i, [64, G + 1], F32) for i in range(2)]
        xad = [p.sb("cxad%d" % i, [64, G + 1], F32) for i in range(2)]
        xin = {n: [p.sb("cx%s%d" % (n, i), [64, G + 1], F32) for i in range(2)] for n in "rkv"}
        mwd = p.sb("cmwd", [64, G], F32)
        mad = p.sb("cmad", [64, G], F32)
        wdb = p.sb("cwdb", [64, G], BF16)
        adb = p.sb("cadb", [64, G], BF16)
        mr = p.sb("cmr", [64, G], F32)
        mk = p.sb("cmk", [64, G], F32)
        mv = [p.sb("cmv%d" % i, [64, G], F32) for i in range(2)]
        lwn = p.sb("clwn", [64, G], F32)
        cs = p.sb("ccs", [64, G], F32)
        e1 = p.sb("ce1", [64, G], F32)
        e2 = p.sb("ce2", [64, G], F32)
        e3 = p.sb("ce3", [64, G], F32)
        e4 = p.sb("ce4", [64, G], F32)
        av = p.sb("cav", [64, G], F32)
        kkn = p.sb("ckkn", [64, G], F32)
        t3 = p.sb("ct3_", [64, G], F32)
        rko = [p.sb("crko%d" % i, [64, G], F32) for i in range(2)]
        gCs = [p.sb("cgC%d" % i, [64, 8], F32) for i in range(2)]
        ARs = [p.sb("cAR%d" % i, [64, 8, 128], BF16) for i in range(2)]
        BK = p.sb("cBK", [64, 8, 128], BF16)
        Bg = p.sb("cBg", [64, G], BF16)
        Kg = p.sb("cKg", [64, G], BF16)
        Vb = p.sb("cVb", [64, G], BF16)
        ABs = [p.sb("cAB%d" % i, [64, 8, 128], BF16) for i in range(2)]
        AKs = [p.sb("cAK%d" % i, [64, 8, 128], BF16) for i in range(2)]
        Pm = [p.sb("cP%d" % i, [64, 8, 64], BF16) for i in range(2)]
        Qm = [p.sb("cQ%d" % i, [64, 8, 64], BF16) for i in range(2)]
        TTm = [p.sb("cTT%d" % i, [64, 8, 64], BF16) for i in range(4)]
        BgTs = [p.sb("cBgT%d" % i, [64, 8, 64], BF16) for i in range(2)]
        KgTs = [p.sb("cKgT%d" % i, [64, 8, 64], BF16) for i in range(2)]
        Vts = [p.sb("cVt%d" % i, [64, 8, 64], BF16) for i in range(2)]
        TAts = [p.sb("cTAt%d" % i, [64, 8, 64], BF16) for i in range(2)]
        UVss = [p.sb("cUV%d" % i, [64, 8, 64], F32) for i in range(2)]
        AtT = p.sb("cAtT", [64, 8, 64], BF16)
        W2sb = p.sb("cW2sb", [64, 8, 64], BF16)
        Usb = p.sb("cUsb", [64, 64], BF16)
        Ysb = [p.sb("cYsb%d" % i, [64, G], F32) for i in range(2)]
        PB = p.ps("cPB", [64, 1024], F32)
        PA = p.ps("cPA", [64, 512], F32)
        PY = p.ps("cPY", [64, 512], F32)
        PI = [p.ps("cPI%d" % i, [64, 512], F32) for i in range(3)]
        PT = p.ps("cPT", [64, 1024], BF16)
        v3 = lambda a: a.rearrange("p (c j) -> p c j", j=64)
        f2 = lambda a: a.rearrange("p c m -> p (c m)")

        def load(x_, row, t0):
            if t0 == 0:
                p.memset("pool", x_[:, 0:1], 0.0, writes=[x_])
                p.dma("sp", x_[:, 1:G + 1], c.projT[row:row + 64, 0:G], reads=[c.projT], writes=[x_])
            else:
                p.dma("sp", x_[:, 0:G + 1], c.projT[row:row + 64, t0 - 1:t0 + G], reads=[c.projT], writes=[x_])

        def mix(x_, mu_ap, dst):
            p.tt("pool", dst[:], x_[:, 0:G], x_[:, 1:G + 1], ALU.subtract, [x_], [dst])
            p.stt(dst[:], dst[:], mu_ap, x_[:, 1:G + 1], ALU.mult, ALU.add, [dst, x_, hv, muw], [dst])

        def prep(it, g, hd):
            t0 = g * G
            st = it % 2
            AR, AB, AK, gC = ARs[st], ABs[st], AKs[st], gCs[st]
            BgT, KgT, Vt, TAt, UVs = BgTs[st], KgTs[st], Vts[st], TAts[st], UVss[st]
            if hd == 0:
                load(xwd[g % 2], B_WA, t0)
                load(xad[g % 2], B_WA + 64, t0)
                mix(xwd[g % 2], muw[:, 0:1], mwd)
                mix(xad[g % 2], muw[:, 2:3], mad)
                p.act(wdb[:], mwd[:], AF.Tanh, [mwd], [wdb])
                p.copy("dve", adb[:], mad[:], [mad], [adb])
                yield
            xr_, xk_, xv_ = [xin[n][it % 2] for n in "rkv"]
            mv_ = mv[it % 2]
            rko_ = rko[it % 2]
            load(xr_, B_R + hd * 64, t0)
            load(xk_, B_K + hd * 64, t0)
            load(xv_, B_V + hd * 64, t0)
            mix(xr_, hvc("mu_r", hd), mr)
            yield
            mix(xk_, hvc("mu_k", hd), mk)
            mix(xv_, hvc("mu_v", hd), mv_)
            yield
            p.dma("pool", c.vD[hd * 64:(hd + 1) * 64, t0:t0 + G], mv_[:], reads=[mv_], writes=[c.vD])
            p.copy("act", Vb[:], mv_[:], [mv_], [Vb])
            pw, pa, pk = PI
            p.mm(pw[:], w2b[:, hd * 64:(hd + 1) * 64], wdb[:], reads=[w2b, wdb], writes=[pw])
            p.mm(pa[:], w2b[:, 1024 + hd * 64:1024 + (hd + 1) * 64], adb[:], reads=[w2b, adb], writes=[pa])
            yield
            p.act(e1[:], pw[:], AF.Exp, [pw, negw0], [e1], bias=negw0[:, hidx(hd):hidx(hd) + 1], scale=-1.0)
            p.act(e1[:], e1[:], AF.Ln, [e1, c.cm], [e1], bias=ONEC)
            yield
            p.act(lwn[:], e1[:], AF.Exp, [e1, c.cm], [lwn], bias=NHALF, scale=-1.0)
            p.act(av[:], pa[:], AF.Sigmoid, [pa, hv], [av], bias=hvc("a0", hd))
            yield
            p.op("dve", lambda e: e.tensor_tensor_scan(out=cs[:], data0=mreset[:], data1=lwn[:], initial=0.0,
                                                       op0=ALU.mult, op1=ALU.add), [mreset, lwn], [cs])
            cs3 = v3(cs[:])
            p.act(e1[:], cs[:], AF.Exp, [cs], [e1])
            yield
            p.act(e2[:], cs[:], AF.Exp, [cs], [e2], scale=-1.0)
            p.tt("pool", e3[:], cs[:], lwn[:], ALU.subtract, [cs, lwn], [e3])
            yield
            p.act(e3[:], e3[:], AF.Exp, [e3], [e3], scale=-1.0)
            p.tt("pool", v3(e4[:]), cs3, cs3[:, :, 63:64].to_broadcast([64, 8, 64]), ALU.subtract, [cs], [e4])
            yield
            p.act(e4[:], e4[:], AF.Exp, [e4], [e4])
            p.act(gC[:], cs3[:, :, 63], AF.Exp, [cs], [gC], scale=-1.0)
            yield
            p.ts("dve", kkn[:], mk[:], hvc("k_k", hd), ALU.mult, [mk, hv], [kkn])
            p.tt("pool", t3[:], kkn[:], kkn[:], ALU.mult, [kkn], [t3])
            p.mm(pk[:], BLK, t3[:], reads=[c.cm, t3], writes=[pk])
            yield
            p.act(t3[:], pk[:], AF.Sqrt, [pk], [t3])
            p.ts("dve", t3[:], t3[:], 1e-12, ALU.max, [t3], [t3])
            yield
            p.op("dve", lambda e, a=t3[:]: e.reciprocal(out=a, in_=a), [t3], [t3])
            p.tt("dve", kkn[:], kkn[:], t3[:], ALU.mult, [kkn, t3], [kkn])
            yield
            p.ts("dve", t3[:], av[:], -1.0, ALU.add, [av, hv], [t3], s2=hvc("k_a", hd), op1=ALU.mult)
            p.ts("dve", t3[:], t3[:], 1.0, ALU.add, [t3], [t3])
            yield
            p.tt("dve", mk[:], mk[:], t3[:], ALU.mult, [mk, t3], [mk])
            p.tt("pool", av[:], av[:], kkn[:], ALU.mult, [av, kkn], [av])
            yield
            p.tt("pool", t3[:], mr[:], mk[:], ALU.mult, [mr, mk], [t3])
            p.ts("dve", t3[:], t3[:], hvc("r_k", hd), ALU.mult, [t3, hv], [t3])
            p.mm(pk[:], BLK, t3[:], reads=[c.cm, t3], writes=[pk])
            yield
            p.copy("act", rko_[:], pk[:], [pk], [rko_])
            p.dma("pool", c.rkD[hd * 64:(hd + 1) * 64, t0:t0 + G], rko_[:], reads=[rko_], writes=[c.rkD])
            p.stt(AR[:, :, 0:64], v3(kkn[:]), -1.0, v3(e3[:]), ALU.mult, ALU.mult, [kkn, e3], [AR])
            yield
            p.tt("pool", AR[:, :, 64:128], v3(mr[:]), v3(e2[:]), ALU.mult, [mr, e2], [AR])
            p.tt("dve", BK[:, :, 0:64], v3(av[:]), v3(e1[:]), ALU.mult, [av, e1], [BK])
            yield
            p.tt("pool", BK[:, :, 64:128], v3(mk[:]), v3(e1[:]), ALU.mult, [mk, e1], [BK])
            p.tt("dve", Bg[:], av[:], e4[:], ALU.mult, [av, e4], [Bg])
            p.tt("pool", Kg[:], mk[:], e4[:], ALU.mult, [mk, e4], [Kg])
            yield
            for ci in range(8):
                p.mm(PB[:, ci * 128:(ci + 1) * 128], BK[:, ci, 0:64], AR[:, ci, :], reads=[BK, AR], writes=[PB])
                if ci % 4 == 3:
                    yield
            p.tt("dve", AB[:], PB[:].rearrange("p (c m) -> p c m", m=128), MASK4.unsqueeze(1).to_broadcast([64, 8, 128]),
                 ALU.mult, [PB, c.cm], [AB])
            yield
            for ci in range(8):
                p.mm(PB[:, ci * 128:(ci + 1) * 128], BK[:, ci, 64:128], AR[:, ci, :], reads=[BK, AR], writes=[PB])
                if ci % 4 == 3:
                    yield
            p.tt("dve", AK[:], PB[:].rearrange("p (c m) -> p c m", m=128), MASK4.unsqueeze(1).to_broadcast([64, 8, 128]),
                 ALU.mult, [PB, c.cm], [AK])
            yield
            for ci in range(8):
                p.mm(PI[0][:, ci * 64:(ci + 1) * 64], AR[:, ci, 0:64], BK[:, ci, 0:64], reads=[AR, BK], writes=[PI[0]])
            yield
            Pc, Qc, Tc = Pm[0], Qm[0], TTm[2 * st]
            p.tt("dve", Pc[:], PI[0][:].rearrange("p (c m) -> p c m", m=64), MLOW.unsqueeze(1).to_broadcast([64, 8, 64]),
                 ALU.mult, [PI[0], c.cm], [Pc])
            p.copy("pool", Qc[:], AB[:, :, 0:64], [AB], [Qc])
            p.tt("pool", Tc[:], AB[:, :, 0:64], I2.unsqueeze(1).to_broadcast([64, 8, 64]), ALU.add, [AB, c.cm], [Tc])
            yield
            cur = 0
            for lev in range(5):
                Pn, Qn, Tn = Pm[1 - cur], Qm[1 - cur], TTm[2 * st + 1 - cur]
                Pc, Qc, Tc = Pm[cur], Qm[cur], TTm[2 * st + cur]
                for ci in range(8):
                    cc = slice(ci * 64, (ci + 1) * 64)
                    p.mm(PI[0][:, cc], Qc[:, ci, :], Pc[:, ci, :], reads=[Qc, Pc], writes=[PI[0]])
                    if ci % 4 == 3:
                        yield
                if lev < 4:
                    for ci in range(8):
                        cc = slice(ci * 64, (ci + 1) * 64)
                        p.mm(PI[1][:, cc], Pc[:, ci, :], Qc[:, ci, :], reads=[Qc, Pc], writes=[PI[1]])
                        if ci % 4 == 3:
                            yield
                p.copy("act", f2(Pn[:]), PI[0][:], [PI[0]], [Pn])
                if lev < 4:
                    p.copy("dve", f2(Qn[:]), PI[1][:], [PI[1]], [Qn])
                yield
                for ci in range(8):
                    cc = slice(ci * 64, (ci + 1) * 64)
                    p.mm(PI[2][:, cc], Pn[:, ci, :], Tc[:, ci, :], reads=[Pn, Tc], writes=[PI[2]])
                    if ci % 4 == 3:
                        yield
                p.tt("dve", f2(Tn[:]), f2(Tc[:]), PI[2][:], ALU.add, [Tc, PI[2]], [Tn])
                yield
                cur = 1 - cur
            TT = TTm[2 * st + cur]
            assert cur == 1
            for (src_, dst_, w0_) in ((Bg, BgT, None), (Kg, KgT, None), (Vb, Vt, None), (AR, AtT, 0)):
                for ci in range(8):
                    in_ = src_[:, ci * 64:(ci + 1) * 64] if w0_ is None else src_[:, ci, 0:64]
                    p.tr(PT[:, ci * 64:(ci + 1) * 64], in_, IDb, [src_, c.cm_bf], [PT])
                    if ci % 4 == 3:
                        yield
                p.copy("act", f2(dst_[:]), PT[:, 0:512], [PT], [dst_])
                yield
            for ci in range(8):
                p.mm(PI[0][:, ci * 64:(ci + 1) * 64], AtT[:, ci, :], TT[:, ci, :], reads=[AtT, TT], writes=[PI[0]])
                if ci % 4 == 3:
                    yield
            p.copy("act", f2(TAt[:]), PI[0][:], [PI[0]], [TAt])
            for ci in range(8):
                p.mm(PI[1][:, ci * 64:(ci + 1) * 64], AK[:, ci, 0:64], Vt[:, ci, :], reads=[AK, Vt], writes=[PI[1]])
                if ci % 4 == 3:
                    yield
            p.copy("dve", f2(W2sb[:]), PI[1][:], [PI[1]], [W2sb])
            yield
            for ci in range(8):
                p.mm(PI[2][:, ci * 64:(ci + 1) * 64], TT[:, ci, :], W2sb[:, ci, :], reads=[TT, W2sb], writes=[PI[2]])
                if ci % 4 == 3:
                    yield
            p.copy("act", f2(UVs[:]), PI[2][:], [PI[2]], [UVs])
            yield

        def seq(it, g, hd):
            t0 = g * G
            st = it % 2
            AR, AB, AK, gC = ARs[st], ABs[st], AKs[st], gCs[st]
            BgT, KgT, Vt, TAt, UVs = BgTs[st], KgTs[st], Vts[st], TAts[st], UVss[st]
            S_f, S_b = Sf[hd], Sb[hd]
            ys_ = Ysb[it % 2]
            for ci in range(8):
                yc = slice(ci * 64, (ci + 1) * 64)
                p.mm(PA[:, 0:64], TAt[:, ci, :], S_b[:], reads=[TAt, S_b], writes=[PA])
                p.mm(PY[:, yc], S_b[:], AR[:, ci, 64:128], start=True, stop=False, reads=[S_b, AR], writes=[PY])
                p.mm(PY[:, yc], Vt[:, ci, :], AK[:, ci, 64:128], start=False, stop=False, reads=[Vt, AK], writes=[PY])
                p.mm(PA[:, 128:192], KgT[:, ci, :], Vt[:, ci, :], start=True, stop=False, reads=[KgT, Vt], writes=[PA])
                yield
                p.tt("dve", Usb[:], PA[:, 0:64], UVs[:, ci, :], ALU.add, [PA, UVs], [Usb])
                yield
                p.mm(PA[:, 128:192], BgT[:, ci, :], Usb[:], start=False, stop=True, reads=[BgT, Usb], writes=[PA])
                p.mm(PY[:, yc], Usb[:], AB[:, ci, 64:128], start=False, stop=True, reads=[Usb, AB], writes=[PY])
                yield
                p.stt(S_f[:], S_f[:], gC[:, ci:ci + 1], PA[:, 128:192], ALU.mult, ALU.add, [S_f, gC, PA], [S_f])
                yield
                p.copy("act", S_b[:], S_f[:], [S_f], [S_b])
                yield
            p.copy("dve", ys_[:], PY[:, 0:512], [PY], [ys_])
            p.dma("pool", c.yD[hd * 64:(hd + 1) * 64, t0:t0 + G], ys_[:], reads=[ys_], writes=[c.yD])
            yield

        work = [(g, hd) for g in range(NG) for hd in range(16)]
        for _ in prep(0, *work[0]):
            pass
        for it, (g, hd) in enumerate(work):
            gs = seq(it, g, hd)
            gp = prep(it + 1, *work[it + 1]) if it + 1 < len(work) else iter(())
            alive_s, alive_p = True, True
            while alive_s or alive_p:
                if alive_s:
                    try:
                        next(gs)
                    except StopIteration:
                        alive_s = False
                for _ in range(2):
                    if alive_p:
                        try:
                            next(gp)
                        except StopIteration:
                            alive_p = False
    _rwkv_b3(p, c, l)


def _rwkv_b3(p, c, l):
    T = c.T
    BLK = c.cm[:, 896:1024]
    with p.scope():
        yt = [p.sb("cy%d" % i, [128, T], F32) for i in range(2)]
        vt = [p.sb("cv_%d" % i, [128, T], F32) for i in range(2)]
        rkt = [p.sb("crk%d" % i, [128, T], F32) for i in range(2)]
        gt = [p.sb("cg%d" % i, [128, T], F32) for i in range(2)]
        ob = [p.sb("cob%d" % i, [128, T], BF16) for i in range(2)]
        t2 = p.sb("ct2", [128, 512], F32)
        t3 = p.sb("ct3", [128, 512], F32)
        PS = [p.ps("cps%d" % i, [128, 512], F32) for i in range(4)]
        for j in range(8):
            y_, v_, rk_, g_, o_ = yt[j % 2], vt[j % 2], rkt[j % 2], gt[j % 2], ob[j % 2]
            rs_ = slice(j * 128, (j + 1) * 128)
            p.dma("sp", y_[:], c.yD[rs_, 0:T], reads=[c.yD], writes=[y_])
            p.dma("sp", v_[:], c.vD[rs_, 0:T], reads=[c.vD], writes=[v_])
            p.dma("sp", rk_[:], c.rkD[rs_, 0:T], reads=[c.rkD], writes=[rk_])
            p.dma("sp", g_[:], c.projT[B_G + j * 128:B_G + (j + 1) * 128, 0:T], reads=[c.projT], writes=[g_])
            for q in range(T // 512):
                sl = slice(q * 512, (q + 1) * 512)
                pm, pv = PS[(2 * q) % 4], PS[(2 * q + 1) % 4]
                p.mm(pm[:], BLK, y_[:, sl], reads=[c.cm, y_], writes=[pm])
                p.stt(y_[:, sl], pm[:], -1.0 / 64, y_[:, sl], ALU.mult, ALU.add, [pm, y_], [y_])
                p.tt("pool", t2[:], y_[:, sl], y_[:, sl], ALU.mult, [y_], [t2])
                p.mm(pv[:], BLK, t2[:], reads=[c.cm, t2], writes=[pv])
                p.ts("dve", t3[:], pv[:], 1.0 / 64, ALU.mult, [pv], [t3], s2=64e-5, op1=ALU.add)
                p.act(t3[:], t3[:], AF.Sqrt, [t3], [t3])
                p.op("dve", lambda e, a=t3[:]: e.reciprocal(out=a, in_=a), [t3], [t3])
                p.tt("dve", y_[:, sl], y_[:, sl], t3[:], ALU.mult, [y_, t3], [y_])
                p.ts("dve", y_[:, sl], y_[:, sl], cvcol(c, l, "ln_w", j), ALU.mult, [y_, c.cv], [y_],
                     s2=cvcol(c, l, "ln_b", j), op1=ALU.add)
                p.tt("pool", t2[:], rk_[:, sl], v_[:, sl], ALU.mult, [rk_, v_], [t2])
                p.tt("dve", y_[:, sl], y_[:, sl], t2[:], ALU.add, [y_, t2], [y_])
                p.act(t2[:], g_[:, sl], AF.Silu, [g_], [t2])
                p.tt("dve", o_[:, sl], y_[:, sl], t2[:], ALU.mult, [y_, t2], [o_])
            p.dma("pool", c.ybT[rs_, 0:T], o_[:], reads=[o_], writes=[c.ybT])


def pad_w_in(w):
    o = np.zeros((2048, NCP), np.float32)
    o[:, 0:3088] = w[:, 0:3088]
    b = 3088
    o[:, B_R:B_R + 1024] = w[:, b:b + 1024]
    o[:, B_WA:B_WA + 64] = w[:, b + 1024:b + 1088]
    o[:, B_K:B_K + 1024] = w[:, b + 1088:b + 2112]
    o[:, B_V:B_V + 1024] = w[:, b + 2112:b + 3136]
    o[:, B_WA + 64:B_WA + 128] = w[:, b + 3136:b + 3200]
    o[:, B_G:B_G + 1024] = w[:, b + 3200:b + 4224]
    cc = 7312
    o[:, C_Q:C_Q + 1024] = w[:, cc:cc + 1024]
    o[:, C_K:C_K + 256] = w[:, cc + 1024:cc + 1280]
    o[:, C_V:C_V + 256] = w[:, cc + 1280:cc + 1536]
    o[:, C_G:C_G + 1024] = w[:, cc + 1536:cc + 2560]
    o[:, C_IQ:C_IQ + 1024] = w[:, cc + 2560:cc + 3584]
    o[:, C_IK:C_IK + 80] = w[:, cc + 3584:cc + 3664]
    o[:, G0:G0 + 6144] = w[:, 10976:17120]
    return o

def col(v):
    v = np.asarray(v, np.float32).reshape(-1)
    n = (v.size + 127) // 128
    o = np.zeros((n * 128,), np.float32)
    o[:v.size] = v
    return o.reshape(n, 128).T

def pack_cvec(inp, L):
    cv = np.zeros((128, L * NV), np.float32)
    def put(l, name, arr):
        o, w = CV[name]
        assert arr.shape == (128, w), (name, arr.shape, w)
        cv[:, l * NV + o:l * NV + o + w] = arr
    for l in range(L):
        put(l, "pre_g", col(inp["pre_norm"][l]))
        put(l, "post_g", col(inp["post_norm"][l]))
        put(l, "bg", col(inp["b_gate"][l]))
        put(l, "conv_w", np.concatenate([col(inp["ssm_conv_w"][l][j]) for j in range(4)], axis=1))
        put(l, "conv_b", col(inp["ssm_conv_b"][l]))
        put(l, "dt_bias", col(inp["ssm_dt_bias"][l]))
        put(l, "a_log", col(inp["ssm_a_log"][l]))
        put(l, "d_exp", col(np.repeat(inp["ssm_d"][l], 64)))
        put(l, "ssm_norm", col(inp["ssm_norm"][l]))
        mu = inp["rwkv_mu"][l]
        put(l, "mu_r", col(mu[0:1024]))
        put(l, "mu_k", col(mu[1088:2112]))
        put(l, "mu_v", col(mu[2112:3136]))
        put(l, "mu_wa", col(np.concatenate([mu[1024:1088], mu[3136:3200]])))
        for n, k in [("w0", "rwkv_w0"), ("a0", "rwkv_a0"), ("k_k", "rwkv_k_k"), ("k_a", "rwkv_k_a"),
                     ("ln_w", "rwkv_ln_w"), ("ln_b", "rwkv_ln_b"), ("r_k", "rwkv_r_k")]:
            put(l, n, col(inp[k][l].reshape(-1)))
        put(l, "ikn_w", col(inp["idx_k_norm_w"][l]))
        put(l, "ikn_b", col(inp["idx_k_norm_b"][l]))
    return cv

def const_mats():
    k = np.arange(128)
    ident = np.eye(128, dtype=np.float32)
    triu = (k[:, None] <= k[None, :]).astype(np.float32)
    su = (k[:, None] > k[None, :]).astype(np.float32)
    ones = np.ones((128, 128), np.float32)
    z = np.zeros((128, 128), np.float32)
    perm = np.zeros((128, 128), np.float32)
    for m in range(128):
        perm[(m // 64) * 64 + ((m % 64) + 32) % 64, m] = 1.0
    negm = np.where(k[None, :] > k[:, None], -1e30, 0.0).astype(np.float32)
    misc = np.zeros((128, 128), np.float32)
    inv = 10000.0 ** (-(np.arange(32, dtype=np.float32) * 2.0 / 64))
    misc[:, 0] = inv[k % 32]
    misc[:, 1] = np.where((k % 64) < 32, -1.0, 1.0)
    misc[:, 2] = np.pi / 2
    misc[:, 3] = -0.5
    misc[:, 4] = 64e-5
    blk = (k[:, None] // 64 == k[None, :] // 64).astype(np.float32)
    s_ = (k % 64)[:, None]
    t_ = (k % 64)[None, :]
    mask4 = np.where(k[None, :] < 64, s_ < t_, s_ <= t_).astype(np.float32)
    m9 = np.zeros((128, 128), np.float32)
    k64 = np.arange(64)
    m9[:, 0:64] = ((k % 64)[:, None] > k64[None, :]).astype(np.float32)
    m9[:, 64:128] = ((k % 64)[:, None] == k64[None, :]).astype(np.float32)
    m10 = np.zeros((128, 128), np.float32)
    m10[:, 0:32] = (0.5 ** np.arange(1, 33, dtype=np.float64)).astype(np.float32)[None, :]
    return np.concatenate([ident, triu, su, ones, perm, negm, misc, blk, mask4, m9, m10], axis=1)


_NC_CACHE = {}


def build_program(T, L):
    nc = bass.Bass("TRN2", target_bir_lowering=False)
    es = ExitStack()
    with es:
        p = Prog(nc, es, nsem=100)
        c = Ctx()
        c.T = T
        c.xT0 = p.dram("xT0", [2048, T], F32, kind="ExternalInput")
        c.w_in_l = [p.dram("w_in%d" % l, [2048, NCP], F32, kind="ExternalInput") for l in range(L)]
        c.w_ba = p.dram("w_ba", [L, 1024, 2048], F32, kind="ExternalInput")
        c.w_bb = p.dram("w_bb", [L, 1024, 2048], F32, kind="ExternalInput")
        c.w_bc = p.dram("w_bc", [L, 1024, 2048], F32, kind="ExternalInput")
        c.w_out = p.dram("w_out", [L, 2048, 2048], F32, kind="ExternalInput")
        c.cvec_d = p.dram("cvec", [128, L * NV], F32, kind="ExternalInput")
        c.cmat_d = p.dram("cmat", [128, 1408], F32, kind="ExternalInput")
        c.pos_d = p.dram("pos", [128, T], I32, kind="ExternalInput")
        c.rw2 = p.dram("rw2", [L, 64, 1024], F32, kind="ExternalInput")
        c.ra2 = p.dram("ra2", [L, 64, 1024], F32, kind="ExternalInput")
        c.xo = p.dram("xo", [2048, T], F32, kind="ExternalOutput")
        xA = p.dram("xA", [2048, T], F32)
        xB = p.dram("xB", [2048, T], F32)
        c.projT = p.dram("projT", [G0, T], F32)
        c.projG = p.dram("projG", [NCP - G0, T], F32)
        c.xbcT = p.dram("xbcT", [2048, T], BF16)
        c.dsaT = p.dram("dsaT", [2432, T], BF16)
        c.yaT = p.dram("yaT", [1024, T], BF16)
        c.ybT = p.dram("ybT", [1024, T], BF16)
        c.ycT = p.dram("ycT", [1024, T], BF16)
        c.vD = p.dram("vD", [1024, T], F32)
        c.rkD = p.dram("rkD", [1024, T], F32)
        c.yD = p.dram("yD", [1024, T], F32)
        alloc_consts(p, c, L)
        p.barrier()
        xs = [c.xT0] + [xA if (l % 2 == 0) else xB for l in range(L - 1)] + [c.xo]
        for l in range(L):
            phase_p1(p, c, l, xs[l])
            phase_ssd(p, c, l)
            phase_rwkv2(p, c, l)
            phase_dsa(p, c, l)
            phase_p3(p, c, l, xs[l], xs[l + 1])
        p.barrier()
    return nc


def kernel(**inputs):
    inp = {k: np.asarray(v) for k, v in inputs.items()}
    x = inp["x"].astype(np.float32, copy=False)
    Bsz, T, _ = x.shape
    L = inp["w_in"].shape[0]
    key = (T, L)
    if key not in _NC_CACHE:
        _NC_CACHE[key] = build_program(T, L)
    nc = _NC_CACHE[key]
    cmat = const_mats()
    cvec = pack_cvec(inp, L)
    w_in_p = [pad_w_in(inp["w_in"][l]) for l in range(L)]
    maps = []
    for b in range(Bsz):
        m = {"xT0": np.ascontiguousarray(x[b].T), "w_ba": inp["w_branch_a"], "w_bb": inp["w_branch_b"],
             "w_bc": inp["w_branch_c"], "w_out": inp["w_out"], "cvec": cvec, "cmat": cmat,
             "pos": np.ascontiguousarray(np.broadcast_to(inp["positions"][b][None, :], (128, T))).astype(np.int32),
             "rw2": inp["rwkv_w2"], "ra2": inp["rwkv_a2"]}
        for l in range(L):
            m["w_in%d" % l] = w_in_p[l]
        maps.append(m)
    res = run_bass_kernel_spmd(nc, maps, core_ids=list(range(Bsz)))
    return np.stack([np.asarray(res.results[b]["xo"]).T for b in range(Bsz)]).astype(np.float32)
```
